# Optimizing a Trainium2 kernel written in Bass

```python
import jax, jax.numpy as jnp
from jax import lax
import numpy as np

D_MODEL = 1024
BATCH = 4
SEQ = 4096
DEPTH = 1

ROPE_THETA = 500000.0
BLOCK = 128
NEG = -1e30
RMS_EPS = 1e-6
LN_EPS = 1e-5

MLA_HEADS = 8
MLA_NOPE_DIM = 64
MLA_ROPE_DIM = 32
MLA_V_DIM = 64
Q_LORA_RANK = 384
KV_LORA_RANK = 256
MLA_WIDTH = MLA_HEADS * MLA_V_DIM

DIL_HEADS = 8
DIL_HEAD_DIM = 64
DIL_ROT_DIM = DIL_HEAD_DIM // 4
DIL_WIDTH = DIL_HEADS * DIL_HEAD_DIM
DIL_CONFIGS = ((128, 1), (512, 4), (2048, 16))

MIX_WIDTH = MLA_WIDTH + DIL_WIDTH
IN_SPLITS = (Q_LORA_RANK, KV_LORA_RANK, MLA_ROPE_DIM, MLA_WIDTH, DIL_WIDTH, DIL_WIDTH, DIL_WIDTH, DIL_WIDTH)
IN_WIDTH = sum(IN_SPLITS)

DEEPNORM_ALPHA = (2.0 * DEPTH) ** 0.25
DEEPNORM_BETA = (8.0 * DEPTH) ** -0.25

kernel_name = "hybrid_mla_dilated_deepnorm"


def rmsnorm(t, g):
    tf = t.astype(jnp.float32)
    tf = tf * lax.rsqrt(jnp.mean(tf * tf, axis=-1, keepdims=True) + RMS_EPS)
    return (tf * g.astype(jnp.float32)).astype(t.dtype)


def layernorm(t, g, b):
    tf = t.astype(jnp.float32)
    mu = jnp.mean(tf, axis=-1, keepdims=True)
    var = jnp.mean(jnp.square(tf - mu), axis=-1, keepdims=True)
    return ((tf - mu) * lax.rsqrt(var + LN_EPS) * g.astype(jnp.float32) + b.astype(jnp.float32)).astype(t.dtype)


def rope_tables(seq_len, dim):
    inv_freq = ROPE_THETA ** (-jnp.arange(0, dim, 2, dtype=jnp.float32) / dim)
    ang = jnp.arange(seq_len, dtype=jnp.float32)[:, None] * inv_freq[None, :]
    return jnp.cos(ang), jnp.sin(ang)


def apply_rope(t, cos, sin):
    t1, t2 = jnp.split(t.astype(jnp.float32), 2, axis=-1)
    c, s = cos[:, None, :], sin[:, None, :]
    return jnp.concatenate([t1 * c - t2 * s, t1 * s + t2 * c], axis=-1).astype(t.dtype)


def mla_attention(c_q, c_kv, k_rope, q_norm_g, kv_norm_g, w_uq, w_ukv):
    B, S, _ = c_q.shape
    H, DN, DR, DV = MLA_HEADS, MLA_NOPE_DIM, MLA_ROPE_DIM, MLA_V_DIM
    cos, sin = rope_tables(S, DR)
    q = (rmsnorm(c_q, q_norm_g) @ w_uq).reshape(B, S, H, DN + DR)
    q = jnp.concatenate([q[..., :DN], apply_rope(q[..., DN:], cos, sin)], axis=-1)
    kv = (rmsnorm(c_kv, kv_norm_g) @ w_ukv).reshape(B, S, H, DN + DV)
    k_nope, v = kv[..., :DN], kv[..., DN:]
    k_pe = apply_rope(k_rope[:, :, None, :], cos, sin)
    k = jnp.concatenate([k_nope, jnp.broadcast_to(k_pe, (B, S, H, DR))], axis=-1)
    scale = (DN + DR) ** -0.5
    nblk = S // BLOCK
    qb = q.reshape(B, nblk, BLOCK, H, DN + DR).transpose(1, 0, 3, 2, 4)
    kpos = jnp.arange(S)

    def one_block(args):
        q_blk, i = args
        s = jnp.einsum('bhqd,bkhd->bhqk', q_blk, k).astype(jnp.float32) * scale
        qpos = i * BLOCK + jnp.arange(BLOCK)
        s = jnp.where(kpos[None, :] <= qpos[:, None], s, NEG)
        p = jax.nn.softmax(s, axis=-1)
        return jnp.einsum('bhqk,bkhd->bqhd', p.astype(v.dtype), v)

    out = lax.map(one_block, (qb, jnp.arange(nblk)))
    return out.transpose(1, 0, 2, 3, 4).reshape(B, S, H * DV)


def dilated_branch(q, k, v, window, dilation):
    B, S, H, D = q.shape
    n_back = window // dilation
    seg = dilation * BLOCK
    S_pad = -(-S // seg) * seg
    pad = ((0, 0), (0, S_pad - S), (0, 0), (0, 0))
    L = S_pad // dilation
    nb = L // BLOCK

    def to_sub(t):
        t = jnp.pad(t, pad).reshape(B, L, dilation, H, D).transpose(0, 2, 3, 1, 4)
        return t.reshape(B, dilation, H, nb, BLOCK, D)

    def with_prev(t):
        prev = jnp.pad(t, ((0, 0), (0, 0), (0, 0), (1, 0), (0, 0), (0, 0)))[:, :, :, :-1]
        return jnp.concatenate([prev, t], axis=4)

    qs = to_sub(q)
    ks = with_prev(to_sub(k))
    vs = with_prev(to_sub(v))
    s = jnp.einsum('bdhnqe,bdhnke->bdhnqk', qs, ks).astype(jnp.float32)
    q_loc = jnp.arange(BLOCK)
    k_loc = jnp.arange(2 * BLOCK) - BLOCK
    dist = q_loc[:, None] - k_loc[None, :]
    valid = (jnp.arange(nb)[:, None, None] * BLOCK + k_loc[None, None, :]) >= 0
    mask = (dist >= 0) & (dist <= n_back) & valid
    s = jnp.where(mask, s, NEG)
    m = jnp.max(s, axis=-1, keepdims=True)
    p = jnp.exp(s - m)
    l = jnp.sum(p, axis=-1, keepdims=True)
    num = jnp.einsum('bdhnqk,bdhnke->bdhnqe', p, vs.astype(jnp.float32))

    def to_seq(t):
        c = t.shape[-1]
        t = t.reshape(B, dilation, H, L, c).transpose(0, 3, 1, 2, 4).reshape(B, S_pad, H, c)
        return t[:, :S]

    return to_seq(num), to_seq(m), to_seq(l)


def dilated_attention(q, k, v):
    B, S, _ = q.shape
    H, D = DIL_HEADS, DIL_HEAD_DIM
    cos, sin = rope_tables(S, DIL_ROT_DIM)

    def heads_rope(t):
        t = t.reshape(B, S, H, D)
        return jnp.concatenate([apply_rope(t[..., :DIL_ROT_DIM], cos, sin), t[..., DIL_ROT_DIM:]], axis=-1)

    qh = heads_rope(q) * (D ** -0.5)
    kh = heads_rope(k)
    vh = v.reshape(B, S, H, D)
    parts = [dilated_branch(qh, kh, vh, w, d) for (w, d) in DIL_CONFIGS]
    m_all = jnp.max(jnp.stack([pm for (_, pm, _) in parts], axis=0), axis=0)
    num = jnp.zeros((B, S, H, D), jnp.float32)
    den = jnp.zeros((B, S, H, 1), jnp.float32)
    for (pn, pm, pl) in parts:
        w = jnp.exp(pm - m_all)
        num = num + w * pn
        den = den + w * pl
    return (num / den).astype(q.dtype).reshape(B, S, H * D)


def setup_inputs(seed: int = 0) -> dict:
    key = jax.random.key(seed)
    ks = jax.random.split(key, 9)
    f32 = jnp.float32
    x = jax.random.normal(ks[0], (BATCH, SEQ, D_MODEL), f32)
    w_in = jax.random.normal(ks[1], (D_MODEL, IN_WIDTH), f32) * D_MODEL ** -0.5
    q_norm_g = 1.0 + 0.02 * jax.random.normal(ks[2], (Q_LORA_RANK,), f32)
    kv_norm_g = 1.0 + 0.02 * jax.random.normal(ks[3], (KV_LORA_RANK,), f32)
    w_uq = jax.random.normal(ks[4], (Q_LORA_RANK, MLA_HEADS * (MLA_NOPE_DIM + MLA_ROPE_DIM)), f32) * Q_LORA_RANK ** -0.5
    w_ukv = jax.random.normal(ks[5], (KV_LORA_RANK, MLA_HEADS * (MLA_NOPE_DIM + MLA_V_DIM)), f32) * KV_LORA_RANK ** -0.5
    w_out = jax.random.normal(ks[6], (MIX_WIDTH, D_MODEL), f32) * (MIX_WIDTH ** -0.5) * DEEPNORM_BETA
    ln_g = 1.0 + 0.02 * jax.random.normal(ks[7], (D_MODEL,), f32)
    ln_b = 0.02 * jax.random.normal(ks[8], (D_MODEL,), f32)
    return {"x": x, "w_in": w_in, "q_norm_g": q_norm_g, "kv_norm_g": kv_norm_g, "w_uq": w_uq,
            "w_ukv": w_ukv, "w_out": w_out, "ln_g": ln_g, "ln_b": ln_b}


def reference(x, w_in, q_norm_g, kv_norm_g, w_uq, w_ukv, w_out, ln_g, ln_b):
    offs = np.cumsum(IN_SPLITS)[:-1].tolist()
    for _ in range(DEPTH):
        h = x @ w_in
        c_q, c_kv, k_rope, g_a, q_b, k_b, v_b, g_b = jnp.split(h, offs, axis=-1)
        y_a = mla_attention(c_q, c_kv, k_rope, q_norm_g, kv_norm_g, w_uq, w_ukv) * jax.nn.silu(g_a)
        y_b = dilated_attention(q_b, k_b, v_b) * jax.nn.silu(g_b)
        mix = jnp.concatenate([y_a, y_b], axis=-1)
        x = layernorm(DEEPNORM_ALPHA * x + mix @ w_out, ln_g, ln_b)
    return x
```

```python
import numpy as np
import concourse.bass as bass
import concourse.mybir as mybir
from concourse.bass_utils import run_bass_kernel_spmd


def eval_ap(fn, loc):
    return fn(loc)

F32 = mybir.dt.float32
BF16 = mybir.dt.bfloat16
U8 = mybir.dt.uint8
ALU = mybir.AluOpType
AF = mybir.ActivationFunctionType

D_MODEL = 1024
BATCH = 4
SEQ = 4096
NT = 4096
NQ = 2048
QOFF = 2048
ROPE_THETA = 500000.0
RMS_EPS = 1e-6
LN_EPS = 1e-5
ALPHA = 2.0 ** 0.25
NEGM = -30000.0

DEBUG_DUMP = False


class Op:
    __slots__ = ("eng", "fn", "deps", "raw", "sig", "val", "dsem", "name")

    def __init__(self, eng, fn, dsem=None, name=""):
        self.eng = eng
        self.fn = fn
        self.deps = []
        self.raw = set()
        self.sig = False
        self.val = 0
        self.dsem = dsem
        self.name = name


class Sched:
    ENGS = ("pe", "act", "dve", "pool", "sp")

    def __init__(self):
        self.ops = []
        self.lw = {}
        self.rd = {}
        self.dma_count = {}
        self.phase_reads = []

    def add(self, eng, fn, reads=(), writes=(), dsem=None, name=""):
        op = Op(eng, fn, dsem, name)
        deps = {}
        reads = list(reads) + [t for t in self.phase_reads if t not in writes]
        for r in reads:
            w = self.lw.get(r)
            if w is not None:
                deps[id(w)] = w
                op.raw.add(id(w))
            if isinstance(r, tuple) and r[0] == "ps":
                for r2 in self.rd.get(r, {}).values():
                    if r2.eng != eng:
                        deps[id(r2)] = r2
        for w_ in writes:
            w = self.lw.get(w_)
            if w is not None:
                deps[id(w)] = w
            for r in self.rd.get(w_, {}).values():
                deps[id(r)] = r
        op.deps = [d for d in deps.values() if d is not op]
        for r in reads:
            self.rd.setdefault(r, {})[eng if dsem is None else ("dma", len(self.ops))] = op
        for w_ in writes:
            self.lw[w_] = op
            self.rd[w_] = {}
        if dsem is not None:
            self.dma_count[dsem] = self.dma_count.get(dsem, 0) + 16
            op.val = self.dma_count[dsem]
        self.ops.append(op)
        return op

    def finalize(self):
        pos = {id(o): i for i, o in enumerate(self.ops)}
        for op in self.ops:
            keep = []
            best = {}
            for d in op.deps:
                if d.dsem is not None:
                    keep.append(d)
                    continue
                if d.eng == op.eng and op.dsem is None:
                    if op.eng in ("act", "dve", "pool") and id(d) in op.raw:
                        k = "self"
                    else:
                        continue
                else:
                    k = d.eng
                if k not in best or pos[id(d)] > pos[id(best[k])]:
                    best[k] = d
            op.deps = keep + list(best.values())
        for op in self.ops:
            for d in op.deps:
                if d.dsem is not None:
                    continue
                if d.eng != op.eng or op.dsem is not None:
                    d.sig = True
                elif op.eng in ("act", "dve", "pool") and id(d) in op.raw:
                    d.sig = True
        cnt = {e: 0 for e in self.ENGS}
        for op in self.ops:
            if op.dsem is None and op.sig:
                cnt[op.eng] += 1
                op.val = cnt[op.eng]
        return cnt

    def emit(self, eng_name, eng, esems, dsems):
        waited = {}
        n = 0
        for op in self.ops:
            if op.eng != eng_name:
                continue
            for d in op.deps:
                if d.dsem is not None:
                    key = ("d", d.dsem)
                    sem = dsems[d.dsem]
                else:
                    if d.eng == op.eng and op.dsem is None:
                        if not (op.eng in ("act", "dve", "pool") and id(d) in op.raw):
                            continue
                    key = ("e", d.eng)
                    sem = esems[d.eng]
                if waited.get(key, 0) >= d.val:
                    continue
                waited[key] = d.val
                eng.wait_ge(sem, d.val)
            if op.fn is None:
                continue
            ins = op.fn(eng)
            n += 1
            if op.dsem is not None:
                ins.then_inc(dsems[op.dsem], 16)
            elif op.sig:
                ins.then_inc(esems[op.eng], 1)
        return n


def _rope_tab(pos, dim):
    inv = (np.float32(ROPE_THETA) ** (-np.arange(0, dim, 2, dtype=np.float32) / np.float32(dim))).astype(np.float32)
    ang = (pos.astype(np.float32)[:, None] * inv[None, :]).astype(np.float32)
    return np.cos(ang).astype(np.float32), np.sin(ang).astype(np.float32)


def _shared_consts(w_in, q_norm_g, kv_norm_g, w_uq, w_ukv, w_out, ln_g, ln_b):
    f = np.float32
    oq, okv, okr, oga, oqb, okb, ovb, ogb = 0, 384, 640, 672, 1184, 1696, 2208, 2720
    w_dil = np.zeros((1024, 4, 4, 128), f)
    for p in range(4):
        for kind, off in enumerate((oqb, okb, ovb, ogb)):
            blk = w_in[:, off + 128 * p: off + 128 * p + 128].copy()
            if kind < 2:
                blk[:, 0:16] = 0.0
                blk[:, 64:80] = 0.0
            w_dil[:, p, kind, :] = blk
    w_dr = np.zeros((1024, 4, 128), f)
    for i, off in enumerate((oqb, okb)):
        t1 = np.concatenate([w_in[:, off + h * 64: off + h * 64 + 8] for h in range(8)], 1)
        t2 = np.concatenate([w_in[:, off + h * 64 + 8: off + h * 64 + 16] for h in range(8)], 1)
        w_dr[:, 2 * i, :] = np.concatenate([t1, t2], 1)
        w_dr[:, 2 * i + 1, :] = np.concatenate([t2, t1], 1)
    kr = w_in[:, okr:okr + 32]
    w_lat = np.concatenate([w_in[:, oq:oq + 384], w_in[:, okv:okv + 256], w_in[:, oga:oga + 512],
                            kr[:, 0:16], kr[:, 16:32], kr[:, 16:32], kr[:, 0:16]], 1).astype(f)
    w_uqh = np.zeros((384, 8, 96), f)
    w_uqr = np.zeros((384, 2, 128), f)
    for h in range(8):
        w_uqh[:, h, 0:64] = w_uq[:, h * 96: h * 96 + 64]
        w_uqr[:, 0, h * 16:(h + 1) * 16] = w_uq[:, h * 96 + 64: h * 96 + 80]
        w_uqr[:, 1, h * 16:(h + 1) * 16] = w_uq[:, h * 96 + 80: h * 96 + 96]
    w_ukn = np.zeros((256, 8, 96), f)
    w_uv = np.zeros((256, 512), f)
    for h in range(8):
        w_ukn[:, h, 0:64] = w_ukv[:, h * 128: h * 128 + 64]
        w_uv[:, h * 64:(h + 1) * 64] = w_ukv[:, h * 128 + 64: h * 128 + 128]
    sel_d = np.zeros((128, 4, 128), f)
    for p in range(4):
        for hh in range(2):
            h = 2 * p + hh
            for fq in range(8):
                sel_d[h * 8 + fq, p, hh * 64 + fq] = 1.0
                sel_d[64 + h * 8 + fq, p, hh * 64 + 8 + fq] = 1.0
    sel_q = np.zeros((128, 8, 2, 96), f)
    for h in range(8):
        for fq in range(16):
            sel_q[h * 16 + fq, h, 0, 64 + fq] = 1.0
            sel_q[h * 16 + fq, h, 1, 80 + fq] = 1.0
    sel_k = np.zeros((32, 96), f)
    for j in range(32):
        sel_k[j, 64 + j] = 1.0
    ident = np.eye(128, dtype=f)
    kk = np.arange(128)[:, None]
    qq = np.arange(128)[None, :]
    mb_cur = np.where(kk <= qq, 0.0, NEGM).astype(f)
    mb_prev = np.where(kk >= qq, 0.0, NEGM).astype(f)
    mb4 = np.concatenate([mb_prev, mb_cur, mb_prev, mb_cur], 1)
    cmat = np.concatenate([ident, mb4], 1).astype(f)
    gq = np.ascontiguousarray(q_norm_g.reshape(3, 128).T).astype(f)
    gkv = np.ascontiguousarray(kv_norm_g.reshape(2, 128).T).astype(f)
    lnp = np.stack([np.broadcast_to(ln_g[None, :], (128, 1024)),
                    np.broadcast_to(ln_b[None, :], (128, 1024))], 1).astype(f)
    part = lambda a, k: np.ascontiguousarray(
        a.reshape(k, 128, *a.shape[1:]).swapaxes(0, 1))
    return {
        "w_dil": part(w_dil, 8),
        "w_dr": part(w_dr, 8),
        "w_lat": part(w_lat, 8),
        "w_uqh": part(w_uqh, 3),
        "w_uqr": part(w_uqr, 3),
        "w_ukn": part(w_ukn, 2),
        "w_uv": part(w_uv, 2),
        "w_out": part(np.ascontiguousarray(w_out).astype(f), 8),
        "sel_d": sel_d, "sel_q": sel_q, "sel_k": sel_k, "cmat": cmat,
        "gq": gq, "gkv": gkv, "lnp": np.ascontiguousarray(lnp),
    }


def _core_consts(h):
    f = np.float32
    pos = np.arange(NT, dtype=np.int64) - 2048 + 2048 * h
    posc = np.maximum(pos, 0)
    c8, s8 = _rope_tab(posc, 16)
    c16, s16 = _rope_tab(posc, 32)
    cc = np.tile(c8.T, (16, 1))
    ss = np.concatenate([-np.tile(s8.T, (8, 1)), np.tile(s8.T, (8, 1))], 0)
    rope_d = np.stack([cc.reshape(128, 8, 512), ss.reshape(128, 8, 512)], 2)
    cq = np.tile(c16.T, (8, 1))[:, QOFF:]
    sq = np.tile(s16.T, (8, 1))[:, QOFF:]
    rope_q = np.stack([cq.reshape(128, 4, 512), sq.reshape(128, 4, 512)], 2)
    rk = np.concatenate([c16.T, c16.T, -s16.T, s16.T], 0)
    rope_k = rk.reshape(64, 8, 512)
    vflag = np.ones((128, 32), f)
    if h == 0:
        vflag[:, 0:16] = 0.0
    return {"rope_d": np.ascontiguousarray(rope_d).astype(f),
            "rope_q": np.ascontiguousarray(rope_q).astype(f),
            "rope_k": np.ascontiguousarray(rope_k).astype(f),
            "vflag": np.ascontiguousarray(vflag)}


class _Stop(Exception):
    pass


class _Step:
    __slots__ = ("qk", "ex", "pv", "post")

    def __init__(self, qk, ex, pv, post=None):
        self.qk, self.ex, self.pv, self.post = qk, ex, pv, post


def build_program(limit=None, dumps=()):
    nc = bass.Bass("TRN2", target_bir_lowering=False)
    S = Sched()

    def done(tag):
        if limit == tag:
            raise _Stop()

    def din(name, shape):
        return nc.dram_tensor(name, list(shape), F32, kind="ExternalInput").ap()

    xT_d = din("xT", [128, 8, NT])
    xq_d = din("xq", [NQ, 1024])
    w_dil_d = din("w_dil", [128, 8, 4, 4, 128])
    w_dr_d = din("w_dr", [128, 8, 4, 128])
    w_lat_d = din("w_lat", [128, 8, 1216])
    w_uqh_d = din("w_uqh", [128, 3, 8, 96])
    w_uqr_d = din("w_uqr", [128, 3, 2, 128])
    w_ukn_d = din("w_ukn", [128, 2, 8, 96])
    w_uv_d = din("w_uv", [128, 2, 512])
    w_out_d = din("w_out", [128, 8, 1024])
    sel_d_d = din("sel_d", [128, 4, 128])
    sel_q_d = din("sel_q", [128, 8, 2, 96])
    sel_k_d = din("sel_k", [32, 96])
    cmat_d = din("cmat", [128, 640])
    gq_d = din("gq", [128, 3])
    gkv_d = din("gkv", [128, 2])
    lnp_d = din("lnp", [128, 2, 1024])
    rope_d_d = din("rope_d", [128, 8, 2, 512])
    rope_q_d = din("rope_q", [128, 4, 2, 512])
    rope_k_d = din("rope_k", [64, 8, 512])
    vflag_d = din("vflag", [128, 32])
    out_d = nc.dram_tensor("out", [NQ, 1024], F32, kind="ExternalOutput").ap()

    ARENA = 207 * 1024
    arena = nc.alloc_sbuf_tensor("arena", [128, ARENA], U8)

    def carve(off, shape, dt):
        esz = 4 if dt == F32 else 2
        n = int(np.prod(shape[1:]))
        assert off % 32 == 0, off
        assert off + n * esz <= ARENA, (off, n * esz, ARENA)
        ap = arena[0:shape[0], off:off + n * esz].bitcast(dt)
        if len(shape) == 3:
            ap = ap.rearrange("p (a b) -> p a b", a=shape[1])
        elif len(shape) == 4:
            ap = ap.rearrange("p (a b c) -> p a b c", a=shape[1], b=shape[2])
        return ap

    K = 1024
    o = 0
    xT = carve(o, [128, 8, NT], BF16); XT_OFF = o; o += 64 * K
    mixT = carve(o, [128, 8, NQ], BF16); o += 32 * K
    NSTG = 2
    stg = [carve(o + i * 4 * K, [128, 1024], F32) for i in range(NSTG)]; o += NSTG * 4 * K
    PT6 = carve(o, [128, 6 * 512], BF16); o += 6 * K
    rden = [carve(o + i * 2 * K, [128, 512], F32) for i in range(2)]; o += 4 * K
    sgt = [carve(o + i * 2 * K, [128, 512], F32) for i in range(2)]; o += 4 * K
    cmat = carve(o, [128, 640], BF16); o += 1280
    ident = cmat[:, 0:128]
    mb4 = cmat[:, 128:640]
    ones_b = carve(o, [128, 128], BF16); o += 256
    m01 = carve(o, [128, 512], BF16); o += 1024
    sel_d = carve(o, [128, 4, 128], BF16); o += 1 * K
    sel_q = carve(o, [128, 8 * 2, 96], BF16); o += 3 * K
    sel_k = carve(o, [32, 96], BF16); o += 192 + 64
    gq = carve(o, [128, 4], F32); o += 32
    gkv = carve(o, [128, 4], F32); o += 32
    small = carve(o, [128, 64], F32); o += 256
    vfl = carve(o, [128, 32], F32); o += 128
    o = (o + 1023) // 1024 * 1024
    R = o
    RSZ = ARENA - R
    wdb = [carve(R, [128, 8, 4, 128], BF16)] * 2
    ropedQ = carve(R + 8 * K, [128, NQ], BF16)
    ropedK = carve(R + 12 * K, [128, NT], BF16)
    qTd = carve(R + 20 * K, [128, NQ], BF16)
    kTd = carve(R + 24 * K, [128, NT], BF16)
    vT = carve(R + 32 * K, [128, NT], BF16)
    wdr = carve(64 * K, [128, 8, 4, 128], BF16)
    Vdb = [carve(R + 40 * K + i * 12 * K, [128, 32, 192], BF16) for i in range(2)]
    accA = carve(R + 64 * K, [128, NQ], F32)
    accB = carve(R + 72 * K, [128, NQ], F32)
    tabd = [carve(R + 64 * K + i * 4 * K, [128, 2, 512], F32) for i in range(2)]
    tmpa = [carve(R + 72 * K + i * 2 * K, [128, 512], F32) for i in range(2)]
    tmpb = [carve(R + 76 * K + i * 2 * K, [128, 512], F32) for i in range(2)]
    assert 80 * K <= RSZ, RSZ
    cqn = carve(R, [128, 3, NQ], BF16)
    ckvn = carve(R + 12 * K, [128, 2, NT], BF16)
    kpe = carve(R + 28 * K, [32, NT], BF16)
    rq1 = carve(R + 36 * K, [128, NQ], BF16)
    rq2 = carve(R + 40 * K, [128, NQ], BF16)
    LT = R + 44 * K
    wlb = [carve(LT + i * 2 * K, [128, 8, 128], BF16) for i in range(6)]
    cf = carve(LT + 12 * K, [128, 3, 512], F32)
    sqb = PT6[:, 0:1536].rearrange("p (a b) -> p a b", a=3)
    rt = PT6[:, 2048:3072].bitcast(F32)
    gtmp = PT6[:, 2048:3072].bitcast(F32)
    cf2 = carve(R + 28 * K, [128, 3, 512], F32)
    sqb2 = carve(R + 34 * K, [128, 3, 512], BF16)
    rt2 = carve(R + 37 * K, [128, 512], F32)
    tabl = carve(LT + 23 * K, [128, 2, 512], F32)
    ta = carve(LT + 27 * K, [128, 512], F32)
    tb = carve(LT + 29 * K, [128, 512], F32)
    w_uqr = carve(LT + 31 * K, [128, 3, 2, 128], BF16)
    tabl2 = carve(LT + 12 * K, [128, 2, 512], F32)
    ta2 = carve(LT + 16 * K, [128, 512], F32)
    tb2 = carve(LT + 18 * K, [128, 512], F32)
    assert 44 * K + 33 * K <= RSZ, RSZ
    X = XT_OFF
    qTA = [carve(X + i * 4 * K, [96, NQ], BF16) for i in range(2)]
    kTA = [carve(X + 8 * K + i * 8 * K, [96, NT], BF16) for i in range(2)]
    VA = [carve(X + 24 * K + i * 12 * K, [128, 32, 192], BF16) for i in range(2)]
    w_uqh = carve(X + 48 * K, [128, 3, 8, 96], BF16)
    w_ukn = carve(X + 53 * K, [128, 2, 8, 96], BF16)
    w_uv = carve(X + 56 * K, [128, 2, 512], BF16)
    lnp = carve(LT + 16 * K, [128, 2, 1024], F32)
    w_out = carve(LT, [128, 8, 1024], BF16)
    xres = [carve(LT + 16 * K + i * 4 * K, [128, 1024], F32) for i in range(2)]
    zb = [carve(LT + 24 * K + i * 4 * K, [128, 1024], F32) for i in range(2)]
    assert 44 * K + 32 * K <= RSZ, RSZ

    pst = nc.alloc_psum_tensor("pst", [128, 8 * 512], F32)

    def bank(i, n=1):
        return pst[:, i * 512:(i + n) * 512]

    dma_sem_names = []

    def dsem(name):
        if name not in dma_sem_names:
            dma_sem_names.append(name)
        return name

    def dma(out_ap, in_ap, reads, writes, sem, eng="sp"):
        return S.add(eng, lambda e: e.dma_start(out=out_ap, in_=in_ap), reads, writes, dsem=dsem(sem))

    def mm(out_ap, lhsT, rhs, start, stop, reads, writes):
        return S.add("pe", lambda e: e.matmul(out_ap, lhsT, rhs, start=start, stop=stop), reads, writes)

    def act(out_ap, in_ap, func, reads, writes, scale=1.0, bias=None):
        if bias is None:
            return S.add("act", lambda e: e.activation(out=out_ap, in_=in_ap, func=func, scale=scale), reads, writes)
        return S.add("act", lambda e: e.activation(out=out_ap, in_=in_ap, func=func, scale=scale, bias=bias), reads, writes)

    def tt(eng, out_ap, in0, in1, op, reads, writes):
        return S.add(eng, lambda e: e.tensor_tensor(out=out_ap, in0=in0, in1=in1, op=op), reads, writes)

    def tsc(eng, out_ap, in0, s1, op0, reads, writes):
        return S.add(eng, lambda e: e.tensor_scalar(out=out_ap, in0=in0, scalar1=s1, scalar2=None, op0=op0), reads, writes)

    def cp(eng, out_ap, in_ap, reads, writes):
        return S.add(eng, lambda e: e.tensor_copy(out=out_ap, in_=in_ap), reads, writes)

    stg_i = [0]
    cast_rr = [0]
    xstg = [carve(R + 40 * K + i * 4 * K, [128, 1024], F32) for i in range(6)]
    ring = {"bufs": xstg, "tok": "xstg", "n": 6}

    XSTG_TOK = [("xstg", k) for k in range(6)]

    def next_stg():
        i = stg_i[0] % ring["n"]
        stg_i[0] += 1
        return ring["bufs"][i], (ring["tok"], i), "%s%d" % (ring["tok"], i)

    def load_cast(dst_ap, src_ap, nelem, writes, engs=("pool",)):
        sbuf_, stok, ssem = next_stg()
        pp = dst_ap.shape[0]
        assert nelem <= 1024
        dma(sbuf_[0:pp, 0:nelem], src_ap, [], [stok], ssem)
        eng = engs[cast_rr[0] % len(engs)]
        cast_rr[0] += 1
        if eng == "act":
            act(dst_ap, sbuf_[0:pp, 0:nelem], AF.Copy, [stok], writes)
        else:
            cp(eng, dst_ap, sbuf_[0:pp, 0:nelem], [stok], writes)

    def load_cast3(dst3, src3, a, b, writes, eng="pool"):
        sbuf_, stok, ssem = next_stg()
        assert a * b <= 1024
        sv = sbuf_[:, 0:a * b].rearrange("p (a b) -> p a b", a=a)
        dma(sv, src3, [], [stok], ssem)
        if eng == "act":
            act(dst3, sv, AF.Copy, [stok], writes)
        else:
            cp(eng, dst3, sv, [stok], writes)

    def write_ones(dst3, eng, wtoks, extra_reads=()):
        src = vfl[:, :].unsqueeze(2).broadcast_to([128, 32, 64])
        if eng == "act":
            act(dst3, src, AF.Copy, ["vfl"] + list(extra_reads), list(wtoks))
        else:
            cp(eng, dst3, src, ["vfl"] + list(extra_reads), list(wtoks))

    psr = {"s": [0, 1, 2], "o": [3, 4], "g": [5, 6, 7]}
    psi = {"s": 0, "o": 0, "g": 0}

    def psum(kind):
        lst = psr[kind]
        i = lst[psi[kind] % len(lst)]
        psi[kind] += 1
        return i

    def pb(i):
        return ("ps", i)

    pti = [0]

    def ptbuf():
        i = pti[0] % 4
        pti[0] += 1
        return i

    pt2i = [0]

    def ptbuf2():
        i = (pt2i[0] % 3) * 2
        pt2i[0] += 1
        return i

    def PT(i, n=1):
        return PT6[:, i * 512:(i + n) * 512]

    def tok(c, n=512):
        return slice(c * n, (c + 1) * n)

    def gate_evac(dst_ap, pa, wtok, tbuf, ttok):
        act(tbuf, bank(pa), AF.Tanh, [pb(pa)], [ttok], scale=0.5)
        S.add("dve", (lambda e: e.scalar_tensor_tensor(out=dst_ap, in0=tbuf, scalar=1.0, in1=bank(pa),
                                                       op0=ALU.add, op1=ALU.mult)),
              [ttok, pb(pa)], [wtok])

    def strided(ap2, start, d, n=128):
        if d == 1:
            return ap2[:, start:start + n]
        r = start % d
        return ap2[:, start - r: start - r + n * d].rearrange("p (i r) -> p r i", r=d)[:, r, :]

    def run_pipeline(steps, fill_iter, nf, LA):
        n = len(steps)
        fi = 0
        tot = n + LA
        for i in range(tot):
            if i < n:
                steps[i].qk()
                steps[i].ex()
            j = i - LA
            if j >= 0:
                steps[j].pv()
                if steps[j].post is not None:
                    steps[j].post()
            while fi < nf and fi * tot < (i + 1) * nf:
                next(fill_iter)
                fi += 1
        for _ in fill_iter:
            pass

    def gen_of(units):
        for u in units:
            u()
            yield

    def xT_tok(ci, c):
        return ("xT", ci, c)

    try:
        pre_tokens = [("tabd", i) for i in range(2)] + [("tmpa", i) for i in range(2)] + [("tmpb", i) for i in range(2)]

        DILC = ((1, 32), (4, 8), (16, 2))

        def dil_proj_units(p, parts=False):
            wd = wdb[p % 2]
            wt = lambda kind: ("wd", 0, kind)
            units = []

            def ku(c):
                pa = psum("g")
                for ci in range(8):
                    mm(bank(pa), wd[:, ci, 1, :], xT[:, ci, tok(c)], ci == 0, False, [wt(1), xT_tok(ci, c)], [pb(pa)])
                mm(bank(pa), sel_d[:, p, :], ropedK[:, tok(c)], False, True, ["sel_d", ("rK", c)], [pb(pa)])
                if p > 0:
                    act(kTd[:, tok(c)], bank(pa), AF.Copy, [pb(pa)], [("kTd", c)])
                else:
                    cp("dve", kTd[:, tok(c)], bank(pa), [pb(pa)], [("kTd", c)])

            def qu(c):
                pa = psum("g")
                for ci in range(8):
                    mm(bank(pa), wd[:, ci, 0, :], xT[:, ci, tok(4 + c)], ci == 0, False, [wt(0), xT_tok(ci, 4 + c)], [pb(pa)])
                mm(bank(pa), sel_d[:, p, :], ropedQ[:, tok(c)], False, True, ["sel_d", ("rQ", c)], [pb(pa)])
                tsc("dve", qTd[:, tok(c)], bank(pa), 0.125, ALU.mult, [pb(pa)], [("qTd", c)])

            def gu(c):
                pa = psum("g")
                for ci in range(8):
                    mm(bank(pa), wd[:, ci, 3, :], xT[:, ci, tok(4 + c)], ci == 0, ci == 7, [wt(3), xT_tok(ci, 4 + c)], [pb(pa)])
                gate_evac(mixT[:, 4 + p, tok(c)], pa, ("mixT", 4 + p, c), gtmp[:, :], "gtmp")

            def vtu(c):
                pa = psum("g")
                for ci in range(8):
                    mm(bank(pa), wd[:, ci, 2, :], xT[:, ci, tok(c)], ci == 0, ci == 7, [wt(2), xT_tok(ci, c)], [pb(pa)])
                if p > 0:
                    act(vT[:, tok(c)], bank(pa), AF.Copy, [pb(pa)], [("vT", c)], scale=0.5)
                else:
                    tsc("dve", vT[:, tok(c)], bank(pa), 0.5, ALU.mult, [pb(pa)], [("vT", c)])

            if parts:
                return ku, vtu, qu, gu
            for c in range(8):
                units.append(lambda c=c: ku(c))
            for c in range(8):
                units.append(lambda c=c: vtu(c))
            for c in range(4):
                units.append(lambda c=c: qu(c))
            for c in range(4):
                units.append(lambda c=c: gu(c))
            return units

        def load_pair_weights(p, kinds=(1, 2, 0, 3), engs=("act", "pool")):
            wd = wdb[p % 2]
            for ki, kind in enumerate(kinds):
                load_cast3(wd[:, :, kind, :], w_dil_d[:, :, p, kind, :], 8, 128, [("wd", 0, kind)],
                           eng=engs[ki % len(engs)])

        dma(vfl[:, :], vflag_d[:, :], [], ["vfl"], "c3")
        S.add("pool", lambda e: e.memset(ones_b[:, :], 1.0), [], ["ones_b"])
        S.add("pool", lambda e: e.memset(small[:, 0:1], RMS_EPS), [], ["small"])
        S.add("pool", lambda e: e.memset(small[:, 1:2], LN_EPS), [], ["small"])
        for kind in (2, 3):
            load_cast3(wdr[:, :, kind, :], w_dr_d[:, :, kind, :], 8, 128, [("wdr", kind)], eng=("act", "dve")[kind % 2])
        load_cast(sel_d.rearrange("p a b -> p (a b)"), sel_d_d.rearrange("p a b -> p (a b)"), 512, ["sel_d"], engs=("dve",))
        load_pair_weights(0, kinds=(1, 2), engs=("act", "dve"))
        xcnt = [0]

        def load_x_quarter(tq):
            for ci in range(8):
                sbuf_, stok, ssem = next_stg()
                dst = xT[:, ci, tq * 1024:(tq + 1) * 1024]
                dma(sbuf_[:, :], xT_d[:, ci, tq * 1024:(tq + 1) * 1024], [], [stok], ssem)
                wr = [("xT", ci, tq * 2), ("xT", ci, tq * 2 + 1)]
                if xcnt[0] % 2 == 1:
                    act(dst, sbuf_[:, :], AF.Copy, [stok], wr)
                else:
                    cp("dve", dst, sbuf_[:, :], [stok], wr)
                xcnt[0] += 1

        load_x_quarter(0)
        load_cast(cmat[:, :], cmat_d[:, :], 640, ["cmat"], engs=("act",))
        S.add("dve", lambda e: e.tensor_scalar(out=m01[:, :], in0=mb4, scalar1=-1.0, scalar2=None, op0=ALU.is_ge),
              ["cmat"], ["m01"])
        load_cast(sel_q[:, 0:8, :].rearrange("p a b -> p (a b)"),
                  sel_q_d.rearrange("p h t c -> p (h t c)")[:, 0:768], 768, ["sel_q"], engs=("dve",))
        load_cast(sel_q[:, 8:16, :].rearrange("p a b -> p (a b)"),
                  sel_q_d.rearrange("p h t c -> p (h t c)")[:, 768:1536], 768, ["sel_q"], engs=("act",))
        load_cast(sel_k[:, :], sel_k_d[:, :], 96, ["sel_k"], engs=("dve",))
        dma(gq[:, 0:3], gq_d[:, :], [], ["gq"], "c0")
        dma(gkv[:, 0:2], gkv_d[:, :], [], ["gkv"], "c2")

        rci = [0]

        def rc(kindA, kindB, dst, c, cc, dname):
            bi_ = rci[0] % 2
            rci[0] += 1
            dma(tabd[bi_][:, :, :], rope_d_d[:, c, :, :], [], [("tabd", bi_)], f"tabd{bi_}", eng="pool")
            pa = psum("g")
            pb_ = psum("g")
            for ci in range(8):
                mm(bank(pa), wdr[:, ci, kindA, :], xT[:, ci, tok(c)], ci == 0, ci == 7,
                   [("wdr", kindA), xT_tok(ci, c)], [pb(pa)])
            for ci in range(8):
                mm(bank(pb_), wdr[:, ci, kindB, :], xT[:, ci, tok(c)], ci == 0, ci == 7,
                   [("wdr", kindB), xT_tok(ci, c)], [pb(pb_)])
            tt("dve", tmpa[bi_][:, :], bank(pa), tabd[bi_][:, 0, :], ALU.mult, [pb(pa), ("tabd", bi_)], [("tmpa", bi_)])
            tt("dve", tmpb[bi_][:, :], bank(pb_), tabd[bi_][:, 1, :], ALU.mult, [pb(pb_), ("tabd", bi_)], [("tmpb", bi_)])
            tt("dve", dst[:, cc * 512: cc * 512 + 512], tmpa[bi_][:, :], tmpb[bi_][:, :], ALU.add,
               [("tmpa", bi_), ("tmpb", bi_)], [(dname, cc)])

        ku0, vtu0, qu0, gu0 = dil_proj_units(0, parts=True)

        def kwork(c):
            rc(2, 3, ropedK, c, c, "rK")
            ku0(c)
            vtu0(c)

        kwork(0)
        kwork(1)
        load_x_quarter(1)
        for kind in (0, 1):
            load_cast3(wdr[:, :, kind, :], w_dr_d[:, :, kind, :], 8, 128, [("wdr", kind)], eng=("act", "dve")[kind % 2])
        load_pair_weights(0, kinds=(0, 3), engs=("act", "dve"))
        kwork(2)
        kwork(3)
        load_x_quarter(2)
        kwork(4)
        kwork(5)
        load_x_quarter(3)
        ring.update(bufs=stg, tok="stg", n=NSTG)
        stg_i[0] = 0
        done("p0")
        kwork(6)
        kwork(7)
        for c in range(4):
            rc(0, 1, ropedQ, 4 + c, c, "rQ")
            qu0(c)
            gu0(c)
        done("d0")
        for vb_ in range(2):
            write_ones(Vdb[vb_][:, :, 64:128], "act", [("Vd_ones", vb_, g4) for g4 in range(4)] + XSTG_TOK)

        def dil_v_units(p, bi):
            wd = wdb[p % 2]
            d, nbl = DILC[bi]
            vb_ = (3 * p + bi) % 2
            Vd = Vdb[vb_]
            jlist = [n_ * d + r for n_ in range(nbl // 2 - 1, nbl) for r in range(d)]
            units = []

            def vu(grp):
                pa = psum("g")
                pbf = bank(pa).bitcast(BF16)
                for gi, j in enumerate(grp):
                    n_, r = divmod(j, d)
                    t0 = d * 128 * n_ + r
                    c_lo = t0 // 512
                    c_hi = (t0 + d * 127) // 512
                    rd = ["cmat"] + [("vT", cc) for cc in range(c_lo, c_hi + 1)]
                    S.add("pe", (lambda e, gi=gi, t0=t0: e.transpose(pbf[:, gi * 128:(gi + 1) * 128],
                                                                      strided(vT[:, :], t0, d), ident)),
                          rd, [pb(pa)])
                ng = len(grp)
                j0 = grp[0]
                assert grp == list(range(j0, j0 + ng))
                src = pbf[:, 0:ng * 128].rearrange("p (a t b) -> p a t b", a=ng, t=2)
                cp("dve", Vd[:, j0:j0 + ng, 0:64], src[:, :, 0, :], [pb(pa)], [("Vd", vb_, j) for j in grp] + XSTG_TOK)
                cp("dve", Vd[:, j0:j0 + ng, 128:192], src[:, :, 1, :], [pb(pa)], [("VdB", vb_, j) for j in grp] + XSTG_TOK)

            for g0 in range(0, len(jlist), 4):
                units.append(lambda grp=jlist[g0:g0 + 4]: vu(grp))
            return units

        def dil_attn_steps(p, bi):
            d, nbl = DILC[bi]
            vb_ = (3 * p + bi) % 2
            Vd = Vdb[vb_]
            steps = []
            for hh in range(2):
                rows = slice(hh * 64, hh * 64 + 64)
                vcols = slice(0, 128) if hh == 0 else slice(64, 192)
                acc = accA if hh == 0 else accB
                qblocks = [(n_, r) for n_ in range(nbl // 2, nbl) for r in range(d)]
                for s0 in range(0, 16, 2):
                    st = {}

                    def qk(s0=s0, st=st, rows=rows, qblocks=qblocks):
                        sb = psum("s")
                        st["sb"] = sb
                        for qi in range(2):
                            n_, r = qblocks[s0 + qi]
                            qc0 = d * 128 * (n_ - nbl // 2) + r
                            qrd = [("qTd", cc) for cc in range(qc0 // 512, (qc0 + d * 127) // 512 + 1)]
                            for pc in range(2):
                                k0 = d * 128 * (n_ - 1 + pc) + r
                                krd = [("kTd", cc) for cc in range(k0 // 512, (k0 + d * 127) // 512 + 1)]
                                mm(bank(sb)[:, (qi * 2 + pc) * 128:(qi * 2 + pc + 1) * 128],
                                   strided(kTd[rows, :], k0, d), strided(qTd[rows, :], qc0, d), True, True,
                                   krd + qrd, [pb(sb)])

                    def ex(st=st):
                        pt = ptbuf()
                        st["pt"] = pt
                        act(PT(pt), bank(st["sb"]), AF.Exp, [pb(st["sb"])], [("PT", pt)])
                        tt("dve", PT(pt), PT(pt), m01[:, :], ALU.mult, [("PT", pt), "m01"], [("PT", pt)])

                    def pv(s0=s0, st=st, vcols=vcols, qblocks=qblocks, hh=hh):
                        key = (p, bi, hh, s0 // 4)
                        if s0 % 4 == 0:
                            otile[key] = psum("o")
                        po = otile[key]
                        pt = st["pt"]
                        for qi in range(2):
                            n_, r = qblocks[s0 + qi]
                            oi = (s0 % 4) + qi
                            for pc in range(2):
                                j = (n_ - 1 + pc) * d + r
                                mm(bank(po)[:, oi * 128:(oi + 1) * 128], Vd[:, j, vcols],
                                   PT(pt)[:, (qi * 2 + pc) * 128:(qi * 2 + pc + 1) * 128],
                                   pc == 0, pc == 1,
                                   [("Vd", vb_, j), ("VdB", vb_, j), ("Vd_ones", vb_, j // 8), ("PT", pt)], [pb(po)])

                    def post(s0=s0, acc=acc, qblocks=qblocks, hh=hh):
                        if s0 % 4 != 2:
                            return
                        o0 = s0 - 2
                        po = otile[(p, bi, hh, s0 // 4)]
                        if d == 1:
                            n_, r = qblocks[o0]
                            qc0 = 128 * (n_ - nbl // 2)
                            dsta = acc[:, qc0:qc0 + 512]
                            srca = bank(po)
                            atoks = [("acc", hh, qc0 // 512)]
                        elif d == 4:
                            n_ = qblocks[o0][0]
                            base = 512 * (n_ - nbl // 2)
                            dsta = acc[:, base:base + 512].rearrange("p (i r) -> p r i", r=4)
                            srca = bank(po).rearrange("p (r i) -> p r i", r=4)
                            atoks = [("acc", hh, base // 512)]
                        else:
                            r0 = qblocks[o0][1]
                            dsta = acc[:, :].rearrange("p (i r) -> p r i", r=16)[:, r0:r0 + 4, :]
                            srca = bank(po).rearrange("p (r i) -> p r i", r=4)
                            atoks = [("acc", hh, cc) for cc in range(4)]
                        if bi == 0:
                            cp("dve", dsta, srca, [pb(po)], atoks + (pre_tokens if p == 0 else []))
                        else:
                            tt("dve", dsta, dsta, srca, ALU.add, [pb(po)] + atoks, atoks)

                    steps.append(_Step(qk, ex, pv, post))
            return steps

        def dil_norm_ops(p):
            p1s, p2s = [], []
            for hh in range(2):
                acc = accA if hh == 0 else accB
                orow = slice(hh * 64, hh * 64 + 64)
                drow = slice(64 - hh * 64, 128 - hh * 64)
                for c in range(4):
                    i = (hh * 4 + c) % 2

                    def p1(hh=hh, acc=acc, orow=orow, drow=drow, c=c, i=i):
                        act(rden[i][orow, :], acc[drow, tok(c)], AF.Ln, [("acc", hh, c)], [("rden", i)])
                        act(rden[i][orow, :], rden[i][orow, :], AF.Exp, [("rden", i)], [("rden", i)], scale=-1.0)

                    def p2(hh=hh, acc=acc, orow=orow, c=c, i=i):
                        tt("dve", sgt[i][orow, :], mixT[orow, 4 + p, tok(c)], rden[i][orow, :], ALU.mult,
                           [("mixT", 4 + p, c), ("rden", i)], [("sgt", i)])
                        tt("dve", mixT[orow, 4 + p, tok(c)], acc[orow, tok(c)], sgt[i][orow, :], ALU.mult,
                           [("acc", hh, c), ("sgt", i)], [("mixT", 4 + p, c)])
                    p1s.append(p1)
                    p2s.append(p2)
            ops = []
            for k in range(len(p1s) + 1):
                def slot(k=k):
                    if k < len(p1s):
                        p1s[k]()
                    if k >= 1:
                        p2s[k - 1]()
                ops.append(slot)
            return ops

        otile = {}
        pending_norm = []
        next_units = None
        carried = 0
        for p in range(4):
            units = next_units if p > 0 else []
            gate_units = units[20:24] if units else []
            v0 = dil_v_units(p, 0)
            v0i = 0
            for ui in range(carried, 20 if units else 0):
                units[ui]()
                if ui % 2 == 1 and ui < 18 and pending_norm:
                    pending_norm.pop(0)()
                if ui >= 15 and v0i < len(v0):
                    v0[v0i]()
                    v0i += 1
            while pending_norm:
                pending_norm.pop(0)()
            while v0i < len(v0):
                v0[v0i]()
                v0i += 1
            done("dproj%d" % p)
            for bi in range(3):
                if bi == 1 and p + 1 < 4:
                    load_pair_weights(p + 1)
                    next_units = dil_proj_units(p + 1)
                steps = dil_attn_steps(p, bi)
                n = len(steps)
                if bi == 0:
                    early = gate_units + dil_v_units(p, 1)
                elif bi == 1:
                    early = dil_v_units(p, 2)
                else:
                    early = []
                late = next_units[0:3] if (bi == 2 and p + 1 < 4) else []

                def sched(early=early, late=late, n=n):
                    ne = len(early)
                    ei = 0
                    for it in range(n + 3):
                        while ei < ne and ei * max(1, n - 4) < (it + 1) * ne:
                            early[ei]()
                            ei += 1
                        if it >= n and (it - n) < len(late):
                            late[it - n]()
                        yield

                run_pipeline(steps, sched(), n + 3, 3)
            carried = 3 if p + 1 < 4 else 0
            done("dattn%d" % p)
            pending_norm = dil_norm_ops(p)
            if limit in ("dpair%d" % p, "d"):
                while pending_norm:
                    pending_norm.pop(0)()
            done("dpair%d" % p)
        done("d")

        dil_tokens = ([("wd", 0, k) for k in range(4)] + [("wdr", k) for k in range(4)] + [("vT", c) for c in range(8)]
                      + [("rQ", c) for c in range(4)]
                      + [("rK", c) for c in range(8)] + [("qTd", c) for c in range(4)] + [("kTd", c) for c in range(8)]
                      + [("Vd", b_, j) for b_ in range(2) for j in range(32)]
                      + [("VdB", b_, j) for b_ in range(2) for j in range(32)]
                      + [("Vd_ones", b_, g) for b_ in range(2) for g in range(4)]
                      + [("PT", i) for i in range(6)] + ["gtmp"])
        S.add("pool", lambda e: e.memset(small[:, 60:61], 0.0), [], dil_tokens + ["Rreg1"])
        S.phase_reads = ["Rreg1"]

        def load_wl(buf, col0, ncol):
            load_cast3(wlb[buf][:, :, 0:ncol], w_lat_d[:, :, col0:col0 + ncol], 8, ncol, [("wl", buf)], eng="act")

        for ft in range(3):
            load_wl(ft, ft * 128, 128)
        load_wl(3, 384, 128)
        load_wl(4, 512, 128)
        load_wl(5, 640, 128)

        CF = [cf, cf2]
        SQB = [sqb, sqb2]
        RT = [rt, rt2]
        rmsi = [0]

        def rms_latent(bufs, c_lo, n_chunks, gains, dst, nfeat, name):
            nft = len(bufs)
            for cc in range(n_chunks):
                c = c_lo + cc
                k2 = rmsi[0] % 2
                rmsi[0] += 1
                cf_, sqb_, rt_ = CF[k2], SQB[k2], RT[k2]
                for ft in range(nft):
                    pa = psum("g")
                    for ci in range(8):
                        mm(bank(pa), wlb[bufs[ft]][:, ci, :], xT[:, ci, tok(c)], ci == 0, ci == 7,
                           [("wl", bufs[ft]), xT_tok(ci, c)], [pb(pa)])
                    cp("dve", cf_[:, ft, :], bank(pa), [pb(pa)], [("cf", k2, ft)])
                    act(sqb_[:, ft, :], bank(pa), AF.Square, [pb(pa)], [("sqb", k2, ft)])
                    if pending_norm:
                        pending_norm.pop(0)()
                pq = psum("g")
                for ft in range(nft):
                    mm(bank(pq), ones_b[:, :], sqb_[:, ft, :], ft == 0, ft == nft - 1, ["ones_b", ("sqb", k2, ft)], [pb(pq)])
                act(rt_[:, :], bank(pq), AF.Ln, [pb(pq), "small"], [("rt", k2)], scale=1.0 / nfeat, bias=small[:, 0:1])
                act(rt_[:, :], rt_[:, :], AF.Exp, [("rt", k2)], [("rt", k2)], scale=-0.5)
                for ft in range(nft):
                    S.add("dve", (lambda e, ft=ft, cc=cc, cf_=cf_, rt_=rt_: e.scalar_tensor_tensor(
                        out=dst[:, ft, tok(cc)], in0=cf_[:, ft, :], scalar=gains[:, ft:ft + 1], in1=rt_[:, :],
                        op0=ALU.mult, op1=ALU.mult)),
                        [("cf", k2, ft), ("rt", k2), "gq", "gkv"], [(name, ft, cc)])

        rms_latent([0, 1, 2], 4, 4, gq, cqn, 384.0, "cqn")
        done("l1")
        load_wl(0, 768, 128)
        load_wl(1, 896, 128)
        load_wl(2, 1024, 128)
        rms_latent([3, 4], 0, 8, gkv, ckvn, 256.0, "ckvn")
        done("l2")
        load_wl(3, 1152, 64)
        for ft, buf in enumerate((5, 0, 1, 2)):
            for c in range(4):
                pa = psum("g")
                for ci in range(8):
                    mm(bank(pa), wlb[buf][:, ci, :], xT[:, ci, tok(4 + c)], ci == 0, ci == 7,
                       [("wl", buf), xT_tok(ci, 4 + c)], [pb(pa)])
                gk = (ft * 4 + c) % 2
                gbuf, gtk = ((sgt[0][:, :], ("sgt", 0)), (sgt[1][:, :], ("sgt", 1)))[gk]
                gate_evac(mixT[:, ft, tok(c)], pa, ("mixT", ft, c), gbuf, gtk)
        done("l3")
        while pending_norm:
            pending_norm.pop(0)()
        S.add("pool", lambda e: e.memset(small[:, 58:59], 0.0), [],
              [("acc", hh, c) for hh in range(2) for c in range(4)] + pre_tokens
              + [("cf", 1, i) for i in range(3)] + [("sqb", 1, i) for i in range(3)] + [("rt", 1), "Rreg2"])
        S.phase_reads = ["Rreg1", "Rreg2"]
        load_cast3(w_uqr.rearrange("p a t c -> p a (t c)"), w_uqr_d.rearrange("p a t c -> p a (t c)"), 3, 256, ["w_uqr"], eng="act")
        TAB = [tabl, tabl2]
        TA = [ta, ta2]
        TB = [tb, tb2]
        al2 = [("cf", 0, i) for i in range(3)]
        for c in range(8):
            k2 = c % 2
            ex2 = al2 if (k2 == 1 and c == 1) else []
            dma(TAB[k2][0:64, 0, :], rope_k_d[:, c, :], [], [("tabl", k2)] + ex2, f"tabl{k2}", eng="pool")
            pa = psum("g")
            for ci in range(8):
                mm(bank(pa)[0:64, :], wlb[3][:, ci, 0:64], xT[:, ci, tok(c)], ci == 0, ci == 7,
                   [("wl", 3), xT_tok(ci, c)], [pb(pa)])
            tt("dve", TA[k2][0:32, :], bank(pa)[0:32, :], TAB[k2][0:32, 0, :], ALU.mult, [pb(pa), ("tabl", k2)],
               [("ta", k2)] + ex2)
            tt("dve", TB[k2][0:32, :], bank(pa)[32:64, :], TAB[k2][32:64, 0, :], ALU.mult, [pb(pa), ("tabl", k2)],
               [("tb", k2)] + ex2)
            tt("dve", kpe[0:32, tok(c)], TA[k2][0:32, :], TB[k2][0:32, :], ALU.add, [("ta", k2), ("tb", k2)], [("kpe", c)])
        done("l4")
        for c in range(4):
            k2 = c % 2
            dma(TAB[k2][:, :, :], rope_q_d[:, c, :, :], [], [("tabl", k2)], f"tabl{k2}", eng="pool")
            p1 = psum("g")
            p2 = psum("g")
            for c3 in range(3):
                mm(bank(p1), w_uqr[:, c3, 0, :], cqn[:, c3, tok(c)], c3 == 0, c3 == 2, ["w_uqr", ("cqn", c3, c)], [pb(p1)])
            for c3 in range(3):
                mm(bank(p2), w_uqr[:, c3, 1, :], cqn[:, c3, tok(c)], c3 == 0, c3 == 2, ["w_uqr", ("cqn", c3, c)], [pb(p2)])
            tt("dve", TA[0][:, :], bank(p1), TAB[k2][:, 0, :], ALU.mult, [pb(p1), ("tabl", k2)], [("ta", 0)])
            tt("dve", TB[0][:, :], bank(p2), TAB[k2][:, 1, :], ALU.mult, [pb(p2), ("tabl", k2)], [("tb", 0)])
            tt("dve", rq1[:, tok(c)], TA[0][:, :], TB[0][:, :], ALU.subtract, [("ta", 0), ("tb", 0)], [("rq1", c)])
            tt("dve", TA[1][:, :], bank(p1), TAB[k2][:, 1, :], ALU.mult, [pb(p1), ("tabl", k2)], [("ta", 1)])
            tt("dve", TB[1][:, :], bank(p2), TAB[k2][:, 0, :], ALU.mult, [pb(p2), ("tabl", k2)], [("tb", 1)])
            tt("dve", rq2[:, tok(c)], TA[1][:, :], TB[1][:, :], ALU.add, [("ta", 1), ("tb", 1)], [("rq2", c)])
        done("l")

        all_xT = [("xT", ci, c) for ci in range(8) for c in range(8)]
        lat_tmp = ([("cf", k_, i) for k_ in range(2) for i in range(3)] + [("sqb", k_, i) for k_ in range(2) for i in range(3)]
                   + [("rt", 0), ("rt", 1), "w_uqr"] + [("tabl", i) for i in range(2)] + [("ta", i) for i in range(2)]
                   + [("tb", i) for i in range(2)] + [("wl", i) for i in range(6)])
        S.add("pool", lambda e: e.memset(small[:, 61:62], 0.0), [], all_xT + lat_tmp + ["xTreg"])
        S.phase_reads = ["xTreg"]
        psr["s"] = [0, 2]
        psr["o"] = [4, 5]
        psr["g"] = [6, 7]

        write_ones(VA[0][:, :, 64:128], "dve", [("VA1s", 0, g4) for g4 in range(4)])
        write_ones(VA[1][:, :, 64:128], "dve", [("VA1s", 1, g4) for g4 in range(4)])
        for c2 in range(2):
            load_cast(w_ukn[:, c2, :, :].rearrange("p a b -> p (a b)"),
                      w_ukn_d[:, c2, :, :].rearrange("p a b -> p (a b)"), 768, ["w_ukn"], engs=("act", "dve"))
        for c2 in range(2):
            load_cast(w_uv[:, c2, :], w_uv_d[:, c2, :], 512, ["w_uv"], engs=("act", "dve"))
        for c3 in range(3):
            load_cast(w_uqh[:, c3, :, :].rearrange("p a b -> p (a b)"),
                      w_uqh_d[:, c3, :, :].rearrange("p a b -> p (a b)"), 768, ["w_uqh"], engs=("act", "dve"))

        def mla_late_loads():
            pass
            dma(lnp[:, :, :], lnp_d[:, :, :], [], ["lnp"], "c1")
            for ci in range(8):
                load_cast(w_out[:, ci, :], w_out_d[:, ci, :], 1024, [("w_out", ci)], engs=("pool",))

        SC_A = 96.0 ** -0.5

        def mla_units(h):
            b = h % 2
            vb2 = (h // 2) % 2
            for c in range(8):
                pa = psum("g")
                for c2 in range(2):
                    mm(bank(pa)[0:96, :], w_ukn[:, c2, h, :], ckvn[:, c2, tok(c)], c2 == 0, False,
                       ["w_ukn", ("ckvn", c2, c)], [pb(pa)])
                    yield
                mm(bank(pa)[0:96, :], sel_k[:, :], kpe[0:32, tok(c)], False, True, ["sel_k", ("kpe", c)], [pb(pa)])
                cp("dve", kTA[b][:, tok(c)], bank(pa)[0:96, :], [pb(pa)], [("kTA", b, c)])
                yield
                if b == 0:
                    pa = psum("g")
                    for j in range(4):
                        tb_ = c * 4 + j
                        for c2 in range(2):
                            mm(bank(pa)[:, j * 128:(j + 1) * 128], ckvn[:, c2, tb_ * 128:(tb_ + 1) * 128],
                               w_uv[:, c2, h * 64:(h + 2) * 64], c2 == 0, c2 == 1,
                               ["w_uv", ("ckvn", c2, c)], [pb(pa)])
                        if j == 1:
                            yield
                    srcp = bank(pa).rearrange("p (a t b) -> p a t b", a=4, t=2)
                    tsc("dve", VA[vb2][:, c * 4:(c + 1) * 4, 0:64], srcp[:, :, 0, :], 0.5, ALU.mult,
                        [pb(pa)], [("VA", vb2, c)])
                    tsc("dve", VA[vb2][:, c * 4:(c + 1) * 4, 128:192], srcp[:, :, 1, :], 0.5, ALU.mult,
                        [pb(pa)], [("VAB", vb2, c)])
                    yield
            for c in range(4):
                pa = psum("g")
                for c3 in range(3):
                    mm(bank(pa)[0:96, :], w_uqh[:, c3, h, :], cqn[:, c3, tok(c)], c3 == 0, False,
                       ["w_uqh", ("cqn", c3, c)], [pb(pa)])
                    if c3 == 1:
                        yield
                mm(bank(pa)[0:96, :], sel_q[:, 2 * h, :], rq1[:, tok(c)], False, False, ["sel_q", ("rq1", c)], [pb(pa)])
                yield
                mm(bank(pa)[0:96, :], sel_q[:, 2 * h + 1, :], rq2[:, tok(c)], False, True, ["sel_q", ("rq2", c)], [pb(pa)])
                tsc("dve", qTA[b][:, tok(c)], bank(pa)[0:96, :], SC_A, ALU.mult, [pb(pa)], [("qTA", b, c)])
                yield

        def mla_nf(h):
            return 8 * 3 + 4 * 3 + (8 * 2 if h % 2 == 0 else 0)

        def mla_steps(h):
            b = h % 2
            vb2 = (h // 2) % 2
            vsl = slice(0, 128) if b == 0 else slice(64, 192)
            orow = slice(b * 64, b * 64 + 64)
            drow = slice(64 - b * 64, 128 - b * 64)
            steps = []
            for c in range(4):
                nkb = 16 + 4 * c + 4
                ost = {}
                for j in range(nkb // 2):
                    st = {}

                    def qk(c=c, j=j, st=st):
                        sb = psum("s")
                        st["sb"] = sb
                        st["w"] = []
                        for u in range(2):
                            kb = 2 * j + u
                            i = kb - (16 + 4 * c)
                            kc = kb // 4
                            bk = sb + u
                            ksl = slice(kb * 128, (kb + 1) * 128)
                            if i < 0:
                                mm(bank(bk), kTA[b][:, ksl], qTA[b][:, tok(c)], True, True,
                                   [("kTA", b, kc), ("qTA", b, c)], [pb(bk)])
                                st["w"].append(512)
                            else:
                                ncol = 512 - 128 * i
                                q0 = c * 512 + 128 * i
                                mm(bank(bk)[:, 0:128], ident, mb4[:, 128:256], True, False, ["cmat"], [pb(bk)])
                                mm(bank(bk)[:, 0:128], kTA[b][:, ksl], qTA[b][:, q0:q0 + 128], False, True,
                                   [("kTA", b, kc), ("qTA", b, c)], [pb(bk)])
                                if ncol > 128:
                                    mm(bank(bk)[:, 128:ncol], kTA[b][:, ksl], qTA[b][:, q0 + 128:(c + 1) * 512],
                                       True, True, [("kTA", b, kc), ("qTA", b, c)], [pb(bk)])
                                st["w"].append(ncol)

                    def ex(st=st):
                        pt = ptbuf2()
                        st["pt"] = pt
                        sb = st["sb"]
                        if st["w"] == [512, 512]:
                            act(PT(pt, 2), bank(sb, 2), AF.Exp, [pb(sb), pb(sb + 1)], [("PT", pt), ("PT", pt + 1)])
                        else:
                            for u in range(2):
                                w = st["w"][u]
                                act(PT(pt + u)[:, 0:w], bank(sb + u)[:, 0:w], AF.Exp, [pb(sb + u)], [("PT", pt + u)])

                    def pv(c=c, j=j, st=st, ost=ost, nkb=nkb):
                        if j == 0:
                            ost["po"] = psum("o")
                        po = ost["po"]
                        pt = st["pt"]
                        for u in range(2):
                            kb = 2 * j + u
                            w = st["w"][u]
                            mm(bank(po)[:, 512 - w:512], VA[vb2][:, kb, vsl], PT(pt + u)[:, 0:w], kb == 0, kb == nkb - 1,
                               [("VA", vb2, kb // 4), ("VAB", vb2, kb // 4), ("VA1s", vb2, kb // 8), ("PT", pt + u)],
                               [pb(po)])

                    def post(c=c, j=j, ost=ost, nkb=nkb):
                        if j != nkb // 2 - 1:
                            return
                        po = ost["po"]
                        ri = (h * 4 + c) % 2
                        S.add("dve", (lambda e: e.reciprocal(out=rden[ri][orow, :], in_=bank(po)[drow, :])),
                              [pb(po)], [("rden", ri)])
                        tt("pool", sgt[ri][orow, :], mixT[orow, h // 2, tok(c)], rden[ri][orow, :], ALU.mult,
                           [("mixT", h // 2, c), ("rden", ri)], [("sgt", ri)])
                        tt("dve", mixT[orow, h // 2, tok(c)], bank(po)[orow, :], sgt[ri][orow, :], ALU.mult,
                           [pb(po), ("sgt", ri)], [("mixT", h // 2, c)])

                    steps.append(_Step(qk, ex, pv, post))
            return steps

        for _ in mla_units(0):
            pass
        mla_late_loads()
        done("mproj0")
        for h in range(8):
            if h < 7:
                run_pipeline(mla_steps(h), mla_units(h + 1), mla_nf(h + 1), 2)
            else:
                run_pipeline(mla_steps(h), iter(()), 0, 2)
            done("mhead%d" % h)
        done("m")

        psr["g"] = [0, 1, 2, 3, 6, 7]
        NZ = 4
        NXR = 3
        zb3 = [carve(X + i * 4 * K, [128, 1024], F32) for i in range(NZ)]
        xres = [carve(X + 16 * K + i * 4 * K, [128, 1024], F32) for i in range(NXR)]
        mla_tok = ([("qTA", b_, c) for b_ in range(2) for c in range(4)] + [("kTA", b_, c) for b_ in range(2) for c in range(8)]
                   + [("VA", b_, g) for b_ in range(2) for g in range(8)] + [("VAB", b_, g) for b_ in range(2) for g in range(8)]
                   + [("VA1s", b_, g) for b_ in range(2) for g in range(4)])
        S.add("pool", lambda e: e.memset(small[:, 57:58], 0.0), [], mla_tok + ["finreg"])
        S.phase_reads = ["xTreg", "finreg"]

        def fin_A(t):
            bi_ = t % NXR
            zi = t % NZ
            s3 = t % 3
            so = 8 + s3 * 12
            dma(xres[bi_][:, :], xq_d[t * 128:(t + 1) * 128, :], [], [("xres", bi_)], f"xres{bi_}", eng="pool")
            c = t // 4
            for nh in range(2):
                pa = psum("g")
                for ci in range(8):
                    mm(bank(pa), mixT[:, ci, t * 128:(t + 1) * 128], w_out[:, ci, nh * 512:(nh + 1) * 512],
                       ci == 0, ci == 7, [("mixT", ci, c), ("w_out", ci)], [pb(pa)])
                S.add("dve", (lambda e, nh=nh, pa=pa: e.scalar_tensor_tensor(
                    out=zb3[zi][:, nh * 512:(nh + 1) * 512], in0=xres[bi_][:, nh * 512:(nh + 1) * 512],
                    scalar=float(ALPHA), in1=bank(pa), op0=ALU.mult, op1=ALU.add)),
                    [("xres", bi_), pb(pa)], [("z", zi, nh)])
            for nh in range(2):
                S.add("dve", (lambda e, nh=nh: e.bn_stats(
                    out=small[:, so + nh * 6: so + nh * 6 + 6], in_=zb3[zi][:, nh * 512:(nh + 1) * 512])),
                    [("z", zi, nh)], [("st", s3, nh)])

        def fin_A2(t):
            s3 = t % 3
            so = 8 + s3 * 12
            mo = 44 + s3 * 4
            S.add("dve", (lambda e: e.bn_aggr(out=small[:, mo: mo + 2], in_=small[:, so: so + 12])),
                  [("st", s3, 0), ("st", s3, 1)], [("mv", s3)])
            act(small[:, mo + 2: mo + 3], small[:, mo + 1: mo + 2], AF.Sqrt,
                [("mv", s3), "small"], [("sd", s3)], scale=1.0, bias=small[:, 1:2])

        def fin_A3(t):
            s3 = t % 3
            mo = 44 + s3 * 4
            S.add("dve", (lambda e: e.reciprocal(out=small[:, mo + 3: mo + 4], in_=small[:, mo + 2: mo + 3])),
                  [("sd", s3)], [("rs", s3)])
            S.add("dve", (lambda e: e.scalar_tensor_tensor(
                out=small[:, mo + 2: mo + 3], in0=small[:, mo: mo + 1], scalar=-1.0, in1=small[:, mo + 3: mo + 4],
                op0=ALU.mult, op1=ALU.mult)), [("mv", s3), ("rs", s3), ("sd", s3)], [("nb", s3)])

        def fin_B(t):
            zi = t % NZ
            s3 = t % 3
            mo = 44 + s3 * 4
            sls = [slice(nh * 512, (nh + 1) * 512) for nh in range(2)]
            for nh in range(2):
                act(zb3[zi][:, sls[nh]], zb3[zi][:, sls[nh]], AF.Identity, [("z", zi, nh), ("rs", s3), ("nb", s3)],
                    [("z", zi, nh)], scale=small[:, mo + 3: mo + 4], bias=small[:, mo + 2: mo + 3])
            for nh in range(2):
                tt("dve", zb3[zi][:, sls[nh]], zb3[zi][:, sls[nh]], lnp[:, 0, sls[nh]], ALU.mult,
                   [("z", zi, nh), "lnp"], [("z", zi, nh)])
            for nh in range(2):
                tt("dve", zb3[zi][:, sls[nh]], zb3[zi][:, sls[nh]], lnp[:, 1, sls[nh]], ALU.add,
                   [("z", zi, nh), "lnp"], [("z", zi, nh)])
            dma(out_d[t * 128:(t + 1) * 128, :], zb3[zi][:, :], [("z", zi, 0), ("z", zi, 1)], [("out", t)], f"out{zi}")

        for t in range(19):
            if 1 <= t <= 16:
                fin_A2(t - 1)
            if t < 16:
                fin_A(t)
            if 1 <= t <= 16:
                fin_A3(t - 1)
            if 2 <= t <= 17:
                fin_B(t - 2)
        S.add("sp", None, [("out", t) for t in range(16)], [])
    except _Stop:
        alltok = list(set(list(S.lw.keys()) + list(S.rd.keys())))
        S.phase_reads = []
        S.add("pool", lambda e: e.memset(small[:, 62:63], 0.0), [], alltok + ["dumpbar"])
        loc = locals()
        for di, (nm, expr) in enumerate(dumps):
            ap = expr(loc)
            shp = list(ap.shape)
            dd = nc.dram_tensor("dump_" + nm, shp, ap.dtype, kind="ExternalOutput").ap()
            S.add("sp", (lambda e, dd=dd, ap=ap: e.dma_start(out=dd, in_=ap)), ["dumpbar"], [("dumpo", di)],
                  dsem=dsem("dump"))
        S.add("sp", None, [("dumpo", di) for di in range(len(dumps))], [])

    S.finalize()
    from contextlib import ExitStack
    with ExitStack() as st:
        esems = {e: st.enter_context(nc.semaphore("e_" + e)) for e in Sched.ENGS}
        dsems = {n: st.enter_context(nc.semaphore("d_" + n)) for n in dma_sem_names}
        block = st.enter_context(nc.Block())

        @block.sync
        def _(e):
            S.emit("sp", e, esems, dsems)

        @block.tensor
        def _(e):
            S.emit("pe", e, esems, dsems)

        @block.scalar
        def _(e):
            S.emit("act", e, esems, dsems)

        @block.vector
        def _(e):
            S.emit("dve", e, esems, dsems)

        @block.gpsimd
        def _(e):
            S.emit("pool", e, esems, dsems)
    return nc


_PROG = {}


def kernel(x, w_in, q_norm_g, kv_norm_g, w_uq, w_ukv, w_out, ln_g, ln_b):
    x = np.asarray(x, dtype=np.float32)
    shared = _shared_consts(np.asarray(w_in, np.float32), np.asarray(q_norm_g, np.float32),
                            np.asarray(kv_norm_g, np.float32), np.asarray(w_uq, np.float32),
                            np.asarray(w_ukv, np.float32), np.asarray(w_out, np.float32),
                            np.asarray(ln_g, np.float32), np.asarray(ln_b, np.float32))
    cc = [_core_consts(0), _core_consts(1)]
    in_maps = []
    for core in range(8):
        b, h = divmod(core, 2)
        xl = np.zeros((NT, 1024), np.float32)
        if h == 0:
            xl[2048:] = x[b, 0:2048]
        else:
            xl[:] = x[b]
        xT = np.ascontiguousarray(xl.T.reshape(8, 128, NT).swapaxes(0, 1))
        m = {"xT": xT, "xq": np.ascontiguousarray(x[b, 2048 * h: 2048 * h + 2048])}
        m.update(shared)
        m.update(cc[h])
        in_maps.append(m)
    if "nc" not in _PROG:
        _PROG["nc"] = build_program()
    res = run_bass_kernel_spmd(_PROG["nc"], in_maps, core_ids=list(range(8)))
    out = np.zeros((BATCH, SEQ, D_MODEL), np.float32)
    for core in range(8):
        b, h = divmod(core, 2)
        out[b, 2048 * h: 2048 * h + 2048] = res.results[core]["out"]
    return out
```

```python
import numpy as np
import concourse.bass as bass
import concourse.mybir as mybir
from concourse.bass_utils import run_bass_kernel_spmd


def eval_ap(fn, loc):
    return fn(loc)

F32 = mybir.dt.float32
BF16 = mybir.dt.bfloat16
U8 = mybir.dt.uint8
ALU = mybir.AluOpType
AF = mybir.ActivationFunctionType

D_MODEL = 1024
BATCH = 4
SEQ = 4096
NT = 4096
NQ = 2048
QOFF = 2048
ROPE_THETA = 500000.0
RMS_EPS = 1e-6
LN_EPS = 1e-5
ALPHA = 2.0 ** 0.25
NEGM = -30000.0

DEBUG_DUMP = False


class Op:
    __slots__ = ("eng", "fn", "deps", "raw", "sig", "val", "dsem", "name")

    def __init__(self, eng, fn, dsem=None, name=""):
        self.eng = eng
        self.fn = fn
        self.deps = []
        self.raw = set()
        self.sig = False
        self.val = 0
        self.dsem = dsem
        self.name = name


class Sched:
    ENGS = ("pe", "act", "dve", "pool", "sp")

    def __init__(self):
        self.ops = []
        self.lw = {}
        self.rd = {}
        self.dma_count = {}
        self.phase_reads = []

    def add(self, eng, fn, reads=(), writes=(), dsem=None, name=""):
        op = Op(eng, fn, dsem, name)
        deps = {}
        reads = list(reads) + [t for t in self.phase_reads if t not in writes]
        for r in reads:
            w = self.lw.get(r)
            if w is not None:
                deps[id(w)] = w
                op.raw.add(id(w))
            if isinstance(r, tuple) and r[0] == "ps":
                for r2 in self.rd.get(r, {}).values():
                    if r2.eng != eng:
                        deps[id(r2)] = r2
        for w_ in writes:
            w = self.lw.get(w_)
            if w is not None:
                deps[id(w)] = w
            for r in self.rd.get(w_, {}).values():
                deps[id(r)] = r
        op.deps = [d for d in deps.values() if d is not op]
        for r in reads:
            self.rd.setdefault(r, {})[eng if dsem is None else ("dma", len(self.ops))] = op
        for w_ in writes:
            self.lw[w_] = op
            self.rd[w_] = {}
        if dsem is not None:
            self.dma_count[dsem] = self.dma_count.get(dsem, 0) + 16
            op.val = self.dma_count[dsem]
        self.ops.append(op)
        return op

    def finalize(self):
        for op in self.ops:
            for d in op.deps:
                if d.dsem is not None:
                    continue
                if d.eng != op.eng or op.dsem is not None:
                    d.sig = True
                elif op.eng in ("act", "dve", "pool") and id(d) in op.raw:
                    d.sig = True
        cnt = {e: 0 for e in self.ENGS}
        for op in self.ops:
            if op.dsem is None and op.sig:
                cnt[op.eng] += 1
                op.val = cnt[op.eng]
        return cnt

    def emit(self, eng_name, eng, esems, dsems):
        waited = {}
        n = 0
        for op in self.ops:
            if op.eng != eng_name:
                continue
            for d in op.deps:
                if d.dsem is not None:
                    key = ("d", d.dsem)
                    sem = dsems[d.dsem]
                else:
                    if d.eng == op.eng and op.dsem is None:
                        if not (op.eng in ("act", "dve", "pool") and id(d) in op.raw):
                            continue
                    key = ("e", d.eng)
                    sem = esems[d.eng]
                if waited.get(key, 0) >= d.val:
                    continue
                waited[key] = d.val
                eng.wait_ge(sem, d.val)
            if op.fn is None:
                continue
            ins = op.fn(eng)
            n += 1
            if op.dsem is not None:
                ins.then_inc(dsems[op.dsem], 16)
            elif op.sig:
                ins.then_inc(esems[op.eng], 1)
        return n


def _rope_tab(pos, dim):
    inv = (np.float32(ROPE_THETA) ** (-np.arange(0, dim, 2, dtype=np.float32) / np.float32(dim))).astype(np.float32)
    ang = (pos.astype(np.float32)[:, None] * inv[None, :]).astype(np.float32)
    return np.cos(ang).astype(np.float32), np.sin(ang).astype(np.float32)


def _shared_consts(w_in, q_norm_g, kv_norm_g, w_uq, w_ukv, w_out, ln_g, ln_b):
    f = np.float32
    oq, okv, okr, oga, oqb, okb, ovb, ogb = 0, 384, 640, 672, 1184, 1696, 2208, 2720
    w_dil = np.zeros((1024, 4, 4, 128), f)
    for p in range(4):
        for kind, off in enumerate((oqb, okb, ovb, ogb)):
            blk = w_in[:, off + 128 * p: off + 128 * p + 128].copy()
            if kind < 2:
                blk[:, 0:16] = 0.0
                blk[:, 64:80] = 0.0
            w_dil[:, p, kind, :] = blk
    w_dr = np.zeros((1024, 4, 128), f)
    for i, off in enumerate((oqb, okb)):
        t1 = np.concatenate([w_in[:, off + h * 64: off + h * 64 + 8] for h in range(8)], 1)
        t2 = np.concatenate([w_in[:, off + h * 64 + 8: off + h * 64 + 16] for h in range(8)], 1)
        w_dr[:, 2 * i, :] = np.concatenate([t1, t2], 1)
        w_dr[:, 2 * i + 1, :] = np.concatenate([t2, t1], 1)
    kr = w_in[:, okr:okr + 32]
    w_lat = np.concatenate([w_in[:, oq:oq + 384], w_in[:, okv:okv + 256], w_in[:, oga:oga + 512],
                            kr[:, 0:16], kr[:, 16:32], kr[:, 16:32], kr[:, 0:16]], 1).astype(f)
    w_uqh = np.zeros((384, 8, 96), f)
    w_uqr = np.zeros((384, 2, 128), f)
    for h in range(8):
        w_uqh[:, h, 0:64] = w_uq[:, h * 96: h * 96 + 64]
        w_uqr[:, 0, h * 16:(h + 1) * 16] = w_uq[:, h * 96 + 64: h * 96 + 80]
        w_uqr[:, 1, h * 16:(h + 1) * 16] = w_uq[:, h * 96 + 80: h * 96 + 96]
    w_ukn = np.zeros((256, 8, 96), f)
    w_uv = np.zeros((256, 512), f)
    for h in range(8):
        w_ukn[:, h, 0:64] = w_ukv[:, h * 128: h * 128 + 64]
        w_uv[:, h * 64:(h + 1) * 64] = w_ukv[:, h * 128 + 64: h * 128 + 128]
    sel_d = np.zeros((128, 4, 128), f)
    for p in range(4):
        for hh in range(2):
            h = 2 * p + hh
            for fq in range(8):
                sel_d[h * 8 + fq, p, hh * 64 + fq] = 1.0
                sel_d[64 + h * 8 + fq, p, hh * 64 + 8 + fq] = 1.0
    sel_q = np.zeros((128, 8, 2, 96), f)
    for h in range(8):
        for fq in range(16):
            sel_q[h * 16 + fq, h, 0, 64 + fq] = 1.0
            sel_q[h * 16 + fq, h, 1, 80 + fq] = 1.0
    sel_k = np.zeros((32, 96), f)
    for j in range(32):
        sel_k[j, 64 + j] = 1.0
    ident = np.eye(128, dtype=f)
    kk = np.arange(128)[:, None]
    qq = np.arange(128)[None, :]
    mb_cur = np.where(kk <= qq, 0.0, NEGM).astype(f)
    mb_prev = np.where(kk >= qq, 0.0, NEGM).astype(f)
    mb4 = np.concatenate([mb_prev, mb_cur, mb_prev, mb_cur], 1)
    cmat = np.concatenate([ident, mb4], 1).astype(f)
    gq = np.ascontiguousarray(q_norm_g.reshape(3, 128).T).astype(f)
    gkv = np.ascontiguousarray(kv_norm_g.reshape(2, 128).T).astype(f)
    lnp = np.stack([np.broadcast_to(ln_g[None, :], (128, 1024)),
                    np.broadcast_to(ln_b[None, :], (128, 1024))], 1).astype(f)
    part = lambda a, k: np.ascontiguousarray(
        a.reshape(k, 128, *a.shape[1:]).swapaxes(0, 1))
    return {
        "w_dil": part(w_dil, 8),
        "w_dr": part(w_dr, 8),
        "w_lat": part(w_lat, 8),
        "w_uqh": part(w_uqh, 3),
        "w_uqr": part(w_uqr, 3),
        "w_ukn": part(w_ukn, 2),
        "w_uv": part(w_uv, 2),
        "w_out": part(np.ascontiguousarray(w_out).astype(f), 8),
        "sel_d": sel_d, "sel_q": sel_q, "sel_k": sel_k, "cmat": cmat,
        "gq": gq, "gkv": gkv, "lnp": np.ascontiguousarray(lnp),
    }


def _core_consts(h):
    f = np.float32
    pos = np.arange(NT, dtype=np.int64) - 2048 + 2048 * h
    posc = np.maximum(pos, 0)
    c8, s8 = _rope_tab(posc, 16)
    c16, s16 = _rope_tab(posc, 32)
    cc = np.tile(c8.T, (16, 1))
    ss = np.concatenate([-np.tile(s8.T, (8, 1)), np.tile(s8.T, (8, 1))], 0)
    rope_d = np.stack([cc.reshape(128, 8, 512), ss.reshape(128, 8, 512)], 2)
    cq = np.tile(c16.T, (8, 1))[:, QOFF:]
    sq = np.tile(s16.T, (8, 1))[:, QOFF:]
    rope_q = np.stack([cq.reshape(128, 4, 512), sq.reshape(128, 4, 512)], 2)
    rk = np.concatenate([c16.T, c16.T, -s16.T, s16.T], 0)
    rope_k = rk.reshape(64, 8, 512)
    vflag = np.ones((128, 32), f)
    if h == 0:
        vflag[:, 0:16] = 0.0
    return {"rope_d": np.ascontiguousarray(rope_d).astype(f),
            "rope_q": np.ascontiguousarray(rope_q).astype(f),
            "rope_k": np.ascontiguousarray(rope_k).astype(f),
            "vflag": np.ascontiguousarray(vflag)}


class _Stop(Exception):
    pass


class _Step:
    __slots__ = ("qk", "ex", "pv", "post")

    def __init__(self, qk, ex, pv, post=None):
        self.qk, self.ex, self.pv, self.post = qk, ex, pv, post


def build_program(limit=None, dumps=()):
    nc = bass.Bass("TRN2", target_bir_lowering=False)
    S = Sched()

    def done(tag):
        if limit == tag:
            raise _Stop()

    def din(name, shape):
        return nc.dram_tensor(name, list(shape), F32, kind="ExternalInput").ap()

    xT_d = din("xT", [128, 8, NT])
    xq_d = din("xq", [NQ, 1024])
    w_dil_d = din("w_dil", [128, 8, 4, 4, 128])
    w_dr_d = din("w_dr", [128, 8, 4, 128])
    w_lat_d = din("w_lat", [128, 8, 1216])
    w_uqh_d = din("w_uqh", [128, 3, 8, 96])
    w_uqr_d = din("w_uqr", [128, 3, 2, 128])
    w_ukn_d = din("w_ukn", [128, 2, 8, 96])
    w_uv_d = din("w_uv", [128, 2, 512])
    w_out_d = din("w_out", [128, 8, 1024])
    sel_d_d = din("sel_d", [128, 4, 128])
    sel_q_d = din("sel_q", [128, 8, 2, 96])
    sel_k_d = din("sel_k", [32, 96])
    cmat_d = din("cmat", [128, 640])
    gq_d = din("gq", [128, 3])
    gkv_d = din("gkv", [128, 2])
    lnp_d = din("lnp", [128, 2, 1024])
    rope_d_d = din("rope_d", [128, 8, 2, 512])
    rope_q_d = din("rope_q", [128, 4, 2, 512])
    rope_k_d = din("rope_k", [64, 8, 512])
    vflag_d = din("vflag", [128, 32])
    out_d = nc.dram_tensor("out", [NQ, 1024], F32, kind="ExternalOutput").ap()

    ARENA = 207 * 1024
    arena = nc.alloc_sbuf_tensor("arena", [128, ARENA], U8)

    def carve(off, shape, dt):
        esz = 4 if dt == F32 else 2
        n = int(np.prod(shape[1:]))
        assert off % 32 == 0, off
        assert off + n * esz <= ARENA, (off, n * esz, ARENA)
        ap = arena[0:shape[0], off:off + n * esz].bitcast(dt)
        if len(shape) == 3:
            ap = ap.rearrange("p (a b) -> p a b", a=shape[1])
        elif len(shape) == 4:
            ap = ap.rearrange("p (a b c) -> p a b c", a=shape[1], b=shape[2])
        return ap

    K = 1024
    o = 0
    xT = carve(o, [128, 8, NT], BF16); XT_OFF = o; o += 64 * K
    mixT = carve(o, [128, 8, NQ], BF16); o += 32 * K
    NSTG = 2
    stg = [carve(o + i * 4 * K, [128, 1024], F32) for i in range(NSTG)]; o += NSTG * 4 * K
    PT6 = carve(o, [128, 6 * 512], BF16); o += 6 * K
    rden = [carve(o + i * 2 * K, [128, 512], F32) for i in range(2)]; o += 4 * K
    sgt = [carve(o + i * 2 * K, [128, 512], F32) for i in range(2)]; o += 4 * K
    cmat = carve(o, [128, 640], BF16); o += 1280
    ident = cmat[:, 0:128]
    mb4 = cmat[:, 128:640]
    ones_b = carve(o, [128, 128], BF16); o += 256
    m01 = carve(o, [128, 512], BF16); o += 1024
    sel_d = carve(o, [128, 4, 128], BF16); o += 1 * K
    sel_q = carve(o, [128, 8 * 2, 96], BF16); o += 3 * K
    sel_k = carve(o, [32, 96], BF16); o += 192 + 64
    gq = carve(o, [128, 4], F32); o += 32
    gkv = carve(o, [128, 4], F32); o += 32
    small = carve(o, [128, 64], F32); o += 256
    vfl = carve(o, [128, 32], F32); o += 128
    o = (o + 1023) // 1024 * 1024
    R = o
    RSZ = ARENA - R
    wdb = [carve(R, [128, 8, 4, 128], BF16)] * 2
    ropedQ = carve(R + 8 * K, [128, NQ], BF16)
    ropedK = carve(R + 12 * K, [128, NT], BF16)
    qTd = carve(R + 20 * K, [128, NQ], BF16)
    kTd = carve(R + 24 * K, [128, NT], BF16)
    vT = carve(R + 32 * K, [128, NT], BF16)
    wdr = carve(64 * K, [128, 8, 4, 128], BF16)
    Vdb = [carve(R + 40 * K + i * 12 * K, [128, 32, 192], BF16) for i in range(2)]
    accA = carve(R + 64 * K, [128, NQ], F32)
    accB = carve(R + 72 * K, [128, NQ], F32)
    tabd = [carve(R + 64 * K + i * 4 * K, [128, 2, 512], F32) for i in range(2)]
    tmpa = [carve(R + 72 * K + i * 2 * K, [128, 512], F32) for i in range(2)]
    tmpb = [carve(R + 76 * K + i * 2 * K, [128, 512], F32) for i in range(2)]
    assert 80 * K <= RSZ, RSZ
    cqn = carve(R, [128, 3, NQ], BF16)
    ckvn = carve(R + 12 * K, [128, 2, NT], BF16)
    kpe = carve(R + 28 * K, [32, NT], BF16)
    rq1 = carve(R + 36 * K, [128, NQ], BF16)
    rq2 = carve(R + 40 * K, [128, NQ], BF16)
    LT = R + 44 * K
    wlb = [carve(LT + i * 2 * K, [128, 8, 128], BF16) for i in range(6)]
    cf = carve(LT + 12 * K, [128, 3, 512], F32)
    sqb = PT6[:, 0:1536].rearrange("p (a b) -> p a b", a=3)
    rt = PT6[:, 2048:3072].bitcast(F32)
    gtmp = PT6[:, 2048:3072].bitcast(F32)
    cf2 = carve(R + 28 * K, [128, 3, 512], F32)
    sqb2 = carve(R + 34 * K, [128, 3, 512], BF16)
    rt2 = carve(R + 37 * K, [128, 512], F32)
    tabl = carve(LT + 23 * K, [128, 2, 512], F32)
    ta = carve(LT + 27 * K, [128, 512], F32)
    tb = carve(LT + 29 * K, [128, 512], F32)
    w_uqr = carve(LT + 31 * K, [128, 3, 2, 128], BF16)
    tabl2 = carve(LT + 12 * K, [128, 2, 512], F32)
    ta2 = carve(LT + 16 * K, [128, 512], F32)
    tb2 = carve(LT + 18 * K, [128, 512], F32)
    assert 44 * K + 33 * K <= RSZ, RSZ
    X = XT_OFF
    qTA = [carve(X + i * 4 * K, [96, NQ], BF16) for i in range(2)]
    kTA = [carve(X + 8 * K + i * 8 * K, [96, NT], BF16) for i in range(2)]
    VA = [carve(X + 24 * K + i * 12 * K, [128, 32, 192], BF16) for i in range(2)]
    w_uqh = carve(X + 48 * K, [128, 3, 8, 96], BF16)
    w_ukn = carve(X + 53 * K, [128, 2, 8, 96], BF16)
    w_uv = carve(X + 56 * K, [128, 2, 512], BF16)
    lnp = carve(LT + 16 * K, [128, 2, 1024], F32)
    w_out = carve(LT, [128, 8, 1024], BF16)
    xres = [carve(LT + 16 * K + i * 4 * K, [128, 1024], F32) for i in range(2)]
    zb = [carve(LT + 24 * K + i * 4 * K, [128, 1024], F32) for i in range(2)]
    assert 44 * K + 32 * K <= RSZ, RSZ

    pst = nc.alloc_psum_tensor("pst", [128, 8 * 512], F32)

    def bank(i, n=1):
        return pst[:, i * 512:(i + n) * 512]

    dma_sem_names = []

    def dsem(name):
        if name not in dma_sem_names:
            dma_sem_names.append(name)
        return name

    def dma(out_ap, in_ap, reads, writes, sem, eng="sp"):
        return S.add(eng, lambda e: e.dma_start(out=out_ap, in_=in_ap), reads, writes, dsem=dsem(sem))

    def mm(out_ap, lhsT, rhs, start, stop, reads, writes):
        return S.add("pe", lambda e: e.matmul(out_ap, lhsT, rhs, start=start, stop=stop), reads, writes)

    def act(out_ap, in_ap, func, reads, writes, scale=1.0, bias=None):
        if bias is None:
            return S.add("act", lambda e: e.activation(out=out_ap, in_=in_ap, func=func, scale=scale), reads, writes)
        return S.add("act", lambda e: e.activation(out=out_ap, in_=in_ap, func=func, scale=scale, bias=bias), reads, writes)

    def tt(eng, out_ap, in0, in1, op, reads, writes):
        return S.add(eng, lambda e: e.tensor_tensor(out=out_ap, in0=in0, in1=in1, op=op), reads, writes)

    def tsc(eng, out_ap, in0, s1, op0, reads, writes):
        return S.add(eng, lambda e: e.tensor_scalar(out=out_ap, in0=in0, scalar1=s1, scalar2=None, op0=op0), reads, writes)

    def cp(eng, out_ap, in_ap, reads, writes):
        return S.add(eng, lambda e: e.tensor_copy(out=out_ap, in_=in_ap), reads, writes)

    stg_i = [0]
    cast_rr = [0]
    xstg = [carve(R + 40 * K + i * 4 * K, [128, 1024], F32) for i in range(6)]
    ring = {"bufs": xstg, "tok": "xstg", "n": 6}

    XSTG_TOK = [("xstg", k) for k in range(6)]

    def next_stg():
        i = stg_i[0] % ring["n"]
        stg_i[0] += 1
        return ring["bufs"][i], (ring["tok"], i), "%s%d" % (ring["tok"], i)

    def load_cast(dst_ap, src_ap, nelem, writes, engs=("pool",)):
        sbuf_, stok, ssem = next_stg()
        pp = dst_ap.shape[0]
        assert nelem <= 1024
        dma(sbuf_[0:pp, 0:nelem], src_ap, [], [stok], ssem)
        eng = engs[cast_rr[0] % len(engs)]
        cast_rr[0] += 1
        if eng == "act":
            act(dst_ap, sbuf_[0:pp, 0:nelem], AF.Copy, [stok], writes)
        else:
            cp(eng, dst_ap, sbuf_[0:pp, 0:nelem], [stok], writes)

    def load_cast3(dst3, src3, a, b, writes, eng="pool"):
        sbuf_, stok, ssem = next_stg()
        assert a * b <= 1024
        sv = sbuf_[:, 0:a * b].rearrange("p (a b) -> p a b", a=a)
        dma(sv, src3, [], [stok], ssem)
        if eng == "act":
            act(dst3, sv, AF.Copy, [stok], writes)
        else:
            cp(eng, dst3, sv, [stok], writes)

    def write_ones(dst3, eng, wtoks, extra_reads=()):
        src = vfl[:, :].unsqueeze(2).broadcast_to([128, 32, 64])
        if eng == "act":
            act(dst3, src, AF.Copy, ["vfl"] + list(extra_reads), list(wtoks))
        else:
            cp(eng, dst3, src, ["vfl"] + list(extra_reads), list(wtoks))

    psr = {"s": [0, 1, 2], "o": [3, 4], "g": [5, 6, 7]}
    psi = {"s": 0, "o": 0, "g": 0}

    def psum(kind):
        lst = psr[kind]
        i = lst[psi[kind] % len(lst)]
        psi[kind] += 1
        return i

    def pb(i):
        return ("ps", i)

    pti = [0]

    def ptbuf():
        i = pti[0] % 4
        pti[0] += 1
        return i

    pt2i = [0]

    def ptbuf2():
        i = (pt2i[0] % 3) * 2
        pt2i[0] += 1
        return i

    def PT(i, n=1):
        return PT6[:, i * 512:(i + n) * 512]

    def tok(c, n=512):
        return slice(c * n, (c + 1) * n)

    def gate_evac(dst_ap, pa, wtok, tbuf, ttok):
        act(tbuf, bank(pa), AF.Tanh, [pb(pa)], [ttok], scale=0.5)
        S.add("dve", (lambda e: e.scalar_tensor_tensor(out=dst_ap, in0=tbuf, scalar=1.0, in1=bank(pa),
                                                       op0=ALU.add, op1=ALU.mult)),
              [ttok, pb(pa)], [wtok])

    def strided(ap2, start, d, n=128):
        if d == 1:
            return ap2[:, start:start + n]
        r = start % d
        return ap2[:, start - r: start - r + n * d].rearrange("p (i r) -> p r i", r=d)[:, r, :]

    def run_pipeline(steps, fill_iter, nf, LA):
        n = len(steps)
        fi = 0
        tot = n + LA
        for i in range(tot):
            if i < n:
                steps[i].qk()
                steps[i].ex()
            j = i - LA
            if j >= 0:
                steps[j].pv()
                if steps[j].post is not None:
                    steps[j].post()
            while fi < nf and fi * tot < (i + 1) * nf:
                next(fill_iter)
                fi += 1
        for _ in fill_iter:
            pass

    def gen_of(units):
        for u in units:
            u()
            yield

    def xT_tok(ci, c):
        return ("xT", ci, c)

    try:
        pre_tokens = [("tabd", i) for i in range(2)] + [("tmpa", i) for i in range(2)] + [("tmpb", i) for i in range(2)]

        DILC = ((1, 32), (4, 8), (16, 2))

        def dil_proj_units(p, parts=False):
            wd = wdb[p % 2]
            wt = lambda kind: ("wd", 0, kind)
            units = []

            def ku(c):
                pa = psum("g")
                for ci in range(8):
                    mm(bank(pa), wd[:, ci, 1, :], xT[:, ci, tok(c)], ci == 0, False, [wt(1), xT_tok(ci, c)], [pb(pa)])
                mm(bank(pa), sel_d[:, p, :], ropedK[:, tok(c)], False, True, ["sel_d", ("rK", c)], [pb(pa)])
                if p > 0:
                    act(kTd[:, tok(c)], bank(pa), AF.Copy, [pb(pa)], [("kTd", c)])
                else:
                    cp("dve", kTd[:, tok(c)], bank(pa), [pb(pa)], [("kTd", c)])

            def qu(c):
                pa = psum("g")
                for ci in range(8):
                    mm(bank(pa), wd[:, ci, 0, :], xT[:, ci, tok(4 + c)], ci == 0, False, [wt(0), xT_tok(ci, 4 + c)], [pb(pa)])
                mm(bank(pa), sel_d[:, p, :], ropedQ[:, tok(c)], False, True, ["sel_d", ("rQ", c)], [pb(pa)])
                tsc("dve", qTd[:, tok(c)], bank(pa), 0.125, ALU.mult, [pb(pa)], [("qTd", c)])

            def gu(c):
                pa = psum("g")
                for ci in range(8):
                    mm(bank(pa), wd[:, ci, 3, :], xT[:, ci, tok(4 + c)], ci == 0, ci == 7, [wt(3), xT_tok(ci, 4 + c)], [pb(pa)])
                gate_evac(mixT[:, 4 + p, tok(c)], pa, ("mixT", 4 + p, c), gtmp[:, :], "gtmp")

            def vtu(c):
                pa = psum("g")
                for ci in range(8):
                    mm(bank(pa), wd[:, ci, 2, :], xT[:, ci, tok(c)], ci == 0, ci == 7, [wt(2), xT_tok(ci, c)], [pb(pa)])
                if p > 0:
                    act(vT[:, tok(c)], bank(pa), AF.Copy, [pb(pa)], [("vT", c)], scale=0.5)
                else:
                    tsc("dve", vT[:, tok(c)], bank(pa), 0.5, ALU.mult, [pb(pa)], [("vT", c)])

            if parts:
                return ku, vtu, qu, gu
            for c in range(8):
                units.append(lambda c=c: ku(c))
            for c in range(8):
                units.append(lambda c=c: vtu(c))
            for c in range(4):
                units.append(lambda c=c: qu(c))
            for c in range(4):
                units.append(lambda c=c: gu(c))
            return units

        def load_pair_weights(p, kinds=(1, 2, 0, 3), engs=("act", "pool")):
            wd = wdb[p % 2]
            for ki, kind in enumerate(kinds):
                load_cast3(wd[:, :, kind, :], w_dil_d[:, :, p, kind, :], 8, 128, [("wd", 0, kind)],
                           eng=engs[ki % len(engs)])

        dma(vfl[:, :], vflag_d[:, :], [], ["vfl"], "c3")
        S.add("pool", lambda e: e.memset(ones_b[:, :], 1.0), [], ["ones_b"])
        S.add("pool", lambda e: e.memset(small[:, 0:1], RMS_EPS), [], ["small"])
        S.add("pool", lambda e: e.memset(small[:, 1:2], LN_EPS), [], ["small"])
        for kind in (2, 3):
            load_cast3(wdr[:, :, kind, :], w_dr_d[:, :, kind, :], 8, 128, [("wdr", kind)], eng=("act", "dve")[kind % 2])
        load_cast(sel_d.rearrange("p a b -> p (a b)"), sel_d_d.rearrange("p a b -> p (a b)"), 512, ["sel_d"], engs=("dve",))
        load_pair_weights(0, kinds=(1, 2), engs=("act", "dve"))
        xcnt = [0]

        def load_x_quarter(tq):
            for ci in range(8):
                sbuf_, stok, ssem = next_stg()
                dst = xT[:, ci, tq * 1024:(tq + 1) * 1024]
                dma(sbuf_[:, :], xT_d[:, ci, tq * 1024:(tq + 1) * 1024], [], [stok], ssem)
                wr = [("xT", ci, tq * 2), ("xT", ci, tq * 2 + 1)]
                if xcnt[0] % 2 == 1:
                    act(dst, sbuf_[:, :], AF.Copy, [stok], wr)
                else:
                    cp("dve", dst, sbuf_[:, :], [stok], wr)
                xcnt[0] += 1

        load_x_quarter(0)
        load_cast(cmat[:, :], cmat_d[:, :], 640, ["cmat"], engs=("act",))
        S.add("dve", lambda e: e.tensor_scalar(out=m01[:, :], in0=mb4, scalar1=-1.0, scalar2=None, op0=ALU.is_ge),
              ["cmat"], ["m01"])
        load_cast(sel_q[:, 0:8, :].rearrange("p a b -> p (a b)"),
                  sel_q_d.rearrange("p h t c -> p (h t c)")[:, 0:768], 768, ["sel_q"], engs=("dve",))
        load_cast(sel_q[:, 8:16, :].rearrange("p a b -> p (a b)"),
                  sel_q_d.rearrange("p h t c -> p (h t c)")[:, 768:1536], 768, ["sel_q"], engs=("act",))
        load_cast(sel_k[:, :], sel_k_d[:, :], 96, ["sel_k"], engs=("dve",))
        dma(gq[:, 0:3], gq_d[:, :], [], ["gq"], "c0")
        dma(gkv[:, 0:2], gkv_d[:, :], [], ["gkv"], "c2")

        rci = [0]

        def rc(kindA, kindB, dst, c, cc, dname):
            bi_ = rci[0] % 2
            rci[0] += 1
            dma(tabd[bi_][:, :, :], rope_d_d[:, c, :, :], [], [("tabd", bi_)], f"tabd{bi_}", eng="pool")
            pa = psum("g")
            pb_ = psum("g")
            for ci in range(8):
                mm(bank(pa), wdr[:, ci, kindA, :], xT[:, ci, tok(c)], ci == 0, ci == 7,
                   [("wdr", kindA), xT_tok(ci, c)], [pb(pa)])
            for ci in range(8):
                mm(bank(pb_), wdr[:, ci, kindB, :], xT[:, ci, tok(c)], ci == 0, ci == 7,
                   [("wdr", kindB), xT_tok(ci, c)], [pb(pb_)])
            tt("dve", tmpa[bi_][:, :], bank(pa), tabd[bi_][:, 0, :], ALU.mult, [pb(pa), ("tabd", bi_)], [("tmpa", bi_)])
            tt("dve", tmpb[bi_][:, :], bank(pb_), tabd[bi_][:, 1, :], ALU.mult, [pb(pb_), ("tabd", bi_)], [("tmpb", bi_)])
            tt("dve", dst[:, cc * 512: cc * 512 + 512], tmpa[bi_][:, :], tmpb[bi_][:, :], ALU.add,
               [("tmpa", bi_), ("tmpb", bi_)], [(dname, cc)])

        ku0, vtu0, qu0, gu0 = dil_proj_units(0, parts=True)

        def kwork(c):
            rc(2, 3, ropedK, c, c, "rK")
            ku0(c)
            vtu0(c)

        kwork(0)
        kwork(1)
        load_x_quarter(1)
        for kind in (0, 1):
            load_cast3(wdr[:, :, kind, :], w_dr_d[:, :, kind, :], 8, 128, [("wdr", kind)], eng=("act", "dve")[kind % 2])
        load_pair_weights(0, kinds=(0, 3), engs=("act", "dve"))
        kwork(2)
        kwork(3)
        load_x_quarter(2)
        kwork(4)
        kwork(5)
        load_x_quarter(3)
        ring.update(bufs=stg, tok="stg", n=NSTG)
        stg_i[0] = 0
        done("p0")
        kwork(6)
        kwork(7)
        for c in range(4):
            rc(0, 1, ropedQ, 4 + c, c, "rQ")
            qu0(c)
            gu0(c)
        done("d0")
        for vb_ in range(2):
            write_ones(Vdb[vb_][:, :, 64:128], "act", [("Vd_ones", vb_, g4) for g4 in range(4)] + XSTG_TOK)

        def dil_v_units(p, bi):
            wd = wdb[p % 2]
            d, nbl = DILC[bi]
            vb_ = (3 * p + bi) % 2
            Vd = Vdb[vb_]
            jlist = [n_ * d + r for n_ in range(nbl // 2 - 1, nbl) for r in range(d)]
            units = []

            def vu(grp):
                pa = psum("g")
                pbf = bank(pa).bitcast(BF16)
                for gi, j in enumerate(grp):
                    n_, r = divmod(j, d)
                    t0 = d * 128 * n_ + r
                    c_lo = t0 // 512
                    c_hi = (t0 + d * 127) // 512
                    rd = ["cmat"] + [("vT", cc) for cc in range(c_lo, c_hi + 1)]
                    S.add("pe", (lambda e, gi=gi, t0=t0: e.transpose(pbf[:, gi * 128:(gi + 1) * 128],
                                                                      strided(vT[:, :], t0, d), ident)),
                          rd, [pb(pa)])
                ng = len(grp)
                j0 = grp[0]
                assert grp == list(range(j0, j0 + ng))
                src = pbf[:, 0:ng * 128].rearrange("p (a t b) -> p a t b", a=ng, t=2)
                cp("dve", Vd[:, j0:j0 + ng, 0:64], src[:, :, 0, :], [pb(pa)], [("Vd", vb_, j) for j in grp] + XSTG_TOK)
                cp("dve", Vd[:, j0:j0 + ng, 128:192], src[:, :, 1, :], [pb(pa)], [("VdB", vb_, j) for j in grp] + XSTG_TOK)

            for g0 in range(0, len(jlist), 4):
                units.append(lambda grp=jlist[g0:g0 + 4]: vu(grp))
            return units

        def dil_attn_steps(p, bi):
            d, nbl = DILC[bi]
            vb_ = (3 * p + bi) % 2
            Vd = Vdb[vb_]
            steps = []
            for hh in range(2):
                rows = slice(hh * 64, hh * 64 + 64)
                vcols = slice(0, 128) if hh == 0 else slice(64, 192)
                acc = accA if hh == 0 else accB
                qblocks = [(n_, r) for n_ in range(nbl // 2, nbl) for r in range(d)]
                for s0 in range(0, 16, 2):
                    st = {}

                    def qk(s0=s0, st=st, rows=rows, qblocks=qblocks):
                        sb = psum("s")
                        st["sb"] = sb
                        for qi in range(2):
                            n_, r = qblocks[s0 + qi]
                            qc0 = d * 128 * (n_ - nbl // 2) + r
                            qrd = [("qTd", cc) for cc in range(qc0 // 512, (qc0 + d * 127) // 512 + 1)]
                            for pc in range(2):
                                k0 = d * 128 * (n_ - 1 + pc) + r
                                krd = [("kTd", cc) for cc in range(k0 // 512, (k0 + d * 127) // 512 + 1)]
                                mm(bank(sb)[:, (qi * 2 + pc) * 128:(qi * 2 + pc + 1) * 128],
                                   strided(kTd[rows, :], k0, d), strided(qTd[rows, :], qc0, d), True, True,
                                   krd + qrd, [pb(sb)])

                    def ex(st=st):
                        pt = ptbuf()
                        st["pt"] = pt
                        act(PT(pt), bank(st["sb"]), AF.Exp, [pb(st["sb"])], [("PT", pt)])
                        tt("dve", PT(pt), PT(pt), m01[:, :], ALU.mult, [("PT", pt), "m01"], [("PT", pt)])

                    def pv(s0=s0, st=st, vcols=vcols, qblocks=qblocks, hh=hh):
                        key = (p, bi, hh, s0 // 4)
                        if s0 % 4 == 0:
                            otile[key] = psum("o")
                        po = otile[key]
                        pt = st["pt"]
                        for qi in range(2):
                            n_, r = qblocks[s0 + qi]
                            oi = (s0 % 4) + qi
                            for pc in range(2):
                                j = (n_ - 1 + pc) * d + r
                                mm(bank(po)[:, oi * 128:(oi + 1) * 128], Vd[:, j, vcols],
                                   PT(pt)[:, (qi * 2 + pc) * 128:(qi * 2 + pc + 1) * 128],
                                   pc == 0, pc == 1,
                                   [("Vd", vb_, j), ("VdB", vb_, j), ("Vd_ones", vb_, j // 8), ("PT", pt)], [pb(po)])

                    def post(s0=s0, acc=acc, qblocks=qblocks, hh=hh):
                        if s0 % 4 != 2:
                            return
                        o0 = s0 - 2
                        po = otile[(p, bi, hh, s0 // 4)]
                        if d == 1:
                            n_, r = qblocks[o0]
                            qc0 = 128 * (n_ - nbl // 2)
                            dsta = acc[:, qc0:qc0 + 512]
                            srca = bank(po)
                            atoks = [("acc", hh, qc0 // 512)]
                        elif d == 4:
                            n_ = qblocks[o0][0]
                            base = 512 * (n_ - nbl // 2)
                            dsta = acc[:, base:base + 512].rearrange("p (i r) -> p r i", r=4)
                            srca = bank(po).rearrange("p (r i) -> p r i", r=4)
                            atoks = [("acc", hh, base // 512)]
                        else:
                            r0 = qblocks[o0][1]
                            dsta = acc[:, :].rearrange("p (i r) -> p r i", r=16)[:, r0:r0 + 4, :]
                            srca = bank(po).rearrange("p (r i) -> p r i", r=4)
                            atoks = [("acc", hh, cc) for cc in range(4)]
                        if bi == 0:
                            cp("dve", dsta, srca, [pb(po)], atoks + (pre_tokens if p == 0 else []))
                        else:
                            tt("dve", dsta, dsta, srca, ALU.add, [pb(po)] + atoks, atoks)

                    steps.append(_Step(qk, ex, pv, post))
            return steps

        def dil_norm_ops(p):
            p1s, p2s = [], []
            for hh in range(2):
                acc = accA if hh == 0 else accB
                orow = slice(hh * 64, hh * 64 + 64)
                drow = slice(64 - hh * 64, 128 - hh * 64)
                for c in range(4):
                    i = (hh * 4 + c) % 2

                    def p1(hh=hh, acc=acc, orow=orow, drow=drow, c=c, i=i):
                        act(rden[i][orow, :], acc[drow, tok(c)], AF.Ln, [("acc", hh, c)], [("rden", i)])
                        act(rden[i][orow, :], rden[i][orow, :], AF.Exp, [("rden", i)], [("rden", i)], scale=-1.0)

                    def p2(hh=hh, acc=acc, orow=orow, c=c, i=i):
                        tt("dve", sgt[i][orow, :], mixT[orow, 4 + p, tok(c)], rden[i][orow, :], ALU.mult,
                           [("mixT", 4 + p, c), ("rden", i)], [("sgt", i)])
                        tt("dve", mixT[orow, 4 + p, tok(c)], acc[orow, tok(c)], sgt[i][orow, :], ALU.mult,
                           [("acc", hh, c), ("sgt", i)], [("mixT", 4 + p, c)])
                    p1s.append(p1)
                    p2s.append(p2)
            ops = []
            for k in range(len(p1s) + 1):
                def slot(k=k):
                    if k < len(p1s):
                        p1s[k]()
                    if k >= 1:
                        p2s[k - 1]()
                ops.append(slot)
            return ops

        otile = {}
        pending_norm = []
        next_units = None
        carried = 0
        for p in range(4):
            units = next_units if p > 0 else []
            gate_units = units[20:24] if units else []
            v0 = dil_v_units(p, 0)
            v0i = 0
            for ui in range(carried, 20 if units else 0):
                units[ui]()
                if ui % 2 == 1 and ui < 18 and pending_norm:
                    pending_norm.pop(0)()
                if ui >= 15 and v0i < len(v0):
                    v0[v0i]()
                    v0i += 1
            while pending_norm:
                pending_norm.pop(0)()
            while v0i < len(v0):
                v0[v0i]()
                v0i += 1
            done("dproj%d" % p)
            for bi in range(3):
                if bi == 1 and p + 1 < 4:
                    load_pair_weights(p + 1)
                    next_units = dil_proj_units(p + 1)
                steps = dil_attn_steps(p, bi)
                n = len(steps)
                if bi == 0:
                    early = gate_units + dil_v_units(p, 1)
                elif bi == 1:
                    early = dil_v_units(p, 2)
                else:
                    early = []
                late = next_units[0:3] if (bi == 2 and p + 1 < 4) else []

                def sched(early=early, late=late, n=n):
                    ne = len(early)
                    ei = 0
                    for it in range(n + 3):
                        while ei < ne and ei * max(1, n - 4) < (it + 1) * ne:
                            early[ei]()
                            ei += 1
                        if it >= n and (it - n) < len(late):
                            late[it - n]()
                        yield

                run_pipeline(steps, sched(), n + 3, 3)
            carried = 3 if p + 1 < 4 else 0
            done("dattn%d" % p)
            pending_norm = dil_norm_ops(p)
            if limit in ("dpair%d" % p, "d"):
                while pending_norm:
                    pending_norm.pop(0)()
            done("dpair%d" % p)
        done("d")

        dil_tokens = ([("wd", 0, k) for k in range(4)] + [("wdr", k) for k in range(4)] + [("vT", c) for c in range(8)]
                      + [("rQ", c) for c in range(4)]
                      + [("rK", c) for c in range(8)] + [("qTd", c) for c in range(4)] + [("kTd", c) for c in range(8)]
                      + [("Vd", b_, j) for b_ in range(2) for j in range(32)]
                      + [("VdB", b_, j) for b_ in range(2) for j in range(32)]
                      + [("Vd_ones", b_, g) for b_ in range(2) for g in range(4)]
                      + [("PT", i) for i in range(6)] + ["gtmp"])
        S.add("pool", lambda e: e.memset(small[:, 60:61], 0.0), [], dil_tokens + ["Rreg1"])
        S.phase_reads = ["Rreg1"]

        def load_wl(buf, col0, ncol):
            load_cast3(wlb[buf][:, :, 0:ncol], w_lat_d[:, :, col0:col0 + ncol], 8, ncol, [("wl", buf)], eng="act")

        for ft in range(3):
            load_wl(ft, ft * 128, 128)
        load_wl(3, 384, 128)
        load_wl(4, 512, 128)
        load_wl(5, 640, 128)

        CF = [cf, cf2]
        SQB = [sqb, sqb2]
        RT = [rt, rt2]
        rmsi = [0]

        def rms_latent(bufs, c_lo, n_chunks, gains, dst, nfeat, name):
            nft = len(bufs)
            for cc in range(n_chunks):
                c = c_lo + cc
                k2 = rmsi[0] % 2
                rmsi[0] += 1
                cf_, sqb_, rt_ = CF[k2], SQB[k2], RT[k2]
                for ft in range(nft):
                    pa = psum("g")
                    for ci in range(8):
                        mm(bank(pa), wlb[bufs[ft]][:, ci, :], xT[:, ci, tok(c)], ci == 0, ci == 7,
                           [("wl", bufs[ft]), xT_tok(ci, c)], [pb(pa)])
                    cp("dve", cf_[:, ft, :], bank(pa), [pb(pa)], [("cf", k2, ft)])
                    act(sqb_[:, ft, :], bank(pa), AF.Square, [pb(pa)], [("sqb", k2, ft)])
                    if pending_norm:
                        pending_norm.pop(0)()
                pq = psum("g")
                for ft in range(nft):
                    mm(bank(pq), ones_b[:, :], sqb_[:, ft, :], ft == 0, ft == nft - 1, ["ones_b", ("sqb", k2, ft)], [pb(pq)])
                act(rt_[:, :], bank(pq), AF.Ln, [pb(pq), "small"], [("rt", k2)], scale=1.0 / nfeat, bias=small[:, 0:1])
                act(rt_[:, :], rt_[:, :], AF.Exp, [("rt", k2)], [("rt", k2)], scale=-0.5)
                for ft in range(nft):
                    S.add("dve", (lambda e, ft=ft, cc=cc, cf_=cf_, rt_=rt_: e.scalar_tensor_tensor(
                        out=dst[:, ft, tok(cc)], in0=cf_[:, ft, :], scalar=gains[:, ft:ft + 1], in1=rt_[:, :],
                        op0=ALU.mult, op1=ALU.mult)),
                        [("cf", k2, ft), ("rt", k2), "gq", "gkv"], [(name, ft, cc)])

        rms_latent([0, 1, 2], 4, 4, gq, cqn, 384.0, "cqn")
        done("l1")
        load_wl(0, 768, 128)
        load_wl(1, 896, 128)
        load_wl(2, 1024, 128)
        rms_latent([3, 4], 0, 8, gkv, ckvn, 256.0, "ckvn")
        done("l2")
        load_wl(3, 1152, 64)
        for ft, buf in enumerate((5, 0, 1, 2)):
            for c in range(4):
                pa = psum("g")
                for ci in range(8):
                    mm(bank(pa), wlb[buf][:, ci, :], xT[:, ci, tok(4 + c)], ci == 0, ci == 7,
                       [("wl", buf), xT_tok(ci, 4 + c)], [pb(pa)])
                gk = (ft * 4 + c) % 2
                gbuf, gtk = ((sgt[0][:, :], ("sgt", 0)), (sgt[1][:, :], ("sgt", 1)))[gk]
                gate_evac(mixT[:, ft, tok(c)], pa, ("mixT", ft, c), gbuf, gtk)
        done("l3")
        while pending_norm:
            pending_norm.pop(0)()
        S.add("pool", lambda e: e.memset(small[:, 58:59], 0.0), [],
              [("acc", hh, c) for hh in range(2) for c in range(4)] + pre_tokens
              + [("cf", 1, i) for i in range(3)] + [("sqb", 1, i) for i in range(3)] + [("rt", 1), "Rreg2"])
        S.phase_reads = ["Rreg1", "Rreg2"]
        load_cast3(w_uqr.rearrange("p a t c -> p a (t c)"), w_uqr_d.rearrange("p a t c -> p a (t c)"), 3, 256, ["w_uqr"], eng="act")
        TAB = [tabl, tabl2]
        TA = [ta, ta2]
        TB = [tb, tb2]
        al2 = [("cf", 0, i) for i in range(3)]
        for c in range(8):
            k2 = c % 2
            ex2 = al2 if (k2 == 1 and c == 1) else []
            dma(TAB[k2][0:64, 0, :], rope_k_d[:, c, :], [], [("tabl", k2)] + ex2, f"tabl{k2}", eng="pool")
            pa = psum("g")
            for ci in range(8):
                mm(bank(pa)[0:64, :], wlb[3][:, ci, 0:64], xT[:, ci, tok(c)], ci == 0, ci == 7,
                   [("wl", 3), xT_tok(ci, c)], [pb(pa)])
            tt("dve", TA[k2][0:32, :], bank(pa)[0:32, :], TAB[k2][0:32, 0, :], ALU.mult, [pb(pa), ("tabl", k2)],
               [("ta", k2)] + ex2)
            tt("dve", TB[k2][0:32, :], bank(pa)[32:64, :], TAB[k2][32:64, 0, :], ALU.mult, [pb(pa), ("tabl", k2)],
               [("tb", k2)] + ex2)
            tt("dve", kpe[0:32, tok(c)], TA[k2][0:32, :], TB[k2][0:32, :], ALU.add, [("ta", k2), ("tb", k2)], [("kpe", c)])
        done("l4")
        for c in range(4):
            k2 = c % 2
            dma(TAB[k2][:, :, :], rope_q_d[:, c, :, :], [], [("tabl", k2)], f"tabl{k2}", eng="pool")
            p1 = psum("g")
            p2 = psum("g")
            for c3 in range(3):
                mm(bank(p1), w_uqr[:, c3, 0, :], cqn[:, c3, tok(c)], c3 == 0, c3 == 2, ["w_uqr", ("cqn", c3, c)], [pb(p1)])
            for c3 in range(3):
                mm(bank(p2), w_uqr[:, c3, 1, :], cqn[:, c3, tok(c)], c3 == 0, c3 == 2, ["w_uqr", ("cqn", c3, c)], [pb(p2)])
            tt("dve", TA[0][:, :], bank(p1), TAB[k2][:, 0, :], ALU.mult, [pb(p1), ("tabl", k2)], [("ta", 0)])
            tt("dve", TB[0][:, :], bank(p2), TAB[k2][:, 1, :], ALU.mult, [pb(p2), ("tabl", k2)], [("tb", 0)])
            tt("dve", rq1[:, tok(c)], TA[0][:, :], TB[0][:, :], ALU.subtract, [("ta", 0), ("tb", 0)], [("rq1", c)])
            tt("dve", TA[1][:, :], bank(p1), TAB[k2][:, 1, :], ALU.mult, [pb(p1), ("tabl", k2)], [("ta", 1)])
            tt("dve", TB[1][:, :], bank(p2), TAB[k2][:, 0, :], ALU.mult, [pb(p2), ("tabl", k2)], [("tb", 1)])
            tt("dve", rq2[:, tok(c)], TA[1][:, :], TB[1][:, :], ALU.add, [("ta", 1), ("tb", 1)], [("rq2", c)])
        done("l")

        all_xT = [("xT", ci, c) for ci in range(8) for c in range(8)]
        lat_tmp = ([("cf", k_, i) for k_ in range(2) for i in range(3)] + [("sqb", k_, i) for k_ in range(2) for i in range(3)]
                   + [("rt", 0), ("rt", 1), "w_uqr"] + [("tabl", i) for i in range(2)] + [("ta", i) for i in range(2)]
                   + [("tb", i) for i in range(2)] + [("wl", i) for i in range(6)])
        S.add("pool", lambda e: e.memset(small[:, 61:62], 0.0), [], all_xT + lat_tmp + ["xTreg"])
        S.phase_reads = ["xTreg"]
        psr["s"] = [0, 2]
        psr["o"] = [4, 5]
        psr["g"] = [6, 7]

        write_ones(VA[0][:, :, 64:128], "dve", [("VA1s", 0, g4) for g4 in range(4)])
        write_ones(VA[1][:, :, 64:128], "dve", [("VA1s", 1, g4) for g4 in range(4)])
        for c2 in range(2):
            load_cast(w_ukn[:, c2, :, :].rearrange("p a b -> p (a b)"),
                      w_ukn_d[:, c2, :, :].rearrange("p a b -> p (a b)"), 768, ["w_ukn"], engs=("act", "dve"))
        for c2 in range(2):
            load_cast(w_uv[:, c2, :], w_uv_d[:, c2, :], 512, ["w_uv"], engs=("act", "dve"))
        for c3 in range(3):
            load_cast(w_uqh[:, c3, :, :].rearrange("p a b -> p (a b)"),
                      w_uqh_d[:, c3, :, :].rearrange("p a b -> p (a b)"), 768, ["w_uqh"], engs=("act", "dve"))

        def mla_late_loads():
            pass
            dma(lnp[:, :, :], lnp_d[:, :, :], [], ["lnp"], "c1")
            for ci in range(8):
                load_cast(w_out[:, ci, :], w_out_d[:, ci, :], 1024, [("w_out", ci)], engs=("pool",))

        SC_A = 96.0 ** -0.5

        def mla_units(h):
            b = h % 2
            vb2 = (h // 2) % 2
            for c in range(8):
                pa = psum("g")
                for c2 in range(2):
                    mm(bank(pa)[0:96, :], w_ukn[:, c2, h, :], ckvn[:, c2, tok(c)], c2 == 0, False,
                       ["w_ukn", ("ckvn", c2, c)], [pb(pa)])
                    yield
                mm(bank(pa)[0:96, :], sel_k[:, :], kpe[0:32, tok(c)], False, True, ["sel_k", ("kpe", c)], [pb(pa)])
                cp("dve", kTA[b][:, tok(c)], bank(pa)[0:96, :], [pb(pa)], [("kTA", b, c)])
                yield
                if b == 0:
                    pa = psum("g")
                    for j in range(4):
                        tb_ = c * 4 + j
                        for c2 in range(2):
                            mm(bank(pa)[:, j * 128:(j + 1) * 128], ckvn[:, c2, tb_ * 128:(tb_ + 1) * 128],
                               w_uv[:, c2, h * 64:(h + 2) * 64], c2 == 0, c2 == 1,
                               ["w_uv", ("ckvn", c2, c)], [pb(pa)])
                        if j == 1:
                            yield
                    srcp = bank(pa).rearrange("p (a t b) -> p a t b", a=4, t=2)
                    tsc("dve", VA[vb2][:, c * 4:(c + 1) * 4, 0:64], srcp[:, :, 0, :], 0.5, ALU.mult,
                        [pb(pa)], [("VA", vb2, c)])
                    tsc("dve", VA[vb2][:, c * 4:(c + 1) * 4, 128:192], srcp[:, :, 1, :], 0.5, ALU.mult,
                        [pb(pa)], [("VAB", vb2, c)])
                    yield
            for c in range(4):
                pa = psum("g")
                for c3 in range(3):
                    mm(bank(pa)[0:96, :], w_uqh[:, c3, h, :], cqn[:, c3, tok(c)], c3 == 0, False,
                       ["w_uqh", ("cqn", c3, c)], [pb(pa)])
                    if c3 == 1:
                        yield
                mm(bank(pa)[0:96, :], sel_q[:, 2 * h, :], rq1[:, tok(c)], False, False, ["sel_q", ("rq1", c)], [pb(pa)])
                yield
                mm(bank(pa)[0:96, :], sel_q[:, 2 * h + 1, :], rq2[:, tok(c)], False, True, ["sel_q", ("rq2", c)], [pb(pa)])
                tsc("dve", qTA[b][:, tok(c)], bank(pa)[0:96, :], SC_A, ALU.mult, [pb(pa)], [("qTA", b, c)])
                yield

        def mla_nf(h):
            return 8 * 3 + 4 * 3 + (8 * 2 if h % 2 == 0 else 0)

        def mla_steps(h):
            b = h % 2
            vb2 = (h // 2) % 2
            vsl = slice(0, 128) if b == 0 else slice(64, 192)
            orow = slice(b * 64, b * 64 + 64)
            drow = slice(64 - b * 64, 128 - b * 64)
            steps = []
            for c in range(4):
                nkb = 16 + 4 * c + 4
                ost = {}
                for j in range(nkb // 2):
                    st = {}

                    def qk(c=c, j=j, st=st):
                        sb = psum("s")
                        st["sb"] = sb
                        st["w"] = []
                        for u in range(2):
                            kb = 2 * j + u
                            i = kb - (16 + 4 * c)
                            kc = kb // 4
                            bk = sb + u
                            ksl = slice(kb * 128, (kb + 1) * 128)
                            if i < 0:
                                mm(bank(bk), kTA[b][:, ksl], qTA[b][:, tok(c)], True, True,
                                   [("kTA", b, kc), ("qTA", b, c)], [pb(bk)])
                                st["w"].append(512)
                            else:
                                ncol = 512 - 128 * i
                                q0 = c * 512 + 128 * i
                                mm(bank(bk)[:, 0:128], ident, mb4[:, 128:256], True, False, ["cmat"], [pb(bk)])
                                mm(bank(bk)[:, 0:128], kTA[b][:, ksl], qTA[b][:, q0:q0 + 128], False, True,
                                   [("kTA", b, kc), ("qTA", b, c)], [pb(bk)])
                                if ncol > 128:
                                    mm(bank(bk)[:, 128:ncol], kTA[b][:, ksl], qTA[b][:, q0 + 128:(c + 1) * 512],
                                       True, True, [("kTA", b, kc), ("qTA", b, c)], [pb(bk)])
                                st["w"].append(ncol)

                    def ex(st=st):
                        pt = ptbuf2()
                        st["pt"] = pt
                        sb = st["sb"]
                        if st["w"] == [512, 512]:
                            act(PT(pt, 2), bank(sb, 2), AF.Exp, [pb(sb), pb(sb + 1)], [("PT", pt), ("PT", pt + 1)])
                        else:
                            for u in range(2):
                                w = st["w"][u]
                                act(PT(pt + u)[:, 0:w], bank(sb + u)[:, 0:w], AF.Exp, [pb(sb + u)], [("PT", pt + u)])

                    def pv(c=c, j=j, st=st, ost=ost, nkb=nkb):
                        if j == 0:
                            ost["po"] = psum("o")
                        po = ost["po"]
                        pt = st["pt"]
                        for u in range(2):
                            kb = 2 * j + u
                            w = st["w"][u]
                            mm(bank(po)[:, 512 - w:512], VA[vb2][:, kb, vsl], PT(pt + u)[:, 0:w], kb == 0, kb == nkb - 1,
                               [("VA", vb2, kb // 4), ("VAB", vb2, kb // 4), ("VA1s", vb2, kb // 8), ("PT", pt + u)],
                               [pb(po)])

                    def post(c=c, j=j, ost=ost, nkb=nkb):
                        if j != nkb // 2 - 1:
                            return
                        po = ost["po"]
                        ri = (h * 4 + c) % 2
                        S.add("dve", (lambda e: e.reciprocal(out=rden[ri][orow, :], in_=bank(po)[drow, :])),
                              [pb(po)], [("rden", ri)])
                        tt("pool", sgt[ri][orow, :], mixT[orow, h // 2, tok(c)], rden[ri][orow, :], ALU.mult,
                           [("mixT", h // 2, c), ("rden", ri)], [("sgt", ri)])
                        tt("dve", mixT[orow, h // 2, tok(c)], bank(po)[orow, :], sgt[ri][orow, :], ALU.mult,
                           [pb(po), ("sgt", ri)], [("mixT", h // 2, c)])

                    steps.append(_Step(qk, ex, pv, post))
            return steps

        for _ in mla_units(0):
            pass
        mla_late_loads()
        done("mproj0")
        for h in range(8):
            if h < 7:
                run_pipeline(mla_steps(h), mla_units(h + 1), mla_nf(h + 1), 2)
            else:
                run_pipeline(mla_steps(h), iter(()), 0, 2)
            done("mhead%d" % h)
        done("m")

        psr["g"] = [0, 1, 2, 3, 6, 7]
        NZ = 4
        NXR = 3
        zb3 = [carve(X + i * 4 * K, [128, 1024], F32) for i in range(NZ)]
        xres = [carve(X + 16 * K + i * 4 * K, [128, 1024], F32) for i in range(NXR)]
        mla_tok = ([("qTA", b_, c) for b_ in range(2) for c in range(4)] + [("kTA", b_, c) for b_ in range(2) for c in range(8)]
                   + [("VA", b_, g) for b_ in range(2) for g in range(8)] + [("VAB", b_, g) for b_ in range(2) for g in range(8)]
                   + [("VA1s", b_, g) for b_ in range(2) for g in range(4)])
        S.add("pool", lambda e: e.memset(small[:, 57:58], 0.0), [], mla_tok + ["finreg"])
        S.phase_reads = ["xTreg", "finreg"]

        def fin_A(t):
            bi_ = t % NXR
            zi = t % NZ
            s3 = t % 3
            so = 8 + s3 * 12
            dma(xres[bi_][:, :], xq_d[t * 128:(t + 1) * 128, :], [], [("xres", bi_)], f"xres{bi_}", eng="pool")
            c = t // 4
            for nh in range(2):
                pa = psum("g")
                for ci in range(8):
                    mm(bank(pa), mixT[:, ci, t * 128:(t + 1) * 128], w_out[:, ci, nh * 512:(nh + 1) * 512],
                       ci == 0, ci == 7, [("mixT", ci, c), ("w_out", ci)], [pb(pa)])
                S.add("dve", (lambda e, nh=nh, pa=pa: e.scalar_tensor_tensor(
                    out=zb3[zi][:, nh * 512:(nh + 1) * 512], in0=xres[bi_][:, nh * 512:(nh + 1) * 512],
                    scalar=float(ALPHA), in1=bank(pa), op0=ALU.mult, op1=ALU.add)),
                    [("xres", bi_), pb(pa)], [("z", zi, nh)])
            for nh in range(2):
                S.add("dve", (lambda e, nh=nh: e.bn_stats(
                    out=small[:, so + nh * 6: so + nh * 6 + 6], in_=zb3[zi][:, nh * 512:(nh + 1) * 512])),
                    [("z", zi, nh)], [("st", s3, nh)])

        def fin_A2(t):
            s3 = t % 3
            so = 8 + s3 * 12
            mo = 44 + s3 * 4
            S.add("dve", (lambda e: e.bn_aggr(out=small[:, mo: mo + 2], in_=small[:, so: so + 12])),
                  [("st", s3, 0), ("st", s3, 1)], [("mv", s3)])
            act(small[:, mo + 2: mo + 3], small[:, mo + 1: mo + 2], AF.Sqrt,
                [("mv", s3), "small"], [("sd", s3)], scale=1.0, bias=small[:, 1:2])

        def fin_A3(t):
            s3 = t % 3
            mo = 44 + s3 * 4
            S.add("dve", (lambda e: e.reciprocal(out=small[:, mo + 3: mo + 4], in_=small[:, mo + 2: mo + 3])),
                  [("sd", s3)], [("rs", s3)])
            S.add("dve", (lambda e: e.scalar_tensor_tensor(
                out=small[:, mo + 2: mo + 3], in0=small[:, mo: mo + 1], scalar=-1.0, in1=small[:, mo + 3: mo + 4],
                op0=ALU.mult, op1=ALU.mult)), [("mv", s3), ("rs", s3), ("sd", s3)], [("nb", s3)])

        def fin_B(t):
            zi = t % NZ
            s3 = t % 3
            mo = 44 + s3 * 4
            zt = [("z", zi, 0), ("z", zi, 1)]
            act(zb3[zi][:, :], zb3[zi][:, :], AF.Identity, zt + [("rs", s3), ("nb", s3)], zt,
                scale=small[:, mo + 3: mo + 4], bias=small[:, mo + 2: mo + 3])
            tt("dve", zb3[zi][:, :], zb3[zi][:, :], lnp[:, 0, :], ALU.mult, zt + ["lnp"], zt)
            tt("dve", zb3[zi][:, :], zb3[zi][:, :], lnp[:, 1, :], ALU.add, zt + ["lnp"], zt)
            dma(out_d[t * 128:(t + 1) * 128, :], zb3[zi][:, :], zt, [("out", t)], f"out{zi}")

        for t in range(19):
            if 1 <= t <= 16:
                fin_A2(t - 1)
            if t < 16:
                fin_A(t)
            if 1 <= t <= 16:
                fin_A3(t - 1)
            if 2 <= t <= 17:
                fin_B(t - 2)
        S.add("sp", None, [("out", t) for t in range(16)], [])
    except _Stop:
        alltok = list(set(list(S.lw.keys()) + list(S.rd.keys())))
        S.phase_reads = []
        S.add("pool", lambda e: e.memset(small[:, 62:63], 0.0), [], alltok + ["dumpbar"])
        loc = locals()
        for di, (nm, expr) in enumerate(dumps):
            ap = expr(loc)
            shp = list(ap.shape)
            dd = nc.dram_tensor("dump_" + nm, shp, ap.dtype, kind="ExternalOutput").ap()
            S.add("sp", (lambda e, dd=dd, ap=ap: e.dma_start(out=dd, in_=ap)), ["dumpbar"], [("dumpo", di)],
                  dsem=dsem("dump"))
        S.add("sp", None, [("dumpo", di) for di in range(len(dumps))], [])

    S.finalize()
    from contextlib import ExitStack
    with ExitStack() as st:
        esems = {e: st.enter_context(nc.semaphore("e_" + e)) for e in Sched.ENGS}
        dsems = {n: st.enter_context(nc.semaphore("d_" + n)) for n in dma_sem_names}
        block = st.enter_context(nc.Block())

        @block.sync
        def _(e):
            S.emit("sp", e, esems, dsems)

        @block.tensor
        def _(e):
            S.emit("pe", e, esems, dsems)

        @block.scalar
        def _(e):
            S.emit("act", e, esems, dsems)

        @block.vector
        def _(e):
            S.emit("dve", e, esems, dsems)

        @block.gpsimd
        def _(e):
            S.emit("pool", e, esems, dsems)
    return nc


_PROG = {}


def kernel(x, w_in, q_norm_g, kv_norm_g, w_uq, w_ukv, w_out, ln_g, ln_b):
    x = np.asarray(x, dtype=np.float32)
    shared = _shared_consts(np.asarray(w_in, np.float32), np.asarray(q_norm_g, np.float32),
                            np.asarray(kv_norm_g, np.float32), np.asarray(w_uq, np.float32),
                            np.asarray(w_ukv, np.float32), np.asarray(w_out, np.float32),
                            np.asarray(ln_g, np.float32), np.asarray(ln_b, np.float32))
    cc = [_core_consts(0), _core_consts(1)]
    in_maps = []
    for core in range(8):
        b, h = divmod(core, 2)
        xl = np.zeros((NT, 1024), np.float32)
        if h == 0:
            xl[2048:] = x[b, 0:2048]
        else:
            xl[:] = x[b]
        xT = np.ascontiguousarray(xl.T.reshape(8, 128, NT).swapaxes(0, 1))
        m = {"xT": xT, "xq": np.ascontiguousarray(x[b, 2048 * h: 2048 * h + 2048])}
        m.update(shared)
        m.update(cc[h])
        in_maps.append(m)
    if "nc" not in _PROG:
        _PROG["nc"] = build_program()
    res = run_bass_kernel_spmd(_PROG["nc"], in_maps, core_ids=list(range(8)))
    out = np.zeros((BATCH, SEQ, D_MODEL), np.float32)
    for core in range(8):
        b, h = divmod(core, 2)
        out[b, 2048 * h: 2048 * h + 2048] = res.results[core]["out"]
    return out
```

```python
import numpy as np
import concourse.bass as bass
import concourse.mybir as mybir
from concourse.bass_utils import run_bass_kernel_spmd


def eval_ap(fn, loc):
    return fn(loc)

F32 = mybir.dt.float32
BF16 = mybir.dt.bfloat16
U8 = mybir.dt.uint8
ALU = mybir.AluOpType
AF = mybir.ActivationFunctionType

D_MODEL = 1024
BATCH = 4
SEQ = 4096
NT = 4096
NQ = 2048
QOFF = 2048
ROPE_THETA = 500000.0
RMS_EPS = 1e-6
LN_EPS = 1e-5
ALPHA = 2.0 ** 0.25
NEGM = -30000.0

DEBUG_DUMP = False


class Op:
    __slots__ = ("eng", "fn", "deps", "raw", "sig", "val", "dsem", "name")

    def __init__(self, eng, fn, dsem=None, name=""):
        self.eng = eng
        self.fn = fn
        self.deps = []
        self.raw = set()
        self.sig = False
        self.val = 0
        self.dsem = dsem
        self.name = name


class Sched:
    ENGS = ("pe", "act", "dve", "pool", "sp")

    def __init__(self):
        self.ops = []
        self.lw = {}
        self.rd = {}
        self.dma_count = {}
        self.phase_reads = []

    def add(self, eng, fn, reads=(), writes=(), dsem=None, name=""):
        op = Op(eng, fn, dsem, name)
        deps = {}
        reads = list(reads) + [t for t in self.phase_reads if t not in writes]
        for r in reads:
            w = self.lw.get(r)
            if w is not None:
                deps[id(w)] = w
                op.raw.add(id(w))
            if isinstance(r, tuple) and r[0] == "ps":
                for r2 in self.rd.get(r, {}).values():
                    if r2.eng != eng:
                        deps[id(r2)] = r2
        for w_ in writes:
            w = self.lw.get(w_)
            if w is not None:
                deps[id(w)] = w
            for r in self.rd.get(w_, {}).values():
                deps[id(r)] = r
        op.deps = [d for d in deps.values() if d is not op]
        for r in reads:
            self.rd.setdefault(r, {})[eng if dsem is None else ("dma", len(self.ops))] = op
        for w_ in writes:
            self.lw[w_] = op
            self.rd[w_] = {}
        if dsem is not None:
            self.dma_count[dsem] = self.dma_count.get(dsem, 0) + 16
            op.val = self.dma_count[dsem]
        self.ops.append(op)
        return op

    def finalize(self):
        for op in self.ops:
            for d in op.deps:
                if d.dsem is not None:
                    continue
                if d.eng != op.eng or op.dsem is not None:
                    d.sig = True
                elif op.eng in ("act", "dve", "pool") and id(d) in op.raw:
                    d.sig = True
        cnt = {e: 0 for e in self.ENGS}
        for op in self.ops:
            if op.dsem is None and op.sig:
                cnt[op.eng] += 1
                op.val = cnt[op.eng]
        return cnt

    def emit(self, eng_name, eng, esems, dsems):
        waited = {}
        n = 0
        for op in self.ops:
            if op.eng != eng_name:
                continue
            for d in op.deps:
                if d.dsem is not None:
                    key = ("d", d.dsem)
                    sem = dsems[d.dsem]
                else:
                    if d.eng == op.eng and op.dsem is None:
                        if not (op.eng in ("act", "dve", "pool") and id(d) in op.raw):
                            continue
                    key = ("e", d.eng)
                    sem = esems[d.eng]
                if waited.get(key, 0) >= d.val:
                    continue
                waited[key] = d.val
                eng.wait_ge(sem, d.val)
            if op.fn is None:
                continue
            ins = op.fn(eng)
            n += 1
            if op.dsem is not None:
                ins.then_inc(dsems[op.dsem], 16)
            elif op.sig:
                ins.then_inc(esems[op.eng], 1)
        return n


def _rope_tab(pos, dim):
    inv = (np.float32(ROPE_THETA) ** (-np.arange(0, dim, 2, dtype=np.float32) / np.float32(dim))).astype(np.float32)
    ang = (pos.astype(np.float32)[:, None] * inv[None, :]).astype(np.float32)
    return np.cos(ang).astype(np.float32), np.sin(ang).astype(np.float32)


def _shared_consts(w_in, q_norm_g, kv_norm_g, w_uq, w_ukv, w_out, ln_g, ln_b):
    f = np.float32
    oq, okv, okr, oga, oqb, okb, ovb, ogb = 0, 384, 640, 672, 1184, 1696, 2208, 2720
    w_dil = np.zeros((1024, 4, 4, 128), f)
    for p in range(4):
        for kind, off in enumerate((oqb, okb, ovb, ogb)):
            blk = w_in[:, off + 128 * p: off + 128 * p + 128].copy()
            if kind < 2:
                blk[:, 0:16] = 0.0
                blk[:, 64:80] = 0.0
            w_dil[:, p, kind, :] = blk
    w_dr = np.zeros((1024, 4, 128), f)
    for i, off in enumerate((oqb, okb)):
        t1 = np.concatenate([w_in[:, off + h * 64: off + h * 64 + 8] for h in range(8)], 1)
        t2 = np.concatenate([w_in[:, off + h * 64 + 8: off + h * 64 + 16] for h in range(8)], 1)
        w_dr[:, 2 * i, :] = np.concatenate([t1, t2], 1)
        w_dr[:, 2 * i + 1, :] = np.concatenate([t2, t1], 1)
    kr = w_in[:, okr:okr + 32]
    w_lat = np.concatenate([w_in[:, oq:oq + 384], w_in[:, okv:okv + 256], w_in[:, oga:oga + 512],
                            kr[:, 0:16], kr[:, 16:32], kr[:, 16:32], kr[:, 0:16]], 1).astype(f)
    w_uqh = np.zeros((384, 8, 96), f)
    w_uqr = np.zeros((384, 2, 128), f)
    for h in range(8):
        w_uqh[:, h, 0:64] = w_uq[:, h * 96: h * 96 + 64]
        w_uqr[:, 0, h * 16:(h + 1) * 16] = w_uq[:, h * 96 + 64: h * 96 + 80]
        w_uqr[:, 1, h * 16:(h + 1) * 16] = w_uq[:, h * 96 + 80: h * 96 + 96]
    w_ukn = np.zeros((256, 8, 96), f)
    w_uv = np.zeros((256, 512), f)
    for h in range(8):
        w_ukn[:, h, 0:64] = w_ukv[:, h * 128: h * 128 + 64]
        w_uv[:, h * 64:(h + 1) * 64] = w_ukv[:, h * 128 + 64: h * 128 + 128]
    sel_d = np.zeros((128, 4, 128), f)
    for p in range(4):
        for hh in range(2):
            h = 2 * p + hh
            for fq in range(8):
                sel_d[h * 8 + fq, p, hh * 64 + fq] = 1.0
                sel_d[64 + h * 8 + fq, p, hh * 64 + 8 + fq] = 1.0
    sel_q = np.zeros((128, 8, 2, 96), f)
    for h in range(8):
        for fq in range(16):
            sel_q[h * 16 + fq, h, 0, 64 + fq] = 1.0
            sel_q[h * 16 + fq, h, 1, 80 + fq] = 1.0
    sel_k = np.zeros((32, 96), f)
    for j in range(32):
        sel_k[j, 64 + j] = 1.0
    ident = np.eye(128, dtype=f)
    kk = np.arange(128)[:, None]
    qq = np.arange(128)[None, :]
    mb_cur = np.where(kk <= qq, 0.0, NEGM).astype(f)
    mb_prev = np.where(kk >= qq, 0.0, NEGM).astype(f)
    mb4 = np.concatenate([mb_prev, mb_cur, mb_prev, mb_cur], 1)
    cmat = np.concatenate([ident, mb4], 1).astype(f)
    gq = np.ascontiguousarray(q_norm_g.reshape(3, 128).T).astype(f)
    gkv = np.ascontiguousarray(kv_norm_g.reshape(2, 128).T).astype(f)
    lnp = np.stack([np.broadcast_to(ln_g[None, :], (128, 1024)),
                    np.broadcast_to(ln_b[None, :], (128, 1024))], 1).astype(f)
    part = lambda a, k: np.ascontiguousarray(
        a.reshape(k, 128, *a.shape[1:]).swapaxes(0, 1))
    return {
        "w_dil": part(w_dil, 8),
        "w_dr": part(w_dr, 8),
        "w_lat": part(w_lat, 8),
        "w_uqh": part(w_uqh, 3),
        "w_uqr": part(w_uqr, 3),
        "w_ukn": part(w_ukn, 2),
        "w_uv": part(w_uv, 2),
        "w_out": part(np.ascontiguousarray(w_out).astype(f), 8),
        "sel_d": sel_d, "sel_q": sel_q, "sel_k": sel_k, "cmat": cmat,
        "gq": gq, "gkv": gkv, "lnp": np.ascontiguousarray(lnp),
    }


def _core_consts(h):
    f = np.float32
    pos = np.arange(NT, dtype=np.int64) - 2048 + 2048 * h
    posc = np.maximum(pos, 0)
    c8, s8 = _rope_tab(posc, 16)
    c16, s16 = _rope_tab(posc, 32)
    cc = np.tile(c8.T, (16, 1))
    ss = np.concatenate([-np.tile(s8.T, (8, 1)), np.tile(s8.T, (8, 1))], 0)
    rope_d = np.stack([cc.reshape(128, 8, 512), ss.reshape(128, 8, 512)], 2)
    cq = np.tile(c16.T, (8, 1))[:, QOFF:]
    sq = np.tile(s16.T, (8, 1))[:, QOFF:]
    rope_q = np.stack([cq.reshape(128, 4, 512), sq.reshape(128, 4, 512)], 2)
    rk = np.concatenate([c16.T, c16.T, -s16.T, s16.T], 0)
    rope_k = rk.reshape(64, 8, 512)
    vflag = np.ones((128, 32), f)
    if h == 0:
        vflag[:, 0:16] = 0.0
    return {"rope_d": np.ascontiguousarray(rope_d).astype(f),
            "rope_q": np.ascontiguousarray(rope_q).astype(f),
            "rope_k": np.ascontiguousarray(rope_k).astype(f),
            "vflag": np.ascontiguousarray(vflag)}


class _Stop(Exception):
    pass


class _Step:
    __slots__ = ("qk", "ex", "pv", "post")

    def __init__(self, qk, ex, pv, post=None):
        self.qk, self.ex, self.pv, self.post = qk, ex, pv, post


def build_program(limit=None, dumps=()):
    nc = bass.Bass("TRN2", target_bir_lowering=False)
    S = Sched()

    def done(tag):
        if limit == tag:
            raise _Stop()

    def din(name, shape):
        return nc.dram_tensor(name, list(shape), F32, kind="ExternalInput").ap()

    xT_d = din("xT", [128, 8, NT])
    xq_d = din("xq", [NQ, 1024])
    w_dil_d = din("w_dil", [128, 8, 4, 4, 128])
    w_dr_d = din("w_dr", [128, 8, 4, 128])
    w_lat_d = din("w_lat", [128, 8, 1216])
    w_uqh_d = din("w_uqh", [128, 3, 8, 96])
    w_uqr_d = din("w_uqr", [128, 3, 2, 128])
    w_ukn_d = din("w_ukn", [128, 2, 8, 96])
    w_uv_d = din("w_uv", [128, 2, 512])
    w_out_d = din("w_out", [128, 8, 1024])
    sel_d_d = din("sel_d", [128, 4, 128])
    sel_q_d = din("sel_q", [128, 8, 2, 96])
    sel_k_d = din("sel_k", [32, 96])
    cmat_d = din("cmat", [128, 640])
    gq_d = din("gq", [128, 3])
    gkv_d = din("gkv", [128, 2])
    lnp_d = din("lnp", [128, 2, 1024])
    rope_d_d = din("rope_d", [128, 8, 2, 512])
    rope_q_d = din("rope_q", [128, 4, 2, 512])
    rope_k_d = din("rope_k", [64, 8, 512])
    vflag_d = din("vflag", [128, 32])
    out_d = nc.dram_tensor("out", [NQ, 1024], F32, kind="ExternalOutput").ap()

    ARENA = 207 * 1024
    arena = nc.alloc_sbuf_tensor("arena", [128, ARENA], U8)

    def carve(off, shape, dt):
        esz = 4 if dt == F32 else 2
        n = int(np.prod(shape[1:]))
        assert off % 32 == 0, off
        assert off + n * esz <= ARENA, (off, n * esz, ARENA)
        ap = arena[0:shape[0], off:off + n * esz].bitcast(dt)
        if len(shape) == 3:
            ap = ap.rearrange("p (a b) -> p a b", a=shape[1])
        elif len(shape) == 4:
            ap = ap.rearrange("p (a b c) -> p a b c", a=shape[1], b=shape[2])
        return ap

    K = 1024
    o = 0
    xT = carve(o, [128, 8, NT], BF16); XT_OFF = o; o += 64 * K
    mixT = carve(o, [128, 8, NQ], BF16); o += 32 * K
    NSTG = 2
    stg = [carve(o + i * 4 * K, [128, 1024], F32) for i in range(NSTG)]; o += NSTG * 4 * K
    PT6 = carve(o, [128, 6 * 512], BF16); o += 6 * K
    rden = [carve(o + i * 2 * K, [128, 512], F32) for i in range(2)]; o += 4 * K
    sgt = [carve(o + i * 2 * K, [128, 512], F32) for i in range(2)]; o += 4 * K
    cmat = carve(o, [128, 640], BF16); o += 1280
    ident = cmat[:, 0:128]
    mb4 = cmat[:, 128:640]
    ones_b = carve(o, [128, 128], BF16); o += 256
    m01 = carve(o, [128, 512], BF16); o += 1024
    sel_d = carve(o, [128, 4, 128], BF16); o += 1 * K
    sel_q = carve(o, [128, 8 * 2, 96], BF16); o += 3 * K
    sel_k = carve(o, [32, 96], BF16); o += 192 + 64
    gq = carve(o, [128, 4], F32); o += 32
    gkv = carve(o, [128, 4], F32); o += 32
    small = carve(o, [128, 64], F32); o += 256
    vfl = carve(o, [128, 32], F32); o += 128
    o = (o + 1023) // 1024 * 1024
    R = o
    RSZ = ARENA - R
    wdb = [carve(R, [128, 8, 4, 128], BF16)] * 2
    ropedQ = carve(R + 8 * K, [128, NQ], BF16)
    ropedK = carve(R + 12 * K, [128, NT], BF16)
    qTd = carve(R + 20 * K, [128, NQ], BF16)
    kTd = carve(R + 24 * K, [128, NT], BF16)
    vT = carve(R + 32 * K, [128, NT], BF16)
    wdr = carve(64 * K, [128, 8, 4, 128], BF16)
    Vdb = [carve(R + 40 * K + i * 12 * K, [128, 32, 192], BF16) for i in range(2)]
    accA = carve(R + 64 * K, [128, NQ], F32)
    accB = carve(R + 72 * K, [128, NQ], F32)
    tabd = [carve(R + 64 * K + i * 4 * K, [128, 2, 512], F32) for i in range(2)]
    tmpa = [carve(R + 72 * K + i * 2 * K, [128, 512], F32) for i in range(2)]
    tmpb = [carve(R + 76 * K + i * 2 * K, [128, 512], F32) for i in range(2)]
    assert 80 * K <= RSZ, RSZ
    cqn = carve(R, [128, 3, NQ], BF16)
    ckvn = carve(R + 12 * K, [128, 2, NT], BF16)
    kpe = carve(R + 28 * K, [32, NT], BF16)
    rq1 = carve(R + 36 * K, [128, NQ], BF16)
    rq2 = carve(R + 40 * K, [128, NQ], BF16)
    LT = R + 44 * K
    wlb = [carve(LT + i * 2 * K, [128, 8, 128], BF16) for i in range(6)]
    cf = carve(LT + 12 * K, [128, 3, 512], F32)
    sqb = PT6[:, 0:1536].rearrange("p (a b) -> p a b", a=3)
    rt = PT6[:, 2048:3072].bitcast(F32)
    gtmp = PT6[:, 2048:3072].bitcast(F32)
    cf2 = carve(R + 28 * K, [128, 3, 512], F32)
    sqb2 = carve(R + 34 * K, [128, 3, 512], BF16)
    rt2 = carve(R + 37 * K, [128, 512], F32)
    tabl = carve(LT + 23 * K, [128, 2, 512], F32)
    ta = carve(LT + 27 * K, [128, 512], F32)
    tb = carve(LT + 29 * K, [128, 512], F32)
    w_uqr = carve(LT + 31 * K, [128, 3, 2, 128], BF16)
    tabl2 = carve(LT + 12 * K, [128, 2, 512], F32)
    ta2 = carve(LT + 16 * K, [128, 512], F32)
    tb2 = carve(LT + 18 * K, [128, 512], F32)
    assert 44 * K + 33 * K <= RSZ, RSZ
    X = XT_OFF
    qTA = [carve(X + i * 4 * K, [96, NQ], BF16) for i in range(2)]
    kTA = [carve(X + 8 * K + i * 8 * K, [96, NT], BF16) for i in range(2)]
    VA = [carve(X + 24 * K + i * 12 * K, [128, 32, 192], BF16) for i in range(2)]
    w_uqh = carve(X + 48 * K, [128, 3, 8, 96], BF16)
    w_ukn = carve(X + 53 * K, [128, 2, 8, 96], BF16)
    w_uv = carve(X + 56 * K, [128, 2, 512], BF16)
    lnp = carve(LT + 16 * K, [128, 2, 1024], F32)
    w_out = carve(LT, [128, 8, 1024], BF16)
    xres = [carve(LT + 16 * K + i * 4 * K, [128, 1024], F32) for i in range(2)]
    zb = [carve(LT + 24 * K + i * 4 * K, [128, 1024], F32) for i in range(2)]
    assert 44 * K + 32 * K <= RSZ, RSZ

    pst = nc.alloc_psum_tensor("pst", [128, 8 * 512], F32)

    def bank(i, n=1):
        return pst[:, i * 512:(i + n) * 512]

    dma_sem_names = []

    def dsem(name):
        if name not in dma_sem_names:
            dma_sem_names.append(name)
        return name

    def dma(out_ap, in_ap, reads, writes, sem, eng="sp"):
        return S.add(eng, lambda e: e.dma_start(out=out_ap, in_=in_ap), reads, writes, dsem=dsem(sem))

    def mm(out_ap, lhsT, rhs, start, stop, reads, writes):
        return S.add("pe", lambda e: e.matmul(out_ap, lhsT, rhs, start=start, stop=stop), reads, writes)

    def act(out_ap, in_ap, func, reads, writes, scale=1.0, bias=None):
        if bias is None:
            return S.add("act", lambda e: e.activation(out=out_ap, in_=in_ap, func=func, scale=scale), reads, writes)
        return S.add("act", lambda e: e.activation(out=out_ap, in_=in_ap, func=func, scale=scale, bias=bias), reads, writes)

    def tt(eng, out_ap, in0, in1, op, reads, writes):
        return S.add(eng, lambda e: e.tensor_tensor(out=out_ap, in0=in0, in1=in1, op=op), reads, writes)

    def tsc(eng, out_ap, in0, s1, op0, reads, writes):
        return S.add(eng, lambda e: e.tensor_scalar(out=out_ap, in0=in0, scalar1=s1, scalar2=None, op0=op0), reads, writes)

    def cp(eng, out_ap, in_ap, reads, writes):
        return S.add(eng, lambda e: e.tensor_copy(out=out_ap, in_=in_ap), reads, writes)

    stg_i = [0]
    cast_rr = [0]
    xstg = [carve(R + 40 * K + i * 4 * K, [128, 1024], F32) for i in range(6)]
    ring = {"bufs": xstg, "tok": "xstg", "n": 6}

    XSTG_TOK = [("xstg", k) for k in range(6)]

    def next_stg():
        i = stg_i[0] % ring["n"]
        stg_i[0] += 1
        return ring["bufs"][i], (ring["tok"], i), "%s%d" % (ring["tok"], i)

    def load_cast(dst_ap, src_ap, nelem, writes, engs=("pool",)):
        sbuf_, stok, ssem = next_stg()
        pp = dst_ap.shape[0]
        assert nelem <= 1024
        dma(sbuf_[0:pp, 0:nelem], src_ap, [], [stok], ssem)
        eng = engs[cast_rr[0] % len(engs)]
        cast_rr[0] += 1
        if eng == "act":
            act(dst_ap, sbuf_[0:pp, 0:nelem], AF.Copy, [stok], writes)
        else:
            cp(eng, dst_ap, sbuf_[0:pp, 0:nelem], [stok], writes)

    def load_cast3(dst3, src3, a, b, writes, eng="pool"):
        sbuf_, stok, ssem = next_stg()
        assert a * b <= 1024
        sv = sbuf_[:, 0:a * b].rearrange("p (a b) -> p a b", a=a)
        dma(sv, src3, [], [stok], ssem)
        if eng == "act":
            act(dst3, sv, AF.Copy, [stok], writes)
        else:
            cp(eng, dst3, sv, [stok], writes)

    def write_ones(dst3, eng, wtoks, extra_reads=()):
        src = vfl[:, :].unsqueeze(2).broadcast_to([128, 32, 64])
        if eng == "act":
            act(dst3, src, AF.Copy, ["vfl"] + list(extra_reads), list(wtoks))
        else:
            cp(eng, dst3, src, ["vfl"] + list(extra_reads), list(wtoks))

    psr = {"s": [0, 1, 2], "o": [3, 4], "g": [5, 6, 7]}
    psi = {"s": 0, "o": 0, "g": 0}

    def psum(kind):
        lst = psr[kind]
        i = lst[psi[kind] % len(lst)]
        psi[kind] += 1
        return i

    def pb(i):
        return ("ps", i)

    pti = [0]

    def ptbuf():
        i = pti[0] % 4
        pti[0] += 1
        return i

    pt2i = [0]

    def ptbuf2():
        i = (pt2i[0] % 3) * 2
        pt2i[0] += 1
        return i

    def PT(i, n=1):
        return PT6[:, i * 512:(i + n) * 512]

    def tok(c, n=512):
        return slice(c * n, (c + 1) * n)

    def gate_evac(dst_ap, pa, wtok, tbuf, ttok):
        act(tbuf, bank(pa), AF.Tanh, [pb(pa)], [ttok], scale=0.5)
        S.add("dve", (lambda e: e.scalar_tensor_tensor(out=dst_ap, in0=tbuf, scalar=1.0, in1=bank(pa),
                                                       op0=ALU.add, op1=ALU.mult)),
              [ttok, pb(pa)], [wtok])

    def strided(ap2, start, d, n=128):
        if d == 1:
            return ap2[:, start:start + n]
        r = start % d
        return ap2[:, start - r: start - r + n * d].rearrange("p (i r) -> p r i", r=d)[:, r, :]

    def run_pipeline(steps, fill_iter, nf, LA):
        n = len(steps)
        fi = 0
        tot = n + LA
        for i in range(tot):
            if i < n:
                steps[i].qk()
                steps[i].ex()
            j = i - LA
            if j >= 0:
                steps[j].pv()
                if steps[j].post is not None:
                    steps[j].post()
            while fi < nf and fi * tot < (i + 1) * nf:
                next(fill_iter)
                fi += 1
        for _ in fill_iter:
            pass

    def gen_of(units):
        for u in units:
            u()
            yield

    def xT_tok(ci, c):
        return ("xT", ci, c)

    try:
        pre_tokens = [("tabd", i) for i in range(2)] + [("tmpa", i) for i in range(2)] + [("tmpb", i) for i in range(2)]

        DILC = ((1, 32), (4, 8), (16, 2))

        def dil_proj_units(p, parts=False):
            wd = wdb[p % 2]
            wt = lambda kind: ("wd", 0, kind)
            units = []

            def ku(c):
                pa = psum("g")
                for ci in range(8):
                    mm(bank(pa), wd[:, ci, 1, :], xT[:, ci, tok(c)], ci == 0, False, [wt(1), xT_tok(ci, c)], [pb(pa)])
                mm(bank(pa), sel_d[:, p, :], ropedK[:, tok(c)], False, True, ["sel_d", ("rK", c)], [pb(pa)])
                if p > 0:
                    act(kTd[:, tok(c)], bank(pa), AF.Copy, [pb(pa)], [("kTd", c)])
                else:
                    cp("dve", kTd[:, tok(c)], bank(pa), [pb(pa)], [("kTd", c)])

            def qu(c):
                pa = psum("g")
                for ci in range(8):
                    mm(bank(pa), wd[:, ci, 0, :], xT[:, ci, tok(4 + c)], ci == 0, False, [wt(0), xT_tok(ci, 4 + c)], [pb(pa)])
                mm(bank(pa), sel_d[:, p, :], ropedQ[:, tok(c)], False, True, ["sel_d", ("rQ", c)], [pb(pa)])
                tsc("dve", qTd[:, tok(c)], bank(pa), 0.125, ALU.mult, [pb(pa)], [("qTd", c)])

            def gu(c):
                pa = psum("g")
                for ci in range(8):
                    mm(bank(pa), wd[:, ci, 3, :], xT[:, ci, tok(4 + c)], ci == 0, ci == 7, [wt(3), xT_tok(ci, 4 + c)], [pb(pa)])
                gate_evac(mixT[:, 4 + p, tok(c)], pa, ("mixT", 4 + p, c), gtmp[:, :], "gtmp")

            def vtu(c):
                pa = psum("g")
                for ci in range(8):
                    mm(bank(pa), wd[:, ci, 2, :], xT[:, ci, tok(c)], ci == 0, ci == 7, [wt(2), xT_tok(ci, c)], [pb(pa)])
                if p > 0:
                    act(vT[:, tok(c)], bank(pa), AF.Copy, [pb(pa)], [("vT", c)], scale=0.5)
                else:
                    tsc("dve", vT[:, tok(c)], bank(pa), 0.5, ALU.mult, [pb(pa)], [("vT", c)])

            if parts:
                return ku, vtu, qu, gu
            for c in range(8):
                units.append(lambda c=c: ku(c))
            for c in range(8):
                units.append(lambda c=c: vtu(c))
            for c in range(4):
                units.append(lambda c=c: qu(c))
            for c in range(4):
                units.append(lambda c=c: gu(c))
            return units

        def load_pair_weights(p, kinds=(1, 2, 0, 3), engs=("act", "pool")):
            wd = wdb[p % 2]
            for ki, kind in enumerate(kinds):
                load_cast3(wd[:, :, kind, :], w_dil_d[:, :, p, kind, :], 8, 128, [("wd", 0, kind)],
                           eng=engs[ki % len(engs)])

        dma(vfl[:, :], vflag_d[:, :], [], ["vfl"], "c3")
        S.add("pool", lambda e: e.memset(ones_b[:, :], 1.0), [], ["ones_b"])
        S.add("pool", lambda e: e.memset(small[:, 0:1], RMS_EPS), [], ["small"])
        S.add("pool", lambda e: e.memset(small[:, 1:2], LN_EPS), [], ["small"])
        for kind in (2, 3):
            load_cast3(wdr[:, :, kind, :], w_dr_d[:, :, kind, :], 8, 128, [("wdr", kind)], eng=("act", "dve")[kind % 2])
        load_cast(sel_d.rearrange("p a b -> p (a b)"), sel_d_d.rearrange("p a b -> p (a b)"), 512, ["sel_d"], engs=("dve",))
        load_pair_weights(0, kinds=(1, 2), engs=("act", "dve"))
        xcnt = [0]

        def load_x_quarter(tq):
            for ci in range(8):
                sbuf_, stok, ssem = next_stg()
                dst = xT[:, ci, tq * 1024:(tq + 1) * 1024]
                dma(sbuf_[:, :], xT_d[:, ci, tq * 1024:(tq + 1) * 1024], [], [stok], ssem)
                wr = [("xT", ci, tq * 2), ("xT", ci, tq * 2 + 1)]
                if xcnt[0] % 2 == 1:
                    act(dst, sbuf_[:, :], AF.Copy, [stok], wr)
                else:
                    cp("dve", dst, sbuf_[:, :], [stok], wr)
                xcnt[0] += 1

        load_x_quarter(0)
        load_cast(cmat[:, :], cmat_d[:, :], 640, ["cmat"], engs=("act",))
        S.add("dve", lambda e: e.tensor_scalar(out=m01[:, :], in0=mb4, scalar1=-1.0, scalar2=None, op0=ALU.is_ge),
              ["cmat"], ["m01"])
        load_cast(sel_q[:, 0:8, :].rearrange("p a b -> p (a b)"),
                  sel_q_d.rearrange("p h t c -> p (h t c)")[:, 0:768], 768, ["sel_q"], engs=("dve",))
        load_cast(sel_q[:, 8:16, :].rearrange("p a b -> p (a b)"),
                  sel_q_d.rearrange("p h t c -> p (h t c)")[:, 768:1536], 768, ["sel_q"], engs=("act",))
        load_cast(sel_k[:, :], sel_k_d[:, :], 96, ["sel_k"], engs=("dve",))
        dma(gq[:, 0:3], gq_d[:, :], [], ["gq"], "c0")
        dma(gkv[:, 0:2], gkv_d[:, :], [], ["gkv"], "c2")

        rci = [0]

        def rc(kindA, kindB, dst, c, cc, dname):
            bi_ = rci[0] % 2
            rci[0] += 1
            dma(tabd[bi_][:, :, :], rope_d_d[:, c, :, :], [], [("tabd", bi_)], f"tabd{bi_}", eng="pool")
            pa = psum("g")
            pb_ = psum("g")
            for ci in range(8):
                mm(bank(pa), wdr[:, ci, kindA, :], xT[:, ci, tok(c)], ci == 0, ci == 7,
                   [("wdr", kindA), xT_tok(ci, c)], [pb(pa)])
            for ci in range(8):
                mm(bank(pb_), wdr[:, ci, kindB, :], xT[:, ci, tok(c)], ci == 0, ci == 7,
                   [("wdr", kindB), xT_tok(ci, c)], [pb(pb_)])
            tt("dve", tmpa[bi_][:, :], bank(pa), tabd[bi_][:, 0, :], ALU.mult, [pb(pa), ("tabd", bi_)], [("tmpa", bi_)])
            tt("dve", tmpb[bi_][:, :], bank(pb_), tabd[bi_][:, 1, :], ALU.mult, [pb(pb_), ("tabd", bi_)], [("tmpb", bi_)])
            tt("dve", dst[:, cc * 512: cc * 512 + 512], tmpa[bi_][:, :], tmpb[bi_][:, :], ALU.add,
               [("tmpa", bi_), ("tmpb", bi_)], [(dname, cc)])

        ku0, vtu0, qu0, gu0 = dil_proj_units(0, parts=True)

        def kwork(c):
            rc(2, 3, ropedK, c, c, "rK")
            ku0(c)
            vtu0(c)

        kwork(0)
        kwork(1)
        load_x_quarter(1)
        for kind in (0, 1):
            load_cast3(wdr[:, :, kind, :], w_dr_d[:, :, kind, :], 8, 128, [("wdr", kind)], eng=("act", "dve")[kind % 2])
        load_pair_weights(0, kinds=(0, 3), engs=("act", "dve"))
        kwork(2)
        kwork(3)
        load_x_quarter(2)
        kwork(4)
        kwork(5)
        load_x_quarter(3)
        ring.update(bufs=stg, tok="stg", n=NSTG)
        stg_i[0] = 0
        done("p0")
        kwork(6)
        kwork(7)
        for c in range(4):
            rc(0, 1, ropedQ, 4 + c, c, "rQ")
            qu0(c)
            gu0(c)
        done("d0")
        for vb_ in range(2):
            write_ones(Vdb[vb_][:, :, 64:128], "act", [("Vd_ones", vb_, g4) for g4 in range(4)] + XSTG_TOK)

        def dil_v_units(p, bi):
            wd = wdb[p % 2]
            d, nbl = DILC[bi]
            vb_ = (3 * p + bi) % 2
            Vd = Vdb[vb_]
            jlist = [n_ * d + r for n_ in range(nbl // 2 - 1, nbl) for r in range(d)]
            units = []

            def vu(grp):
                pa = psum("g")
                pbf = bank(pa).bitcast(BF16)
                for gi, j in enumerate(grp):
                    n_, r = divmod(j, d)
                    t0 = d * 128 * n_ + r
                    c_lo = t0 // 512
                    c_hi = (t0 + d * 127) // 512
                    rd = ["cmat"] + [("vT", cc) for cc in range(c_lo, c_hi + 1)]
                    S.add("pe", (lambda e, gi=gi, t0=t0: e.transpose(pbf[:, gi * 128:(gi + 1) * 128],
                                                                      strided(vT[:, :], t0, d), ident)),
                          rd, [pb(pa)])
                ng = len(grp)
                j0 = grp[0]
                assert grp == list(range(j0, j0 + ng))
                src = pbf[:, 0:ng * 128].rearrange("p (a t b) -> p a t b", a=ng, t=2)
                cp("dve", Vd[:, j0:j0 + ng, 0:64], src[:, :, 0, :], [pb(pa)], [("Vd", vb_, j) for j in grp] + XSTG_TOK)
                cp("dve", Vd[:, j0:j0 + ng, 128:192], src[:, :, 1, :], [pb(pa)], [("VdB", vb_, j) for j in grp] + XSTG_TOK)

            for g0 in range(0, len(jlist), 4):
                units.append(lambda grp=jlist[g0:g0 + 4]: vu(grp))
            return units

        def dil_attn_steps(p, bi):
            d, nbl = DILC[bi]
            vb_ = (3 * p + bi) % 2
            Vd = Vdb[vb_]
            steps = []
            for hh in range(2):
                rows = slice(hh * 64, hh * 64 + 64)
                vcols = slice(0, 128) if hh == 0 else slice(64, 192)
                acc = accA if hh == 0 else accB
                qblocks = [(n_, r) for n_ in range(nbl // 2, nbl) for r in range(d)]
                for s0 in range(0, 16, 2):
                    st = {}

                    def qk(s0=s0, st=st, rows=rows, qblocks=qblocks):
                        sb = psum("s")
                        st["sb"] = sb
                        for qi in range(2):
                            n_, r = qblocks[s0 + qi]
                            qc0 = d * 128 * (n_ - nbl // 2) + r
                            qrd = [("qTd", cc) for cc in range(qc0 // 512, (qc0 + d * 127) // 512 + 1)]
                            for pc in range(2):
                                k0 = d * 128 * (n_ - 1 + pc) + r
                                krd = [("kTd", cc) for cc in range(k0 // 512, (k0 + d * 127) // 512 + 1)]
                                mm(bank(sb)[:, (qi * 2 + pc) * 128:(qi * 2 + pc + 1) * 128],
                                   strided(kTd[rows, :], k0, d), strided(qTd[rows, :], qc0, d), True, True,
                                   krd + qrd, [pb(sb)])

                    def ex(st=st):
                        pt = ptbuf()
                        st["pt"] = pt
                        act(PT(pt), bank(st["sb"]), AF.Exp, [pb(st["sb"])], [("PT", pt)])
                        tt("dve", PT(pt), PT(pt), m01[:, :], ALU.mult, [("PT", pt), "m01"], [("PT", pt)])

                    def pv(s0=s0, st=st, vcols=vcols, qblocks=qblocks, hh=hh):
                        key = (p, bi, hh, s0 // 4)
                        if s0 % 4 == 0:
                            otile[key] = psum("o")
                        po = otile[key]
                        pt = st["pt"]
                        for qi in range(2):
                            n_, r = qblocks[s0 + qi]
                            oi = (s0 % 4) + qi
                            for pc in range(2):
                                j = (n_ - 1 + pc) * d + r
                                mm(bank(po)[:, oi * 128:(oi + 1) * 128], Vd[:, j, vcols],
                                   PT(pt)[:, (qi * 2 + pc) * 128:(qi * 2 + pc + 1) * 128],
                                   pc == 0, pc == 1,
                                   [("Vd", vb_, j), ("VdB", vb_, j), ("Vd_ones", vb_, j // 8), ("PT", pt)], [pb(po)])

                    def post(s0=s0, acc=acc, qblocks=qblocks, hh=hh):
                        if s0 % 4 != 2:
                            return
                        o0 = s0 - 2
                        po = otile[(p, bi, hh, s0 // 4)]
                        if d == 1:
                            n_, r = qblocks[o0]
                            qc0 = 128 * (n_ - nbl // 2)
                            dsta = acc[:, qc0:qc0 + 512]
                            srca = bank(po)
                            atoks = [("acc", hh, qc0 // 512)]
                        elif d == 4:
                            n_ = qblocks[o0][0]
                            base = 512 * (n_ - nbl // 2)
                            dsta = acc[:, base:base + 512].rearrange("p (i r) -> p r i", r=4)
                            srca = bank(po).rearrange("p (r i) -> p r i", r=4)
                            atoks = [("acc", hh, base // 512)]
                        else:
                            r0 = qblocks[o0][1]
                            dsta = acc[:, :].rearrange("p (i r) -> p r i", r=16)[:, r0:r0 + 4, :]
                            srca = bank(po).rearrange("p (r i) -> p r i", r=4)
                            atoks = [("acc", hh, cc) for cc in range(4)]
                        if bi == 0:
                            cp("dve", dsta, srca, [pb(po)], atoks + (pre_tokens if p == 0 else []))
                        else:
                            tt("dve", dsta, dsta, srca, ALU.add, [pb(po)] + atoks, atoks)

                    steps.append(_Step(qk, ex, pv, post))
            return steps

        def dil_norm_ops(p):
            p1s, p2s = [], []
            for hh in range(2):
                acc = accA if hh == 0 else accB
                orow = slice(hh * 64, hh * 64 + 64)
                drow = slice(64 - hh * 64, 128 - hh * 64)
                for c in range(4):
                    i = (hh * 4 + c) % 2

                    def p1(hh=hh, acc=acc, orow=orow, drow=drow, c=c, i=i):
                        act(rden[i][orow, :], acc[drow, tok(c)], AF.Ln, [("acc", hh, c)], [("rden", i)])
                        act(rden[i][orow, :], rden[i][orow, :], AF.Exp, [("rden", i)], [("rden", i)], scale=-1.0)

                    def p2(hh=hh, acc=acc, orow=orow, c=c, i=i):
                        tt("dve", sgt[i][orow, :], mixT[orow, 4 + p, tok(c)], rden[i][orow, :], ALU.mult,
                           [("mixT", 4 + p, c), ("rden", i)], [("sgt", i)])
                        tt("dve", mixT[orow, 4 + p, tok(c)], acc[orow, tok(c)], sgt[i][orow, :], ALU.mult,
                           [("acc", hh, c), ("sgt", i)], [("mixT", 4 + p, c)])
                    p1s.append(p1)
                    p2s.append(p2)
            ops = []
            for k in range(len(p1s) + 1):
                def slot(k=k):
                    if k < len(p1s):
                        p1s[k]()
                    if k >= 1:
                        p2s[k - 1]()
                ops.append(slot)
            return ops

        otile = {}
        pending_norm = []
        next_units = None
        carried = 0
        for p in range(4):
            units = next_units if p > 0 else []
            gate_units = units[20:24] if units else []
            v0 = dil_v_units(p, 0)
            v0i = 0
            for ui in range(carried, 20 if units else 0):
                units[ui]()
                if ui % 2 == 1 and ui < 18 and pending_norm:
                    pending_norm.pop(0)()
                if ui >= 15 and v0i < len(v0):
                    v0[v0i]()
                    v0i += 1
            while pending_norm:
                pending_norm.pop(0)()
            while v0i < len(v0):
                v0[v0i]()
                v0i += 1
            done("dproj%d" % p)
            for bi in range(3):
                if bi == 1 and p + 1 < 4:
                    load_pair_weights(p + 1)
                    next_units = dil_proj_units(p + 1)
                steps = dil_attn_steps(p, bi)
                n = len(steps)
                if bi == 0:
                    early = gate_units + dil_v_units(p, 1)
                elif bi == 1:
                    early = dil_v_units(p, 2)
                else:
                    early = []
                late = next_units[0:3] if (bi == 2 and p + 1 < 4) else []

                def sched(early=early, late=late, n=n):
                    ne = len(early)
                    ei = 0
                    for it in range(n + 3):
                        while ei < ne and ei * max(1, n - 4) < (it + 1) * ne:
                            early[ei]()
                            ei += 1
                        if it >= n and (it - n) < len(late):
                            late[it - n]()
                        yield

                run_pipeline(steps, sched(), n + 3, 3)
            carried = 3 if p + 1 < 4 else 0
            done("dattn%d" % p)
            pending_norm = dil_norm_ops(p)
            if limit in ("dpair%d" % p, "d"):
                while pending_norm:
                    pending_norm.pop(0)()
            done("dpair%d" % p)
        done("d")

        dil_tokens = ([("wd", 0, k) for k in range(4)] + [("wdr", k) for k in range(4)] + [("vT", c) for c in range(8)]
                      + [("rQ", c) for c in range(4)]
                      + [("rK", c) for c in range(8)] + [("qTd", c) for c in range(4)] + [("kTd", c) for c in range(8)]
                      + [("Vd", b_, j) for b_ in range(2) for j in range(32)]
                      + [("VdB", b_, j) for b_ in range(2) for j in range(32)]
                      + [("Vd_ones", b_, g) for b_ in range(2) for g in range(4)]
                      + [("PT", i) for i in range(6)] + ["gtmp"])
        S.add("pool", lambda e: e.memset(small[:, 60:61], 0.0), [], dil_tokens + ["Rreg1"])
        S.phase_reads = ["Rreg1"]

        def load_wl(buf, col0, ncol):
            load_cast3(wlb[buf][:, :, 0:ncol], w_lat_d[:, :, col0:col0 + ncol], 8, ncol, [("wl", buf)], eng="act")

        for ft in range(3):
            load_wl(ft, ft * 128, 128)
        load_wl(3, 384, 128)
        load_wl(4, 512, 128)
        load_wl(5, 640, 128)

        CF = [cf, cf2]
        SQB = [sqb, sqb2]
        RT = [rt, rt2]
        rmsi = [0]

        def rms_latent(bufs, c_lo, n_chunks, gains, dst, nfeat, name):
            nft = len(bufs)
            for cc in range(n_chunks):
                c = c_lo + cc
                k2 = rmsi[0] % 2
                rmsi[0] += 1
                cf_, sqb_, rt_ = CF[k2], SQB[k2], RT[k2]
                for ft in range(nft):
                    pa = psum("g")
                    for ci in range(8):
                        mm(bank(pa), wlb[bufs[ft]][:, ci, :], xT[:, ci, tok(c)], ci == 0, ci == 7,
                           [("wl", bufs[ft]), xT_tok(ci, c)], [pb(pa)])
                    cp("dve", cf_[:, ft, :], bank(pa), [pb(pa)], [("cf", k2, ft)])
                    act(sqb_[:, ft, :], bank(pa), AF.Square, [pb(pa)], [("sqb", k2, ft)])
                    if pending_norm:
                        pending_norm.pop(0)()
                pq = psum("g")
                for ft in range(nft):
                    mm(bank(pq), ones_b[:, :], sqb_[:, ft, :], ft == 0, ft == nft - 1, ["ones_b", ("sqb", k2, ft)], [pb(pq)])
                act(rt_[:, :], bank(pq), AF.Ln, [pb(pq), "small"], [("rt", k2)], scale=1.0 / nfeat, bias=small[:, 0:1])
                act(rt_[:, :], rt_[:, :], AF.Exp, [("rt", k2)], [("rt", k2)], scale=-0.5)
                for ft in range(nft):
                    S.add("dve", (lambda e, ft=ft, cc=cc, cf_=cf_, rt_=rt_: e.scalar_tensor_tensor(
                        out=dst[:, ft, tok(cc)], in0=cf_[:, ft, :], scalar=gains[:, ft:ft + 1], in1=rt_[:, :],
                        op0=ALU.mult, op1=ALU.mult)),
                        [("cf", k2, ft), ("rt", k2), "gq", "gkv"], [(name, ft, cc)])

        rms_latent([0, 1, 2], 4, 4, gq, cqn, 384.0, "cqn")
        done("l1")
        load_wl(0, 768, 128)
        load_wl(1, 896, 128)
        load_wl(2, 1024, 128)
        rms_latent([3, 4], 0, 8, gkv, ckvn, 256.0, "ckvn")
        done("l2")
        load_wl(3, 1152, 64)
        for ft, buf in enumerate((5, 0, 1, 2)):
            for c in range(4):
                pa = psum("g")
                for ci in range(8):
                    mm(bank(pa), wlb[buf][:, ci, :], xT[:, ci, tok(4 + c)], ci == 0, ci == 7,
                       [("wl", buf), xT_tok(ci, 4 + c)], [pb(pa)])
                gk = (ft * 4 + c) % 2
                gbuf, gtk = ((sgt[0][:, :], ("sgt", 0)), (sgt[1][:, :], ("sgt", 1)))[gk]
                gate_evac(mixT[:, ft, tok(c)], pa, ("mixT", ft, c), gbuf, gtk)
        done("l3")
        while pending_norm:
            pending_norm.pop(0)()
        S.add("pool", lambda e: e.memset(small[:, 58:59], 0.0), [],
              [("acc", hh, c) for hh in range(2) for c in range(4)] + pre_tokens
              + [("cf", 1, i) for i in range(3)] + [("sqb", 1, i) for i in range(3)] + [("rt", 1), "Rreg2"])
        S.phase_reads = ["Rreg1", "Rreg2"]
        load_cast3(w_uqr.rearrange("p a t c -> p a (t c)"), w_uqr_d.rearrange("p a t c -> p a (t c)"), 3, 256, ["w_uqr"], eng="act")
        TAB = [tabl, tabl2]
        TA = [ta, ta2]
        TB = [tb, tb2]
        al2 = [("cf", 0, i) for i in range(3)]
        for c in range(8):
            k2 = c % 2
            ex2 = al2 if (k2 == 1 and c == 1) else []
            dma(TAB[k2][0:64, 0, :], rope_k_d[:, c, :], [], [("tabl", k2)] + ex2, f"tabl{k2}", eng="pool")
            pa = psum("g")
            for ci in range(8):
                mm(bank(pa)[0:64, :], wlb[3][:, ci, 0:64], xT[:, ci, tok(c)], ci == 0, ci == 7,
                   [("wl", 3), xT_tok(ci, c)], [pb(pa)])
            tt("dve", TA[k2][0:32, :], bank(pa)[0:32, :], TAB[k2][0:32, 0, :], ALU.mult, [pb(pa), ("tabl", k2)],
               [("ta", k2)] + ex2)
            tt("dve", TB[k2][0:32, :], bank(pa)[32:64, :], TAB[k2][32:64, 0, :], ALU.mult, [pb(pa), ("tabl", k2)],
               [("tb", k2)] + ex2)
            tt("dve", kpe[0:32, tok(c)], TA[k2][0:32, :], TB[k2][0:32, :], ALU.add, [("ta", k2), ("tb", k2)], [("kpe", c)])
        done("l4")
        for c in range(4):
            k2 = c % 2
            dma(TAB[k2][:, :, :], rope_q_d[:, c, :, :], [], [("tabl", k2)], f"tabl{k2}", eng="pool")
            p1 = psum("g")
            p2 = psum("g")
            for c3 in range(3):
                mm(bank(p1), w_uqr[:, c3, 0, :], cqn[:, c3, tok(c)], c3 == 0, c3 == 2, ["w_uqr", ("cqn", c3, c)], [pb(p1)])
            for c3 in range(3):
                mm(bank(p2), w_uqr[:, c3, 1, :], cqn[:, c3, tok(c)], c3 == 0, c3 == 2, ["w_uqr", ("cqn", c3, c)], [pb(p2)])
            tt("dve", TA[0][:, :], bank(p1), TAB[k2][:, 0, :], ALU.mult, [pb(p1), ("tabl", k2)], [("ta", 0)])
            tt("dve", TB[0][:, :], bank(p2), TAB[k2][:, 1, :], ALU.mult, [pb(p2), ("tabl", k2)], [("tb", 0)])
            tt("dve", rq1[:, tok(c)], TA[0][:, :], TB[0][:, :], ALU.subtract, [("ta", 0), ("tb", 0)], [("rq1", c)])
            tt("dve", TA[1][:, :], bank(p1), TAB[k2][:, 1, :], ALU.mult, [pb(p1), ("tabl", k2)], [("ta", 1)])
            tt("dve", TB[1][:, :], bank(p2), TAB[k2][:, 0, :], ALU.mult, [pb(p2), ("tabl", k2)], [("tb", 1)])
            tt("dve", rq2[:, tok(c)], TA[1][:, :], TB[1][:, :], ALU.add, [("ta", 1), ("tb", 1)], [("rq2", c)])
        done("l")

        all_xT = [("xT", ci, c) for ci in range(8) for c in range(8)]
        lat_tmp = ([("cf", k_, i) for k_ in range(2) for i in range(3)] + [("sqb", k_, i) for k_ in range(2) for i in range(3)]
                   + [("rt", 0), ("rt", 1), "w_uqr"] + [("tabl", i) for i in range(2)] + [("ta", i) for i in range(2)]
                   + [("tb", i) for i in range(2)] + [("wl", i) for i in range(6)])
        S.add("pool", lambda e: e.memset(small[:, 61:62], 0.0), [], all_xT + lat_tmp + ["xTreg"])
        S.phase_reads = ["xTreg"]
        psr["s"] = [0, 2]
        psr["o"] = [4, 5]
        psr["g"] = [6, 7]

        write_ones(VA[0][:, :, 64:128], "dve", [("VA1s", 0, g4) for g4 in range(4)])
        write_ones(VA[1][:, :, 64:128], "dve", [("VA1s", 1, g4) for g4 in range(4)])
        for c2 in range(2):
            load_cast(w_ukn[:, c2, :, :].rearrange("p a b -> p (a b)"),
                      w_ukn_d[:, c2, :, :].rearrange("p a b -> p (a b)"), 768, ["w_ukn"], engs=("act", "dve"))
        for c2 in range(2):
            load_cast(w_uv[:, c2, :], w_uv_d[:, c2, :], 512, ["w_uv"], engs=("act", "dve"))
        for c3 in range(3):
            load_cast(w_uqh[:, c3, :, :].rearrange("p a b -> p (a b)"),
                      w_uqh_d[:, c3, :, :].rearrange("p a b -> p (a b)"), 768, ["w_uqh"], engs=("act", "dve"))

        def mla_late_loads():
            pass
            dma(lnp[:, :, :], lnp_d[:, :, :], [], ["lnp"], "c1")
            for ci in range(8):
                load_cast(w_out[:, ci, :], w_out_d[:, ci, :], 1024, [("w_out", ci)], engs=("pool",))

        SC_A = 96.0 ** -0.5

        def mla_units(h):
            b = h % 2
            vb2 = (h // 2) % 2
            for c in range(8):
                pa = psum("g")
                for c2 in range(2):
                    mm(bank(pa)[0:96, :], w_ukn[:, c2, h, :], ckvn[:, c2, tok(c)], c2 == 0, False,
                       ["w_ukn", ("ckvn", c2, c)], [pb(pa)])
                    yield
                mm(bank(pa)[0:96, :], sel_k[:, :], kpe[0:32, tok(c)], False, True, ["sel_k", ("kpe", c)], [pb(pa)])
                cp("dve", kTA[b][:, tok(c)], bank(pa)[0:96, :], [pb(pa)], [("kTA", b, c)])
                yield
                if b == 0:
                    pa = psum("g")
                    for j in range(4):
                        tb_ = c * 4 + j
                        for c2 in range(2):
                            mm(bank(pa)[:, j * 128:(j + 1) * 128], ckvn[:, c2, tb_ * 128:(tb_ + 1) * 128],
                               w_uv[:, c2, h * 64:(h + 2) * 64], c2 == 0, c2 == 1,
                               ["w_uv", ("ckvn", c2, c)], [pb(pa)])
                        if j == 1:
                            yield
                    srcp = bank(pa).rearrange("p (a t b) -> p a t b", a=4, t=2)
                    tsc("dve", VA[vb2][:, c * 4:(c + 1) * 4, 0:64], srcp[:, :, 0, :], 0.5, ALU.mult,
                        [pb(pa)], [("VA", vb2, c)])
                    tsc("dve", VA[vb2][:, c * 4:(c + 1) * 4, 128:192], srcp[:, :, 1, :], 0.5, ALU.mult,
                        [pb(pa)], [("VAB", vb2, c)])
                    yield
            for c in range(4):
                pa = psum("g")
                for c3 in range(3):
                    mm(bank(pa)[0:96, :], w_uqh[:, c3, h, :], cqn[:, c3, tok(c)], c3 == 0, False,
                       ["w_uqh", ("cqn", c3, c)], [pb(pa)])
                    if c3 == 1:
                        yield
                mm(bank(pa)[0:96, :], sel_q[:, 2 * h, :], rq1[:, tok(c)], False, False, ["sel_q", ("rq1", c)], [pb(pa)])
                yield
                mm(bank(pa)[0:96, :], sel_q[:, 2 * h + 1, :], rq2[:, tok(c)], False, True, ["sel_q", ("rq2", c)], [pb(pa)])
                tsc("dve", qTA[b][:, tok(c)], bank(pa)[0:96, :], SC_A, ALU.mult, [pb(pa)], [("qTA", b, c)])
                yield

        def mla_nf(h):
            return 8 * 3 + 4 * 3 + (8 * 2 if h % 2 == 0 else 0)

        def mla_steps(h):
            b = h % 2
            vb2 = (h // 2) % 2
            vsl = slice(0, 128) if b == 0 else slice(64, 192)
            orow = slice(b * 64, b * 64 + 64)
            drow = slice(64 - b * 64, 128 - b * 64)
            steps = []
            for c in range(4):
                nkb = 16 + 4 * c + 4
                ost = {}
                for j in range(nkb // 2):
                    st = {}

                    def qk(c=c, j=j, st=st):
                        sb = psum("s")
                        st["sb"] = sb
                        st["w"] = []
                        for u in range(2):
                            kb = 2 * j + u
                            i = kb - (16 + 4 * c)
                            kc = kb // 4
                            bk = sb + u
                            ksl = slice(kb * 128, (kb + 1) * 128)
                            if i < 0:
                                mm(bank(bk), kTA[b][:, ksl], qTA[b][:, tok(c)], True, True,
                                   [("kTA", b, kc), ("qTA", b, c)], [pb(bk)])
                                st["w"].append(512)
                            else:
                                ncol = 512 - 128 * i
                                q0 = c * 512 + 128 * i
                                mm(bank(bk)[:, 0:128], ident, mb4[:, 128:256], True, False, ["cmat"], [pb(bk)])
                                mm(bank(bk)[:, 0:128], kTA[b][:, ksl], qTA[b][:, q0:q0 + 128], False, True,
                                   [("kTA", b, kc), ("qTA", b, c)], [pb(bk)])
                                if ncol > 128:
                                    mm(bank(bk)[:, 128:ncol], kTA[b][:, ksl], qTA[b][:, q0 + 128:(c + 1) * 512],
                                       True, True, [("kTA", b, kc), ("qTA", b, c)], [pb(bk)])
                                st["w"].append(ncol)

                    def ex(st=st):
                        pt = ptbuf2()
                        st["pt"] = pt
                        sb = st["sb"]
                        if st["w"] == [512, 512]:
                            act(PT(pt, 2), bank(sb, 2), AF.Exp, [pb(sb), pb(sb + 1)], [("PT", pt), ("PT", pt + 1)])
                        else:
                            for u in range(2):
                                w = st["w"][u]
                                act(PT(pt + u)[:, 0:w], bank(sb + u)[:, 0:w], AF.Exp, [pb(sb + u)], [("PT", pt + u)])

                    def pv(c=c, j=j, st=st, ost=ost, nkb=nkb):
                        if j == 0:
                            ost["po"] = psum("o")
                        po = ost["po"]
                        pt = st["pt"]
                        for u in range(2):
                            kb = 2 * j + u
                            w = st["w"][u]
                            mm(bank(po)[:, 512 - w:512], VA[vb2][:, kb, vsl], PT(pt + u)[:, 0:w], kb == 0, kb == nkb - 1,
                               [("VA", vb2, kb // 4), ("VAB", vb2, kb // 4), ("VA1s", vb2, kb // 8), ("PT", pt + u)],
                               [pb(po)])

                    def post(c=c, j=j, ost=ost, nkb=nkb):
                        if j != nkb // 2 - 1:
                            return
                        po = ost["po"]
                        ri = (h * 4 + c) % 2
                        S.add("dve", (lambda e: e.reciprocal(out=rden[ri][orow, :], in_=bank(po)[drow, :])),
                              [pb(po)], [("rden", ri)])
                        tt("dve", sgt[ri][orow, :], mixT[orow, h // 2, tok(c)], rden[ri][orow, :], ALU.mult,
                           [("mixT", h // 2, c), ("rden", ri)], [("sgt", ri)])
                        tt("dve", mixT[orow, h // 2, tok(c)], bank(po)[orow, :], sgt[ri][orow, :], ALU.mult,
                           [pb(po), ("sgt", ri)], [("mixT", h // 2, c)])

                    steps.append(_Step(qk, ex, pv, post))
            return steps

        for _ in mla_units(0):
            pass
        mla_late_loads()
        done("mproj0")
        for h in range(8):
            if h < 7:
                run_pipeline(mla_steps(h), mla_units(h + 1), mla_nf(h + 1), 2)
            else:
                run_pipeline(mla_steps(h), iter(()), 0, 2)
            done("mhead%d" % h)
        done("m")

        psr["g"] = [0, 1, 2, 3, 6, 7]
        NZ = 4
        NXR = 3
        zb3 = [carve(X + i * 4 * K, [128, 1024], F32) for i in range(NZ)]
        xres = [carve(X + 16 * K + i * 4 * K, [128, 1024], F32) for i in range(NXR)]
        mla_tok = ([("qTA", b_, c) for b_ in range(2) for c in range(4)] + [("kTA", b_, c) for b_ in range(2) for c in range(8)]
                   + [("VA", b_, g) for b_ in range(2) for g in range(8)] + [("VAB", b_, g) for b_ in range(2) for g in range(8)]
                   + [("VA1s", b_, g) for b_ in range(2) for g in range(4)])
        S.add("pool", lambda e: e.memset(small[:, 57:58], 0.0), [], mla_tok + ["finreg"])
        S.phase_reads = ["xTreg", "finreg"]

        def fin_A(t):
            bi_ = t % NXR
            zi = t % NZ
            s3 = t % 3
            so = 8 + s3 * 12
            dma(xres[bi_][:, :], xq_d[t * 128:(t + 1) * 128, :], [], [("xres", bi_)], f"xres{bi_}", eng="pool")
            c = t // 4
            for nh in range(2):
                pa = psum("g")
                for ci in range(8):
                    mm(bank(pa), mixT[:, ci, t * 128:(t + 1) * 128], w_out[:, ci, nh * 512:(nh + 1) * 512],
                       ci == 0, ci == 7, [("mixT", ci, c), ("w_out", ci)], [pb(pa)])
                S.add("dve", (lambda e, nh=nh, pa=pa: e.scalar_tensor_tensor(
                    out=zb3[zi][:, nh * 512:(nh + 1) * 512], in0=xres[bi_][:, nh * 512:(nh + 1) * 512],
                    scalar=float(ALPHA), in1=bank(pa), op0=ALU.mult, op1=ALU.add)),
                    [("xres", bi_), pb(pa)], [("z", zi, nh)])
            for nh in range(2):
                S.add("dve", (lambda e, nh=nh: e.bn_stats(
                    out=small[:, so + nh * 6: so + nh * 6 + 6], in_=zb3[zi][:, nh * 512:(nh + 1) * 512])),
                    [("z", zi, nh)], [("st", s3, nh)])

        def fin_A2(t):
            s3 = t % 3
            so = 8 + s3 * 12
            mo = 44 + s3 * 4
            S.add("dve", (lambda e: e.bn_aggr(out=small[:, mo: mo + 2], in_=small[:, so: so + 12])),
                  [("st", s3, 0), ("st", s3, 1)], [("mv", s3)])
            act(small[:, mo + 2: mo + 3], small[:, mo + 1: mo + 2], AF.Sqrt,
                [("mv", s3), "small"], [("sd", s3)], scale=1.0, bias=small[:, 1:2])

        def fin_A3(t):
            s3 = t % 3
            mo = 44 + s3 * 4
            S.add("dve", (lambda e: e.reciprocal(out=small[:, mo + 3: mo + 4], in_=small[:, mo + 2: mo + 3])),
                  [("sd", s3)], [("rs", s3)])
            S.add("dve", (lambda e: e.scalar_tensor_tensor(
                out=small[:, mo + 2: mo + 3], in0=small[:, mo: mo + 1], scalar=-1.0, in1=small[:, mo + 3: mo + 4],
                op0=ALU.mult, op1=ALU.mult)), [("mv", s3), ("rs", s3), ("sd", s3)], [("nb", s3)])

        def fin_B(t):
            zi = t % NZ
            s3 = t % 3
            mo = 44 + s3 * 4
            sls = [slice(nh * 512, (nh + 1) * 512) for nh in range(2)]
            for nh in range(2):
                act(zb3[zi][:, sls[nh]], zb3[zi][:, sls[nh]], AF.Identity, [("z", zi, nh), ("rs", s3), ("nb", s3)],
                    [("z", zi, nh)], scale=small[:, mo + 3: mo + 4], bias=small[:, mo + 2: mo + 3])
            for nh in range(2):
                tt("dve", zb3[zi][:, sls[nh]], zb3[zi][:, sls[nh]], lnp[:, 0, sls[nh]], ALU.mult,
                   [("z", zi, nh), "lnp"], [("z", zi, nh)])
            for nh in range(2):
                tt("dve", zb3[zi][:, sls[nh]], zb3[zi][:, sls[nh]], lnp[:, 1, sls[nh]], ALU.add,
                   [("z", zi, nh), "lnp"], [("z", zi, nh)])
            dma(out_d[t * 128:(t + 1) * 128, :], zb3[zi][:, :], [("z", zi, 0), ("z", zi, 1)], [("out", t)], f"out{zi}")

        for t in range(19):
            if 1 <= t <= 16:
                fin_A2(t - 1)
            if t < 16:
                fin_A(t)
            if 1 <= t <= 16:
                fin_A3(t - 1)
            if 2 <= t <= 17:
                fin_B(t - 2)
        S.add("sp", None, [("out", t) for t in range(16)], [])
    except _Stop:
        alltok = list(set(list(S.lw.keys()) + list(S.rd.keys())))
        S.phase_reads = []
        S.add("pool", lambda e: e.memset(small[:, 62:63], 0.0), [], alltok + ["dumpbar"])
        loc = locals()
        for di, (nm, expr) in enumerate(dumps):
            ap = expr(loc)
            shp = list(ap.shape)
            dd = nc.dram_tensor("dump_" + nm, shp, ap.dtype, kind="ExternalOutput").ap()
            S.add("sp", (lambda e, dd=dd, ap=ap: e.dma_start(out=dd, in_=ap)), ["dumpbar"], [("dumpo", di)],
                  dsem=dsem("dump"))
        S.add("sp", None, [("dumpo", di) for di in range(len(dumps))], [])

    S.finalize()
    from contextlib import ExitStack
    with ExitStack() as st:
        esems = {e: st.enter_context(nc.semaphore("e_" + e)) for e in Sched.ENGS}
        dsems = {n: st.enter_context(nc.semaphore("d_" + n)) for n in dma_sem_names}
        block = st.enter_context(nc.Block())

        @block.sync
        def _(e):
            S.emit("sp", e, esems, dsems)

        @block.tensor
        def _(e):
            S.emit("pe", e, esems, dsems)

        @block.scalar
        def _(e):
            S.emit("act", e, esems, dsems)

        @block.vector
        def _(e):
            S.emit("dve", e, esems, dsems)

        @block.gpsimd
        def _(e):
            S.emit("pool", e, esems, dsems)
    return nc


_PROG = {}


def kernel(x, w_in, q_norm_g, kv_norm_g, w_uq, w_ukv, w_out, ln_g, ln_b):
    x = np.asarray(x, dtype=np.float32)
    shared = _shared_consts(np.asarray(w_in, np.float32), np.asarray(q_norm_g, np.float32),
                            np.asarray(kv_norm_g, np.float32), np.asarray(w_uq, np.float32),
                            np.asarray(w_ukv, np.float32), np.asarray(w_out, np.float32),
                            np.asarray(ln_g, np.float32), np.asarray(ln_b, np.float32))
    cc = [_core_consts(0), _core_consts(1)]
    in_maps = []
    for core in range(8):
        b, h = divmod(core, 2)
        xl = np.zeros((NT, 1024), np.float32)
        if h == 0:
            xl[2048:] = x[b, 0:2048]
        else:
            xl[:] = x[b]
        xT = np.ascontiguousarray(xl.T.reshape(8, 128, NT).swapaxes(0, 1))
        m = {"xT": xT, "xq": np.ascontiguousarray(x[b, 2048 * h: 2048 * h + 2048])}
        m.update(shared)
        m.update(cc[h])
        in_maps.append(m)
    if "nc" not in _PROG:
        _PROG["nc"] = build_program()
    res = run_bass_kernel_spmd(_PROG["nc"], in_maps, core_ids=list(range(8)))
    out = np.zeros((BATCH, SEQ, D_MODEL), np.float32)
    for core in range(8):
        b, h = divmod(core, 2)
        out[b, 2048 * h: 2048 * h + 2048] = res.results[core]["out"]
    return out
```
